# Optimizing a Trainium2 kernel written in Bass

```python
import math
import jax, jax.numpy as jnp
from jax import lax
import numpy as np


D_MODEL = 1024
BATCH = 2
SEQ = 8192
DEPTH = 4

GRID_W = 64
CTX_LEN = 256
MIX_W = D_MODEL // 2
N_BRANCH = 4
CHUNK = 128
GMLP_GROUPS = 4
GMLP_GD = MIX_W // GMLP_GROUPS
CONV_W = 31
MLSTM_HEADS = 4
MLSTM_DH = MIX_W // MLSTM_HEADS
QK_CONV = 3
POOL_WINDOWS = (2, 4, 8, 16)
POOL_GD = MIX_W // len(POOL_WINDOWS)
D_FF = 128 * ((8 * D_MODEL // 3 + 127) // 128)
ALPHA = (2 * DEPTH) ** 0.25
BETA = (8 * DEPTH) ** -0.25
LN_EPS = 1e-6
FFN_RES = 0.5

OFF_A = 0
OFF_B = OFF_A + 2 * MIX_W
OFF_C = OFF_B + 2 * MIX_W
OFF_C_GATES = OFF_C + 3 * MIX_W
OFF_C_O = OFF_C_GATES + 4 * MLSTM_HEADS
OFF_D = OFF_C_O + MIX_W
OFF_G = OFF_D + MIX_W
IN_COLS = OFF_G + N_BRANCH * D_MODEL

kernel_name = "hybrid_gated_branch_dit_trunk"


def _norm_f32(x):
    xf = x.astype(jnp.float32)
    mu = jnp.mean(xf, axis=-1, keepdims=True)
    var = jnp.mean(jnp.square(xf - mu), axis=-1, keepdims=True)
    return (xf - mu) * lax.rsqrt(var + LN_EPS)


def layer_norm(x, g, b):
    return (_norm_f32(x) * g + b).astype(x.dtype)


def modulate(h, mods, s):
    return h * (1 + mods[..., s, 1, :]) + mods[..., s, 0, :]


def residual(h, y, mods, s, g, b, r):
    return layer_norm(ALPHA * h + r * mods[..., s, 2, :] * y, g, b)


def pos_emb_2d(rows):
    quarter = D_MODEL // 4
    t = jnp.arange(rows * GRID_W)
    r = (t // GRID_W).astype(jnp.float32)
    col = (t % GRID_W).astype(jnp.float32)
    freqs = jnp.exp(-math.log(10000.0) * jnp.arange(quarter, dtype=jnp.float32) / quarter)
    er = r[:, None] * freqs
    ec = col[:, None] * freqs
    return jnp.concatenate([jnp.sin(er), jnp.cos(er), jnp.sin(ec), jnp.cos(ec)], axis=-1)


def swiglu_ffn(h, w_in, w_out):
    a = h @ w_in
    return (jax.nn.silu(a[..., :D_FF]) * a[..., D_FF:]) @ w_out


def depthwise_conv(x, w):
    k = w.shape[0]
    return lax.conv_general_dilated(
        x, w[:, None, :], window_strides=(1,), padding=[(k // 2, k - 1 - k // 2)],
        dimension_numbers=('NWC', 'WIO', 'NWC'), feature_group_count=x.shape[-1])


def gmlp_mixer(p, ln_g, ln_b, ws, bs):
    B, N, _ = p.shape
    a = jax.nn.gelu(p)
    u, v = a[..., :MIX_W], a[..., MIX_W:]
    v = layer_norm(v, ln_g, ln_b).reshape(B, N // CHUNK, CHUNK, GMLP_GROUPS, GMLP_GD)
    z = jnp.einsum('gts,bcsgd->bctgd', ws, v) + bs.T[:, :, None]
    return u * z.reshape(B, N, MIX_W)


def conv_module(p, w, b, ln_g, ln_b):
    a = p[..., :MIX_W] * jax.nn.sigmoid(p[..., MIX_W:])
    a = depthwise_conv(a, w) + b
    return jax.nn.silu(layer_norm(a, ln_g, ln_b))


def _heads(t):
    B, N, _ = t.shape
    return t.astype(jnp.float32).reshape(B, N, MLSTM_HEADS, MLSTM_DH).transpose(0, 2, 1, 3)


def _gates(pg):
    g = pg.astype(jnp.float32).reshape(pg.shape[0], pg.shape[1], 4, MLSTM_HEADS).transpose(2, 0, 3, 1)
    return g[0], jax.nn.log_sigmoid(g[1]), g[2], jax.nn.log_sigmoid(g[3])


def _chunked(t):
    B, H, N = t.shape[:3]
    return jnp.moveaxis(t.reshape((B, H, N // CHUNK, CHUNK) + t.shape[3:]), 2, 0)


def _unchunk(t):
    t = jnp.moveaxis(t, 0, 2)
    return t.reshape(t.shape[:2] + (t.shape[2] * t.shape[3],) + t.shape[4:])


def _flip(t):
    return jnp.flip(t, axis=2)


def zero_state(b):
    return (jnp.zeros((b, MLSTM_HEADS, MLSTM_DH, MLSTM_DH), jnp.float32),
            jnp.zeros((b, MLSTM_HEADS, MLSTM_DH), jnp.float32),
            jnp.zeros((b, MLSTM_HEADS), jnp.float32))


def _state_update(state, k, v, li, b):
    C, n, m = state
    b_end = b[..., -1]
    logw = b_end[..., None] - b + li
    m_new = jnp.maximum(b_end + m, jnp.max(logw, axis=-1))
    w = jnp.exp(logw - m_new[..., None])
    decay = jnp.exp(b_end + m - m_new)
    C_new = decay[..., None, None] * C + jnp.einsum('bhs,bhsv,bhsk->bhvk', w, v, k)
    n_new = decay[..., None] * n + jnp.einsum('bhs,bhsk->bhk', w, k)
    return (C_new, n_new, m_new)


def _chunk_step(state, xs):
    q, k, v, li, lf = xs
    C, n, m = state
    b = jnp.cumsum(lf, axis=-1)
    seen = jnp.tril(jnp.ones((CHUNK, CHUNK), dtype=bool))
    d = jnp.where(seen, b[..., :, None] - b[..., None, :] + li[..., None, :], -jnp.inf)
    inter = b + m[..., None]
    m_t = jnp.maximum(inter, jnp.max(d, axis=-1))
    s = jnp.einsum('bhtk,bhsk->bhts', q, k) * jnp.exp(d - m_t[..., None])
    w_inter = jnp.exp(inter - m_t)
    num = jnp.einsum('bhts,bhsv->bhtv', s, v) + w_inter[..., None] * jnp.einsum('bhvk,bhtk->bhtv', C, q)
    den = jnp.sum(s, axis=-1) + w_inter * jnp.einsum('bhk,bhtk->bht', n, q)
    h = num / jnp.maximum(jnp.abs(den), jnp.exp(-m_t))[..., None]
    return _state_update(state, k, v, li, b), h


def _state_step(state, xs):
    k, v, li, lf = xs
    return _state_update(state, k, v, li, jnp.cumsum(lf, axis=-1)), None


def mlstm_scan(q, k, v, li, lf, state):
    state, h = lax.scan(_chunk_step, state, tuple(_chunked(t) for t in (q, k, v, li, lf)))
    return _unchunk(h), state


def mlstm_final_state(k, v, li, lf, state):
    state, _ = lax.scan(_state_step, state, tuple(_chunked(t) for t in (k, v, li, lf)))
    return state


def _mlstm_k(pk, w_k):
    return _heads(jax.nn.silu(depthwise_conv(pk, w_k))) * MLSTM_DH ** -0.5


def mlstm_mixer(pc, qk_conv_w, ln_g, st_f, st_b):
    B, N, _ = pc.shape
    q = _heads(jax.nn.silu(depthwise_conv(pc[..., :MIX_W], qk_conv_w[:, :MIX_W])))
    k = _mlstm_k(pc[..., MIX_W:2 * MIX_W], qk_conv_w[:, MIX_W:])
    v = _heads(pc[..., 2 * MIX_W:3 * MIX_W])
    li_f, lf_f, li_b, lf_b = _gates(pc[..., 3 * MIX_W:3 * MIX_W + 4 * MLSTM_HEADS])
    o = pc[..., 3 * MIX_W + 4 * MLSTM_HEADS:]
    h_f, st_f = mlstm_scan(q, k, v, li_f, lf_f, st_f)
    h_b, st_b = mlstm_scan(_flip(q), _flip(k), _flip(v), _flip(li_b), _flip(lf_b), st_b)
    h = _norm_f32(h_f + _flip(h_b)).transpose(0, 2, 1, 3).reshape(B, N, MIX_W)
    return jax.nn.sigmoid(o) * (h * ln_g).astype(pc.dtype), st_f, st_b


def mlstm_context_states(h_ctx, w_kvg, b_kvg, w_k):
    pc = h_ctx @ w_kvg + b_kvg
    k = _mlstm_k(pc[..., :MIX_W], w_k)
    v = _heads(pc[..., MIX_W:2 * MIX_W])
    li_f, lf_f, li_b, lf_b = _gates(pc[..., 2 * MIX_W:])
    st0 = zero_state(h_ctx.shape[0])
    return (mlstm_final_state(k, v, li_f, lf_f, st0),
            mlstm_final_state(_flip(k), _flip(v), _flip(li_b), _flip(lf_b), st0))


def pool_mixer(p, w, scale):
    B, N, _ = p.shape
    pf = p.astype(jnp.float32)
    cs = jnp.concatenate([jnp.zeros((B, 1, MIX_W), jnp.float32), jnp.cumsum(pf, axis=1)], axis=1)
    t = jnp.arange(N)
    outs = []
    for g, win in enumerate(POOL_WINDOWS):
        lo, hi = win // 2, win - 1 - win // 2
        start = jnp.clip(t - lo, 0, N)
        stop = jnp.clip(t + hi + 1, 0, N)
        sl = slice(g * POOL_GD, (g + 1) * POOL_GD)
        mean = (cs[:, stop, sl] - cs[:, start, sl]) / (stop - start).astype(jnp.float32)[:, None]
        outs.append(mean - pf[..., sl])
    d = jnp.stack(outs, axis=2)
    y = jnp.einsum('bngd,gde->bnge', d, w.astype(jnp.float32)).reshape(B, N, MIX_W)
    return (y * scale).astype(p.dtype)


def gated_merge(branches, pg, w_branch, w_out):
    B, N, _ = pg.shape
    ys = jnp.stack(branches, axis=2)
    proj = jnp.einsum('bnim,imd->bnid', ys, w_branch)
    gate = jax.nn.sigmoid(pg.reshape(B, N, N_BRANCH, D_MODEL))
    return jnp.sum(gate * proj, axis=2) @ w_out


def token_mixer(h, lw, st_f, st_b):
    p = h @ lw['w_in'] + lw['b_in']
    y_a = gmlp_mixer(p[..., OFF_A:OFF_B], lw['gmlp_ln_g'], lw['gmlp_ln_b'], lw['gmlp_ws'], lw['gmlp_bs'])
    y_b = conv_module(p[..., OFF_B:OFF_C], lw['conv_w'], lw['conv_b'], lw['conv_ln_g'], lw['conv_ln_b'])
    y_c, st_f, st_b = mlstm_mixer(p[..., OFF_C:OFF_D], lw['qk_conv_w'], lw['mlstm_ln_g'], st_f, st_b)
    y_d = pool_mixer(p[..., OFF_D:OFF_G], lw['pool_w'], lw['pool_scale'])
    y = gated_merge([y_a, y_b, y_c, y_d], p[..., OFF_G:], lw['w_branch'], lw['w_out'])
    return y, st_f, st_b


def setup_inputs(seed: int = 0) -> dict:
    key = jax.random.key(seed)
    ks = jax.random.split(key, 32)

    def nrm(k, shape, s):
        return jax.random.normal(k, shape, jnp.float32) * s

    d = D_MODEL
    b_in = nrm(ks[11], (DEPTH, IN_COLS), 0.02)
    f_bias = jnp.linspace(3.0, 6.0, MLSTM_HEADS, dtype=jnp.float32) + nrm(ks[12], (DEPTH, 2, MLSTM_HEADS), 0.1)
    fo = OFF_C_GATES
    b_in = b_in.at[:, fo + MLSTM_HEADS:fo + 2 * MLSTM_HEADS].set(f_bias[:, 0])
    b_in = b_in.at[:, fo + 3 * MLSTM_HEADS:fo + 4 * MLSTM_HEADS].set(f_bias[:, 1])
    return {
        'x': nrm(ks[0], (BATCH, SEQ, d), 1.0),
        'c': nrm(ks[1], (BATCH, d), 1.0),
        'ctx': nrm(ks[2], (BATCH, CTX_LEN, d), 1.0),
        'c_ctx': nrm(ks[3], (d,), 1.0),
        'w_ada': nrm(ks[4], (DEPTH, d, 9 * d), 0.5 * d ** -0.5),
        'b_ada': nrm(ks[5], (DEPTH, 9 * d), 0.02),
        'ln_g': 1.0 + nrm(ks[6], (DEPTH, 3, d), 0.05),
        'ln_b': nrm(ks[7], (DEPTH, 3, d), 0.02),
        'ffn_w_in': nrm(ks[8], (DEPTH, 2, d, 2 * D_FF), d ** -0.5),
        'ffn_w_out': nrm(ks[9], (DEPTH, 2, D_FF, d), BETA * D_FF ** -0.5),
        'w_in': nrm(ks[10], (DEPTH, d, IN_COLS), d ** -0.5),
        'b_in': b_in,
        'gmlp_ln_g': 1.0 + nrm(ks[13], (DEPTH, MIX_W), 0.05),
        'gmlp_ln_b': nrm(ks[14], (DEPTH, MIX_W), 0.02),
        'gmlp_ws': nrm(ks[15], (DEPTH, GMLP_GROUPS, CHUNK, CHUNK), CHUNK ** -0.5),
        'gmlp_bs': 1.0 + nrm(ks[16], (DEPTH, GMLP_GROUPS, CHUNK), 0.05),
        'conv_w': nrm(ks[17], (DEPTH, CONV_W, MIX_W), CONV_W ** -0.5),
        'conv_b': nrm(ks[18], (DEPTH, MIX_W), 0.02),
        'conv_ln_g': 1.0 + nrm(ks[19], (DEPTH, MIX_W), 0.05),
        'conv_ln_b': nrm(ks[20], (DEPTH, MIX_W), 0.02),
        'qk_conv_w': nrm(ks[21], (DEPTH, QK_CONV, 2 * MIX_W), QK_CONV ** -0.5),
        'mlstm_ln_g': 1.0 + nrm(ks[22], (DEPTH, MIX_W), 0.05),
        'pool_w': nrm(ks[23], (DEPTH, len(POOL_WINDOWS), POOL_GD, POOL_GD), POOL_GD ** -0.5),
        'pool_scale': 1.0 + nrm(ks[24], (DEPTH, MIX_W), 0.05),
        'w_branch': nrm(ks[25], (DEPTH, N_BRANCH, MIX_W, d), MIX_W ** -0.5),
        'w_out': nrm(ks[26], (DEPTH, d, d), BETA * d ** -0.5),
    }


def reference(x, c, ctx, c_ctx, w_ada, b_ada, ln_g, ln_b, ffn_w_in, ffn_w_out, w_in, b_in,
              gmlp_ln_g, gmlp_ln_b, gmlp_ws, gmlp_bs, conv_w, conv_b, conv_ln_g, conv_ln_b,
              qk_conv_w, mlstm_ln_g, pool_w, pool_scale, w_branch, w_out):
    n_lat = x.shape[1]
    rows = n_lat // GRID_W
    x = x + pos_emb_2d(rows).astype(x.dtype)[None]
    h_ctx = ctx
    for l in range(DEPTH):
        last = l == DEPTH - 1
        lw = {'w_in': w_in[l], 'b_in': b_in[l], 'gmlp_ln_g': gmlp_ln_g[l], 'gmlp_ln_b': gmlp_ln_b[l],
              'gmlp_ws': gmlp_ws[l], 'gmlp_bs': gmlp_bs[l], 'conv_w': conv_w[l], 'conv_b': conv_b[l],
              'conv_ln_g': conv_ln_g[l], 'conv_ln_b': conv_ln_b[l], 'qk_conv_w': qk_conv_w[l],
              'mlstm_ln_g': mlstm_ln_g[l], 'pool_w': pool_w[l], 'pool_scale': pool_scale[l],
              'w_branch': w_branch[l], 'w_out': w_out[l]}
        mod_x = (jax.nn.silu(c) @ w_ada[l] + b_ada[l]).reshape(c.shape[0], 1, 3, 3, D_MODEL)
        mod_c = (jax.nn.silu(c_ctx) @ w_ada[l] + b_ada[l]).reshape(3, 3, D_MODEL)

        x = residual(x, swiglu_ffn(modulate(x, mod_x, 0), ffn_w_in[l, 0], ffn_w_out[l, 0]),
                     mod_x, 0, ln_g[l, 0], ln_b[l, 0], FFN_RES)
        h_ctx = residual(h_ctx, swiglu_ffn(modulate(h_ctx, mod_c, 0), ffn_w_in[l, 0], ffn_w_out[l, 0]),
                         mod_c, 0, ln_g[l, 0], ln_b[l, 0], FFN_RES)

        cm = modulate(h_ctx, mod_c, 1)
        if last:
            st_f, st_b = mlstm_context_states(cm, w_in[l, :, OFF_C + MIX_W:OFF_C_O],
                                              b_in[l, OFF_C + MIX_W:OFF_C_O], qk_conv_w[l, :, MIX_W:])
        else:
            st0 = zero_state(h_ctx.shape[0])
            y_ctx, st_f, st_b = token_mixer(cm, lw, st0, st0)
            h_ctx = residual(h_ctx, y_ctx, mod_c, 1, ln_g[l, 1], ln_b[l, 1], 1.0)
        y_x, _, _ = token_mixer(modulate(x, mod_x, 1), lw, st_f, st_b)
        x = residual(x, y_x, mod_x, 1, ln_g[l, 1], ln_b[l, 1], 1.0)

        x = residual(x, swiglu_ffn(modulate(x, mod_x, 2), ffn_w_in[l, 1], ffn_w_out[l, 1]),
                     mod_x, 2, ln_g[l, 2], ln_b[l, 2], FFN_RES)
        if not last:
            h_ctx = residual(h_ctx, swiglu_ffn(modulate(h_ctx, mod_c, 2), ffn_w_in[l, 1], ffn_w_out[l, 1]),
                             mod_c, 2, ln_g[l, 2], ln_b[l, 2], FFN_RES)
    return x
```

```python
import math
from contextlib import ExitStack

import numpy as np
import concourse.bass as bass
import concourse.mybir as mybir
from concourse.bass_utils import run_bass_kernel_spmd

F32 = mybir.dt.float32
BF16 = mybir.dt.bfloat16
AF = mybir.ActivationFunctionType
ALU = mybir.AluOpType

D = 1024
DC = 8
DFF = 2816
FC = 22
MIX = 512
NCTX = 256
DEPTH = 4
ALPHA = (2 * DEPTH) ** 0.25
LN_EPS = 1e-6
DH = 128
LNS = math.log(DH ** -0.5)
OFF_A, OFF_B, OFF_C = 0, 1024, 2048
OFF_GATES = 3584
OFF_O = 3600
OFF_D = 4112
OFF_G = 4624
IN_COLS = 8720
HALO = 16
TM = 256
TMH = TM + 2 * HALO
TF = 256
GK = 2.0 * math.sqrt(2.0 / math.pi)


class Em:
    def __init__(self, nc, es):
        self.nc = nc
        self.es = es
        self.engs = {'pe': nc.tensor, 'act': nc.scalar, 'dve': nc.vector, 'pool': nc.gpsimd, 'sp': nc.sync}
        self.sems = {}
        self.cnt = {}
        self.cur = {}
        self.epoch = 0
        self.seen = {k: {} for k in self.engs}
        self.lastw = {}
        self.readers = {}
        self.depoch = 0
        self.new_epoch()

    def _mksem(self, name):
        return self.es.enter_context(self.nc.semaphore(name))

    def new_epoch(self, engs=None):
        self.epoch += 1
        for k in (engs or self.engs):
            if k == 'sp' and 'sp' in self.cur:
                continue
            key = f"{k}{self.epoch}"
            self.sems[key] = self._mksem("s_" + key)
            self.cnt[key] = 0
            self.cur[k] = key

    def dsem(self, slot):
        if slot not in self.sems:
            self.sems[slot] = self._mksem("d_" + slot)
            self.cnt[slot] = 0
        return self.sems[slot]

    def _wait(self, e, semkey, val):
        if self.seen[e].get(semkey, 0) >= val:
            return
        self.engs[e].wait_ge(self.sems[semkey], val)
        self.seen[e][semkey] = val

    def deps(self, e, reads, writes, skip_self=False):
        mine = self.cur[e]
        for k in reads:
            w = self.lastw.get(k)
            if w and not (skip_self and w[0] == mine):
                self._wait(e, *w)
        for k in writes:
            w = self.lastw.get(k)
            if w and not (skip_self and w[0] == mine):
                self._wait(e, *w)
            for sk, v in self.readers.get(k, {}).items():
                if not (skip_self and sk == mine):
                    self._wait(e, sk, v)

    def done(self, semkey, val, reads, writes):
        for k in reads:
            d = self.readers.setdefault(k, {})
            d[semkey] = max(d.get(semkey, 0), val)
        for k in writes:
            self.lastw[k] = (semkey, val)
            self.readers[k] = {}

    @staticmethod
    def _norm(reads, writes):
        r2 = [k for k in reads if not k.startswith('ps')]
        w2 = [k[:3] if k.startswith('ps') else k for k in writes]
        w2 += [k[:3] for k in reads if k.startswith('ps')]
        return r2, list(dict.fromkeys(w2))

    def op(self, e, fn, reads=(), writes=(), inc=True):
        reads, writes = self._norm(list(reads), list(writes))
        self.deps(e, reads, writes, e == 'pe')
        ins = fn(self.engs[e])
        key = self.cur[e]
        if inc:
            self.cnt[key] += 1
            ins.then_inc(self.sems[key], 1)
            self.done(key, self.cnt[key], reads, writes)
        else:
            self.done(key, self.cnt[key] + 1, reads, writes)
        return ins

    def barrier(self):
        for e in self.engs:
            for key, v in self.cnt.items():
                if v > 0:
                    self._wait(e, key, v)

    def dma(self, e, slot, out, in_, reads=(), writes=()):
        self.deps(e, reads, writes)
        sem = self.dsem(slot)
        ins = self.engs[e].dma_start(out=out, in_=in_)
        self.cnt[slot] += 16
        ins.then_inc(sem, 16)
        self.done(slot, self.cnt[slot], reads, writes)
        return ins


class Packer:
    def __init__(self):
        self.off = {}
        self.parts = []
        self.n = 0

    def add(self, name, arr):
        arr = np.ascontiguousarray(arr, dtype=np.float32).reshape(128, -1)
        self.off[name] = (self.n, arr.shape[1])
        self.parts.append(arr)
        self.n += arr.shape[1]

    def get(self):
        return np.concatenate(self.parts, axis=1)


def fm(v):
    v = np.asarray(v, np.float32)
    n = v.shape[-1] // 128
    return np.moveaxis(v.reshape(v.shape[:-1] + (n, 128)), -1, 0)


def bc(v):
    v = np.asarray(v, np.float32)
    return np.broadcast_to(v[None], (128,) + v.shape)


def pack_params(nl, c_b, c_ctx, b_ada, ln_g, ln_b, b_in, gmlp_ln_g, gmlp_ln_b, gmlp_ws, gmlp_bs, conv_w, conv_b,
                conv_ln_g, conv_ln_b, qk_conv_w, mlstm_ln_g, pool_w, pool_scale):
    pk = Packer()
    pk.add('cvec', np.stack([fm(c_b), fm(c_ctx)], axis=-1))
    ii = np.arange(128, dtype=np.float32)
    pk.add('ident', np.eye(128, dtype=np.float32))
    pk.add('triu', (ii[:, None] <= ii[None, :]).astype(np.float32))
    pk.add('tril', (ii[:, None] >= ii[None, :]).astype(np.float32))
    pk.add('ones', np.ones((128, 128), np.float32))
    pk.add('pidx', ii[:, None])
    pk.add('ridx', bc(np.arange(128, dtype=np.float32)))
    pk.add('cidx', bc(np.arange(64, dtype=np.float32)))
    for l in range(nl):
        pk.add(f'b_ada{l}', fm(b_ada[l]))
        pk.add(f'ln_g{l}', fm(ln_g[l]))
        pk.add(f'ln_b{l}', fm(ln_b[l]))
    pls = []
    for l in range(nl):
        pl = Packer()
        bi = b_in[l]
        fmcols = np.concatenate([bi[1024:2048], bi[2048:3072], bi[OFF_D:OFF_D + 512], bi[0:512],
                                 bi[OFF_O:OFF_O + 512], bi[OFF_G:OFF_G + 4096]])
        pl.add('b_fm', fm(fmcols))
        pl.add('b_gv', bc(bi[512:1024]))
        pl.add('b_mv', bc(bi[3072:3584]))
        pl.add('b_gt', bc(bi[OFF_GATES:OFF_GATES + 16]))
        pl.add('gln_g', bc(gmlp_ln_g[l]))
        pl.add('gln_b', bc(gmlp_ln_b[l]))
        pl.add('wsT', np.transpose(gmlp_ws[l], (2, 0, 1)))
        pl.add('bs', bc(gmlp_bs[l]))
        pl.add('conv_w', np.transpose(fm(conv_w[l]), (0, 2, 1)))
        pl.add('conv_b', fm(conv_b[l]))
        pl.add('cln_g', fm(conv_ln_g[l]))
        pl.add('cln_b', fm(conv_ln_b[l]))
        pl.add('qkw', np.transpose(fm(qk_conv_w[l]), (0, 2, 1)))
        pl.add('mln_g', fm(mlstm_ln_g[l]))
        pl.add('pool_w', np.transpose(pool_w[l], (1, 0, 2)))
        pl.add('pool_s', fm(pool_scale[l]))
        pls.append(pl)
    return pk, pls


def build(N, nl, prm_off, nprm, prl_off, nprl, stop=None):
    nc = bass.Bass("TRN2", target_bir_lowering=False)
    last_layer = nl - 1
    NFT = N // TF
    NMT = N // TM
    xin = nc.dram_tensor("xT", [D, N], F32, kind="ExternalInput").ap()
    cin = nc.dram_tensor("ctxT", [D, NCTX], F32, kind="ExternalInput").ap()
    prm_d = nc.dram_tensor("prm", [128, nprm], F32, kind="ExternalInput").ap()
    prl_d = nc.dram_tensor("prl", [nl, 128, nprl], F32, kind="ExternalInput").ap()
    w_ada = nc.dram_tensor("w_ada", [nl, D, 9 * D], F32, kind="ExternalInput").ap()
    ffn_w_in = nc.dram_tensor("ffn_w_in", [nl, 2, D, 2 * DFF], F32, kind="ExternalInput").ap()
    ffn_w_out = nc.dram_tensor("ffn_w_out", [nl, 2, DFF, D], F32, kind="ExternalInput").ap()
    w_in = nc.dram_tensor("w_in", [nl, D, IN_COLS], F32, kind="ExternalInput").ap()
    w_branch = nc.dram_tensor("w_branch", [nl, 4, MIX, D], F32, kind="ExternalInput").ap()
    w_out = nc.dram_tensor("w_out", [nl, D, D], F32, kind="ExternalInput").ap()
    yout = nc.dram_tensor("yT", [D, N], F32, kind="ExternalOutput").ap()
    xs = [nc.dram_tensor(f"xs{i}", [D, N], F32).ap() for i in range(2)]
    cs = [nc.dram_tensor(f"cs{i}", [D, NCTX], F32).ap() for i in range(2)]
    NCH = N // 128
    cb_x = nc.dram_tensor("cb_x", [NCH, 128, 4 * 130], BF16).ap()
    cb_c = nc.dram_tensor("cb_c", [NCTX // 128, 128, 4 * 130], BF16).ap()
    UPL = 2 * 11 + 2 * 8 + 18 + 4 + 2
    wq = nc.dram_tensor("wq", [nl * UPL, 128, 4096], BF16).ap()

    es = ExitStack()
    with es:
        em = Em(nc, es)
        _n = [0]

        def sb(shape, dt, name=None):
            _n[0] += 1
            return es.enter_context(nc.sbuf_tensor(name or f"t{_n[0]}", shape, dt))

        PS = [es.enter_context(nc.psum_tensor(f"ps{i}", [128, 512], F32)) for i in range(8)]
        prm = sb([128, nprm], F32, "prm_sb")
        prl = sb([128, nprl], F32, "prl_sb")

        def pp(name, a=None, b=None):
            import re as _re
            m = _re.match(r"^(.*?)(\d+)$", name)
            if name in prm_off:
                o, n = prm_off[name]
                buf = prm
            else:
                base = m.group(1) if (m and m.group(1) in prl_off) else name
                o, n = prl_off[base]
                buf = prl
            if a is None:
                return buf[:, o:o + n]
            return buf[:, o + a:o + (b if b is not None else a + 1)]

        em.dma('sp', 'prm', prm[:], prm_d, writes=['prm'])

        ident = pp('ident')
        triu = pp('triu')
        tril = pp('tril')
        ones = pp('ones')

        sc = sb([128, 8, 2], F32, "sc")
        mods = sb([128, nl, 72, 2], F32, "mods")
        es_pro = ExitStack()
        stg = [es_pro.enter_context(nc.sbuf_tensor(f"stg{i}", [128, 4096], F32)) for i in range(2)]
        stb = [es_pro.enter_context(nc.sbuf_tensor(f"stb{i}", [128, 4096], BF16)) for i in range(2)]
        ucount = [0]
        cast_eng = ['act', 'dve', 'pool']

        def prologue_unit(uidx, srcs):
            i = ucount[0] % 2
            ucount[0] += 1
            tot = 0
            for (o, a, b, ap) in srcs:
                dst = stg[i][:, o:o + a * b].rearrange("p (a b) -> p a b", a=a, b=b)
                em.dma('sp', f'pl{i}', dst, ap, writes=[f'stg{i}'])
                tot = max(tot, o + a * b)
            ce = cast_eng[uidx % 3]
            if ce == 'act':
                em.op('act', lambda e: e.activation(out=stb[i][:, 0:tot], in_=stg[i][:, 0:tot], func=AF.Copy),
                      [f'stg{i}'], [f'stb{i}'])
            else:
                em.op(ce, lambda e: e.tensor_copy(out=stb[i][:, 0:tot], in_=stg[i][:, 0:tot]),
                      [f'stg{i}'], [f'stb{i}'])
            em.dma('pool', f'plst{i}', wq[uidx, :, 0:tot], stb[i][:, 0:tot], reads=[f'stb{i}'], writes=[f'wq{i}'])

        def kc_ap(w2d, c0, ncols):
            return w2d[:, c0:c0 + ncols].rearrange("(kc p) c -> p kc c", p=128)

        UNIT = {}
        u = 0
        for l in range(nl):
            for f in range(2):
                for j in range(11):
                    w2 = ffn_w_in[l, f]
                    prologue_unit(u, [(0, 8, 256, kc_ap(w2, j * 256, 256)),
                                      (2048, 8, 256, kc_ap(w2, DFF + j * 256, 256))])
                    UNIT[('fi', l, f, j)] = u
                    u += 1
                for i in range(8):
                    prologue_unit(u, [(0, 22, 128, kc_ap(ffn_w_out[l, f], i * 128, 128))])
                    UNIT[('fo', l, f, i)] = u
                    u += 1
            incols = [1024, 1536, 2048, 2560, OFF_D, 0, OFF_O, 512, 3072] + [OFF_G + 512 * i for i in range(8)]
            for j, c0 in enumerate(incols):
                prologue_unit(u, [(0, 8, 512, kc_ap(w_in[l], c0, 512))])
                UNIT[('in', l, j)] = u
                u += 1
            prologue_unit(u, [(0, 8, 16, kc_ap(w_in[l], OFF_GATES, 16))])
            UNIT[('in', l, 'gt')] = u
            u += 1
            for b in range(4):
                prologue_unit(u, [(0, 4, 1024, kc_ap(w_branch[l, b], 0, 1024))])
                UNIT[('br', l, b)] = u
                u += 1
            for j in range(2):
                prologue_unit(u, [(0, 8, 512, kc_ap(w_out[l], j * 512, 512))])
                UNIT[('wo', l, j)] = u
                u += 1
        assert u == nl * UPL, (u, nl * UPL)

        cv = pp('cvec').rearrange("p (k t) -> p k t", k=8, t=2)
        em.op('act', lambda e: e.activation(out=sc[:], in_=cv, func=AF.Silu), ['prm'], ['sc'])
        for l in range(nl):
            for og in range(18):
                i = ucount[0] % 2
                ucount[0] += 1
                em.dma('sp', f'pl{i}', stg[i][:].rearrange("p (a b) -> p a b", a=8, b=512),
                       kc_ap(w_ada[l], og * 512, 512), writes=[f'stg{i}'])
                for q in range(4):
                    oc = og * 4 + q
                    for kc in range(8):
                        em.op('pe', lambda e: e.matmul(PS[0][:, oc * 2:oc * 2 + 2],
                                                       lhsT=stg[i][:, kc * 512 + q * 128: kc * 512 + (q + 1) * 128],
                                                       rhs=sc[:, kc, :], start=(kc == 0), stop=(kc == 7)),
                              [f'stg{i}', 'sc'], ['ps0'], inc=(kc == 7))
            ba = pp(f'b_ada{l}')
            for t in range(2):
                em.op('dve', lambda e: e.tensor_tensor(out=mods[:, l, :, t],
                                                       in0=PS[0][:, 0:144].rearrange("p (c t) -> p c t", t=2)[:, :, t],
                                                       in1=ba, op=ALU.add), ['ps0', 'prm'], ['mods'])
        for l in range(nl):
            for j in range(3):
                r = 1.0 if j == 1 else 0.5
                em.op('dve', lambda e: e.tensor_scalar(out=mods[:, l, (j * 3 + 1) * 8:(j * 3 + 2) * 8, :],
                                                       in0=mods[:, l, (j * 3 + 1) * 8:(j * 3 + 2) * 8, :],
                                                       scalar1=1.0, scalar2=None, op0=ALU.add), ['mods'], ['mods'])
                em.op('dve', lambda e: e.tensor_scalar(out=mods[:, l, (j * 3 + 2) * 8:(j * 3 + 3) * 8, :],
                                                       in0=mods[:, l, (j * 3 + 2) * 8:(j * 3 + 3) * 8, :],
                                                       scalar1=r, scalar2=None, op0=ALU.mult), ['mods'], ['mods'])

        em.barrier()
        es_pro.close()

        def mod(l, j, t, i, s):
            c = (j * 3 + t) * 8 + i
            return mods[:, l, c, s:s + 1]

        RING = 4
        ring = [sb([128, 4096], BF16, f"ring{i}") for i in range(RING)]
        rcount = [0]

        def wload(key, ncols=4096):
            i = rcount[0] % RING
            rcount[0] += 1
            em.dma('sp', f'wr{i}_{em.depoch}', ring[i][:, 0:ncols], wq[UNIT[key], :, 0:ncols], reads=['wq0', 'wq1'], writes=[f'ring{i}'])
            return ring[i], f'ring{i}'

        ptab = sb([128, 2, 2, 128], F32, "ptab")
        ctab = sb([128, 2, 2, 64], F32, "ctab")
        frq = sb([128, 2], F32, "frq")
        ang = sb([128, 128], F32, "ang")
        angi = sb([128, 128], mybir.dt.int32, "angi")
        angf = sb([128, 128], F32, "angf")
        for c in range(2):
            em.op('dve', lambda e: e.tensor_scalar(out=frq[:, c:c + 1], in0=pp('pidx'), scalar1=float(c * 128),
                                                   scalar2=None, op0=ALU.add), ['prm'], ['frq'])
        em.op('act', lambda e: e.activation(out=frq[:], in_=frq[:], func=AF.Exp, scale=-math.log(10000.0) / 256.0),
              ['frq'], ['frq'])

        def sincos(dst, idx_ap, n, c, phase):
            em.op('dve', lambda e: e.tensor_scalar(out=ang[:, 0:n], in0=idx_ap, scalar1=frq[:, c:c + 1],
                                                   scalar2=1.0 / (2 * math.pi), op0=ALU.mult, op1=ALU.mult),
                  ['prm', 'frq'], ['ang'])
            if phase:
                em.op('dve', lambda e: e.tensor_scalar(out=ang[:, 0:n], in0=ang[:, 0:n], scalar1=phase,
                                                       scalar2=None, op0=ALU.add), ['ang'], ['ang'])
            em.op('dve', lambda e: e.tensor_copy(out=angi[:, 0:n], in_=ang[:, 0:n]), ['ang'], ['angi'])
            em.op('dve', lambda e: e.tensor_copy(out=angf[:, 0:n], in_=angi[:, 0:n]), ['angi'], ['angf'])
            em.op('dve', lambda e: e.tensor_tensor(out=ang[:, 0:n], in0=ang[:, 0:n], in1=angf[:, 0:n],
                                                   op=ALU.subtract), ['ang', 'angf'], ['ang'])
            em.op('dve', lambda e: e.tensor_scalar(out=angf[:, 0:n], in0=ang[:, 0:n], scalar1=0.5, scalar2=None,
                                                   op0=ALU.is_gt), ['ang'], ['angf'])
            em.op('dve', lambda e: e.tensor_tensor(out=ang[:, 0:n], in0=ang[:, 0:n], in1=angf[:, 0:n],
                                                   op=ALU.subtract), ['ang', 'angf'], ['ang'])
            em.op('dve', lambda e: e.tensor_scalar(out=angf[:, 0:n], in0=ang[:, 0:n], scalar1=-0.5, scalar2=None,
                                                   op0=ALU.is_lt), ['ang'], ['angf'])
            em.op('dve', lambda e: e.tensor_tensor(out=ang[:, 0:n], in0=ang[:, 0:n], in1=angf[:, 0:n],
                                                   op=ALU.add), ['ang', 'angf'], ['ang'])
            em.op('act', lambda e: e.activation(out=dst, in_=ang[:, 0:n], func=AF.Sin, scale=2 * math.pi),
                  ['ang'], ['ptab'])

        for c in range(2):
            sincos(ptab[:, c, 0, :], pp('ridx'), 128, c, 0.0)
            sincos(ptab[:, c, 1, :], pp('ridx'), 128, c, 0.25)
            sincos(ctab[:, c, 0, :], pp('cidx'), 64, c, 0.0)
            sincos(ctab[:, c, 1, :], pp('cidx'), 64, c, 0.25)

        XT = sb([128, 8, TMH], F32, "XT")
        XA = sb([128, 8, TMH], F32, "XA")
        HT = sb([128, 8, TMH], BF16, "HT")
        G = sb([128, FC, TF], BF16, "G")
        R = sb([128, 8, TF], F32, "R")
        SQ = sb([128, 8, TF], F32, "SQ")
        T1 = sb([128, 512], F32, "T1")
        T2 = sb([128, 512], F32, "T2")
        MEAN = sb([128, 512], F32, "MEAN")
        RSTD = sb([128, 512], F32, "RSTD")

        def layer_norm_T(T, gname, bname, gi, dst, key_dst, nch=8, src=R, src_key='R', silu=False, banks=(6, 7)):
            nf = float(nch * 128)
            BK7 = ['ps7', 'ps7r0', 'ps7r1', 'ps7r2', 'ps7c']
            pa, pb_ = PS[banks[0]], PS[banks[1]]
            ka = BK7 if banks[0] == 7 else [f'ps{banks[0]}']
            kb = BK7 if banks[1] == 7 else [f'ps{banks[1]}']
            em.op('act', lambda e: e.activation(out=SQ[:, 0:nch, 0:T], in_=src[:, 0:nch, 0:T], func=AF.Square),
                  [src_key], ['SQ'])
            for i in range(nch):
                em.op('pe', lambda e: e.matmul(pa[:, 0:T], lhsT=ones, rhs=src[:, i, 0:T], start=(i == 0),
                                               stop=(i == nch - 1)), [src_key, 'prm'], ka, inc=(i == nch - 1))
            for i in range(nch):
                em.op('pe', lambda e: e.matmul(pb_[:, 0:T], lhsT=ones, rhs=SQ[:, i, 0:T], start=(i == 0),
                                               stop=(i == nch - 1)), ['SQ', 'prm'], kb, inc=(i == nch - 1))
            em.op('act', lambda e: e.activation(out=MEAN[:, 0:T], in_=pa[:, 0:T], func=AF.Copy, scale=1.0 / nf),
                  ka, ['MEAN'])
            em.op('dve', lambda e: e.tensor_tensor(out=T1[:, 0:T], in0=MEAN[:, 0:T], in1=MEAN[:, 0:T], op=ALU.mult),
                  ['MEAN'], ['T1'])
            em.op('dve', lambda e: e.scalar_tensor_tensor(out=T1[:, 0:T], in0=pb_[:, 0:T], scalar=1.0 / nf,
                                                          in1=T1[:, 0:T], op0=ALU.mult, op1=ALU.subtract),
                  kb + ['T1'], ['T1'])
            em.op('act', lambda e: e.activation(out=T1[:, 0:T], in_=T1[:, 0:T], func=AF.Sqrt, bias=LN_EPS),
                  ['T1'], ['T1'])
            em.op('dve', lambda e: e.reciprocal(out=RSTD[:, 0:T], in_=T1[:, 0:T]), ['T1'], ['RSTD'])
            for i in range(nch):
                em.op('pool', lambda e: e.tensor_tensor(out=SQ[:, i, 0:T], in0=src[:, i, 0:T], in1=MEAN[:, 0:T],
                                                        op=ALU.subtract), [src_key, 'MEAN'], ['SQ'])
                em.op('dve', lambda e: e.tensor_tensor(out=SQ[:, i, 0:T], in0=SQ[:, i, 0:T], in1=RSTD[:, 0:T],
                                                       op=ALU.mult), ['SQ', 'RSTD'], ['SQ'])
                em.op('act', lambda e: e.activation(out=dst(i), in_=SQ[:, i, 0:T],
                                                    func=AF.Silu if silu else AF.Identity,
                                                    scale=pp(gname, gi + i), bias=pp(bname, gi + i)),
                      ['SQ', 'prm'], [key_dst])

        def modulate(l, j, s, T, c0=0):
            for i in range(8):
                em.op('dve', lambda e: e.tensor_scalar(out=HT[:, i, 0:T], in0=XT[:, i, 0:T], scalar1=mod(l, j, 1, i, s),
                                                       scalar2=mod(l, j, 0, i, s), op0=ALU.mult, op1=ALU.add),
                      ['XT', 'mods'], ['HT'])
            em.op('pool', lambda e: e.tensor_scalar(out=XA[:, :, 0:T - 2 * c0], in0=XT[:, :, c0:T - c0], scalar1=ALPHA,
                                                    scalar2=0.0, op0=ALU.mult, op1=ALU.add), ['XT'], ['XA'])

        def ffn(l, f, s, T):
            j = 0 if f == 0 else 2
            modulate(l, j, s, T)
            for uu in range(11):
                W, wk = wload(('fi', l, f, uu))
                for q in range(2):
                    jj = uu * 2 + q
                    p1, p2 = PS[(jj % 2) * 2], PS[(jj % 2) * 2 + 1]
                    k1, k2 = f'ps{(jj % 2) * 2}', f'ps{(jj % 2) * 2 + 1}'
                    for kc in range(8):
                        em.op('pe', lambda e: e.matmul(p1[:, 0:T], lhsT=W[:, kc * 256 + q * 128:kc * 256 + (q + 1) * 128],
                                                       rhs=HT[:, kc, 0:T], start=(kc == 0), stop=(kc == 7)),
                              [wk, 'HT'], [k1], inc=(kc == 7))
                    for kc in range(8):
                        em.op('pe', lambda e: e.matmul(p2[:, 0:T],
                                                       lhsT=W[:, 2048 + kc * 256 + q * 128:2048 + kc * 256 + (q + 1) * 128],
                                                       rhs=HT[:, kc, 0:T], start=(kc == 0), stop=(kc == 7)),
                              [wk, 'HT'], [k2], inc=(kc == 7))
                    tt, tk = (T1, 'T1') if jj % 2 == 0 else (T2, 'T2')
                    em.op('act', lambda e: e.activation(out=tt[:, 0:T], in_=p1[:, 0:T], func=AF.Silu), [k1], [tk])
                    em.op('dve', lambda e: e.tensor_tensor(out=G[:, jj, 0:T], in0=tt[:, 0:T], in1=p2[:, 0:T],
                                                           op=ALU.mult), [tk, k2], ['G'])
            for i in range(8):
                W, wk = wload(('fo', l, f, i), 22 * 128)
                p, k = PS[4 + (i % 2)], f'ps{4 + (i % 2)}'
                for jj in range(FC):
                    em.op('pe', lambda e: e.matmul(p[:, 0:T], lhsT=W[:, jj * 128:(jj + 1) * 128], rhs=G[:, jj, 0:T],
                                                   start=(jj == 0), stop=(jj == FC - 1)), [wk, 'G'], [k],
                          inc=(jj == FC - 1))
                em.op('dve', lambda e: e.scalar_tensor_tensor(out=R[:, i, 0:T], in0=p[:, 0:T], scalar=mod(l, j, 2, i, s),
                                                              in1=XA[:, i, 0:T], op0=ALU.mult, op1=ALU.add),
                      [k, 'mods', 'XA'], ['R'])
            layer_norm_T(T, f'ln_g{l}', f'ln_b{l}', j * 8, lambda i: XT[:, i, 0:T], 'XT')

        CST = sb([128, 8, TMH], F32, "CST")
        CACC = sb([128, 4, TM], F32, "CACC")
        QT = sb([128, 4, TM], BF16, "QT")
        KTb = sb([128, 4, TM], BF16, "KTb")
        KTf = sb([128, 4, TM], F32, "KTf")
        UT = sb([128, 4, TM], F32, "UT")
        SO = sb([128, 4, TM], F32, "SO")
        VN = sb([128, 2, 512], BF16, "VN")
        VT = sb([128, 2, 512], F32, "VT")
        VE = sb([128, 2, 4, 130], BF16, "VE")
        YB = sb([128, 4, 4, TM], BF16, "YB")
        MB = sb([128, 8, TM], BF16, "MB")
        GT = sb([128, 16], F32, "GT")
        EG = sb([128, 16], F32, "EG")
        LF = sb([128, 8], F32, "LF")
        ARG = sb([128, 8, 4], F32, "ARG")
        EX = sb([128, 8, 4], F32, "EX")
        SF = sb([128, 4, 128], BF16, "SF")
        SB_ = sb([128, 4, 128], BF16, "SBk")
        KW = sb([128, 4, 128], BF16, "KW")
        CF32 = sb([128, 4, 130], F32, "CF32")
        CFB = sb([128, 4, 130], BF16, "CFB")
        CB32 = sb([128, 4, 130], F32, "CB32")
        CBB = sb([128, 4, 130], BF16, "CBB")
        CBL = [sb([128, 4, 130], BF16, f"CBL{i}") for i in range(2)]
        HS = sb([128, 4, 128], F32, "HS")
        HN = sb([128, 4, 128], F32, "HN")
        BST = sb([128, 4, 6], F32, "BST")
        MV = sb([128, 4, 2], F32, "MV")
        SM = sb([128, 16], F32, "SM")
        WST = sb([128, 4, 128], BF16, "WST")
        PWB = sb([128, 4, 128], BF16, "PWB")
        PT = [sb([128, TMH], F32, f"PT{i}") for i in range(3)]
        em.op('pool', lambda e: e.memset(VE[:], 1.0), [], ['VE'])

        def inproj_fm(l, ukey, T, cols, bias_c0, evac):
            W, wk = wload(ukey)
            for q in range(4):
                p, k = PS[q % 3], f'ps{q % 3}'
                for kc in range(8):
                    em.op('pe', lambda e: e.matmul(p[:, 0:T], lhsT=W[:, kc * 512 + q * 128:kc * 512 + (q + 1) * 128],
                                                   rhs=HT[:, kc, cols[0]:cols[1]], start=(kc == 0), stop=(kc == 7)),
                          [wk, 'HT'], [k], inc=(kc == 7))
                evac(q, p, k, pp(f'b_fm{l}', bias_c0 + q))

        def inproj_tm(l, ukey, ncols, c, evac, wcols=512):
            W, wk = ukey
            p, k = PS[c % 2], f'ps{c % 2}'
            t0 = HALO + c * 128
            for kc in range(8):
                em.op('pe', lambda e: e.matmul(p[:, 0:ncols], lhsT=HT[:, kc, t0:t0 + 128],
                                               rhs=W[:, kc * wcols:kc * wcols + ncols], start=(kc == 0), stop=(kc == 7)),
                      [wk, 'HT'], [k], inc=(kc == 7))
            evac(p, k)

        def zero_halo(buf, key, q, first, lastt):
            if first:
                em.op('pool', lambda e: e.memset(buf[:, q, 0:HALO], 0.0), [], [key])
            if lastt:
                em.op('pool', lambda e: e.memset(buf[:, q, HALO + TM:TMH], 0.0), [], [key])

        def load_tile_halo(src, ntok, ti):
            t0 = ti * TM
            lo = max(t0 - HALO, 0)
            hi = min(t0 + TM + HALO, ntok)
            if lo > t0 - HALO:
                em.op('pool', lambda e: e.memset(XT[:, :, 0:HALO], 0.0), [], ['XT'])
            if hi < t0 + TM + HALO:
                em.op('pool', lambda e: e.memset(XT[:, :, HALO + TM:TMH], 0.0), [], ['XT'])
            em.dma('sp', 'ldx', XT[:, :, lo - (t0 - HALO):hi - (t0 - HALO)],
                   src[:, lo:hi].rearrange("(c p) t -> p c t", p=128), reads=['xs'], writes=['XT'])

        def gate_scalars(l, c, W, wk):
            def ev(p, k):
                em.op('dve', lambda e: e.tensor_tensor(out=GT[:], in0=p[:, 0:16], in1=pp(f'b_gt{l}'), op=ALU.add),
                      [k, 'prm'], ['GT'])
            inproj_tm(l, (W, wk), 16, c, ev, wcols=16)
            em.op('act', lambda e: e.activation(out=EG[:], in_=GT[:], func=AF.Exp, scale=-1.0), ['GT'], ['EG'])
            em.op('act', lambda e: e.activation(out=EG[:], in_=EG[:], func=AF.Ln, bias=1.0), ['EG'], ['EG'])
            em.op('dve', lambda e: e.tensor_scalar(out=LF[:, 0:4], in0=EG[:, 4:8], scalar1=-1.0, scalar2=None,
                                                   op0=ALU.mult), ['EG'], ['LF'])
            em.op('dve', lambda e: e.tensor_scalar(out=LF[:, 4:8], in0=EG[:, 12:16], scalar1=-1.0, scalar2=None,
                                                   op0=ALU.mult), ['EG'], ['LF'])
            pc = PS[7]
            for n, m in enumerate([triu, tril, ones]):
                em.op('pe', lambda e: e.matmul(pc[:, 400 + n * 8:400 + n * 8 + 8], lhsT=m, rhs=LF[:], start=True,
                                               stop=True), ['LF', 'prm'], ['ps7c'])
            bf, bb = pc[:, 400:404], pc[:, 412:416]
            totf, totb = pc[:, 416:420], pc[:, 420:424]
            em.op('dve', lambda e: e.scalar_tensor_tensor(out=ARG[:, 0, :], in0=GT[:, 0:4], scalar=LNS, in1=bf,
                                                          op0=ALU.add, op1=ALU.subtract), ['GT', 'ps7c'], ['ARG'])
            em.op('dve', lambda e: e.scalar_tensor_tensor(out=ARG[:, 1, :], in0=GT[:, 8:12], scalar=LNS, in1=bb,
                                                          op0=ALU.add, op1=ALU.subtract), ['GT', 'ps7c'], ['ARG'])
            em.op('dve', lambda e: e.tensor_copy(out=ARG[:, 2, :], in_=bf), ['ps7c'], ['ARG'])
            em.op('dve', lambda e: e.tensor_copy(out=ARG[:, 3, :], in_=bb), ['ps7c'], ['ARG'])
            em.op('dve', lambda e: e.tensor_tensor(out=ARG[:, 4, :], in0=ARG[:, 0, :], in1=totf, op=ALU.add),
                  ['ARG', 'ps7c'], ['ARG'])
            em.op('dve', lambda e: e.tensor_tensor(out=ARG[:, 5, :], in0=ARG[:, 1, :], in1=totb, op=ALU.add),
                  ['ARG', 'ps7c'], ['ARG'])
            em.op('dve', lambda e: e.tensor_copy(out=ARG[:, 6, :], in_=totf), ['ps7c'], ['ARG'])
            em.op('dve', lambda e: e.tensor_copy(out=ARG[:, 7, :], in_=totb), ['ps7c'], ['ARG'])
            em.op('act', lambda e: e.activation(out=EX[:], in_=ARG[:], func=AF.Exp), ['ARG'], ['EX'])

        def qk_conv(l, q, qi, dst_list):
            w = lambda j: pp(f'qkw{l}', qi * 3 + j)
            em.op('dve', lambda e: e.tensor_scalar(out=T1[:, 0:TM], in0=CST[:, q, HALO - 1:HALO - 1 + TM], scalar1=w(0),
                                                   scalar2=None, op0=ALU.mult), ['CST', 'prm'], ['T1'])
            em.op('dve', lambda e: e.scalar_tensor_tensor(out=T1[:, 0:TM], in0=CST[:, q, HALO:HALO + TM], scalar=w(1),
                                                          in1=T1[:, 0:TM], op0=ALU.mult, op1=ALU.add),
                  ['CST', 'prm', 'T1'], ['T1'])
            em.op('dve', lambda e: e.scalar_tensor_tensor(out=T1[:, 0:TM], in0=CST[:, q, HALO + 1:HALO + 1 + TM],
                                                          scalar=w(2), in1=T1[:, 0:TM], op0=ALU.mult, op1=ALU.add),
                  ['CST', 'prm', 'T1'], ['T1'])
            for dst, key in dst_list:
                em.op('act', lambda e: e.activation(out=dst, in_=T1[:, 0:TM], func=AF.Silu), ['T1'], [key])

        def evac_bias(buf, key, qoff, first, lastt):
            def ev(q, p, k, b):
                em.op('act', lambda e: e.activation(out=buf[:, qoff + q, 0:TMH], in_=p[:, 0:TMH], func=AF.Identity,
                                                    bias=b), [k, 'prm'], [key])
                zero_halo(buf, key, qoff + q, first, lastt)
            return ev

        def k_transposed_scaled(c, h, grp):
            em.op('pe', lambda e: e.transpose(PS[4][:, h * 128:(h + 1) * 128], KTf[:, h, c * 128:(c + 1) * 128], ident),
                  ['KTf', 'prm'], ['ps4'])
            em.op('dve', lambda e: e.tensor_scalar(out=KW[:, h, :], in0=PS[4][:, h * 128:(h + 1) * 128],
                                                   scalar1=EX[:, grp, h:h + 1], scalar2=None, op0=ALU.mult),
                  ['ps4', 'EX'], ['KW'])

        def state_update(c, h, C32, Cb, key32, keyb, grp_dec):
            reg = h % 3
            pr = PS[7][:, reg * 130:reg * 130 + 129]
            em.op('pe', lambda e: e.matmul(pr, lhsT=KW[:, h, :], rhs=VE[:, c, h, 0:129], start=True, stop=True),
                  ['KW', 'VE'], [f'ps7r{reg}'])
            em.op('dve', lambda e: e.scalar_tensor_tensor(out=C32[:, h, 0:129], in0=C32[:, h, 0:129],
                                                          scalar=EX[:, grp_dec, h:h + 1], in1=pr, op0=ALU.mult,
                                                          op1=ALU.add), [key32, 'EX', f'ps7r{reg}'], [key32])
            em.op('pool', lambda e: e.tensor_copy(out=Cb[:, h, 0:129], in_=C32[:, h, 0:129]), [key32], [keyb])

        def state_pass(l, s, src, ntok, ti, backward, save_ap):
            first, lastt = ti == 0, ti == ntok // TM - 1
            load_tile_halo(src, ntok, ti)
            modulate(l, 1, s, TMH, c0=HALO)
            inproj_fm(l, ('in', l, 3), TMH, (0, TMH), 12, evac_bias(CST, 'CST', 4, first, lastt))
            for q in range(4):
                qk_conv(l, 4 + q, 4 + q, [(KTf[:, q, :], 'KTf')])
            Wg, wgk = wload(('in', l, 'gt'), 128)
            cs_order = [1, 0] if backward else [0, 1]
            Wv_ = wload(('in', l, 8))
            for c in cs_order:
                W, wk = Wv_

                def ev(p, k, c=c):
                    em.op('dve', lambda e: e.tensor_tensor(out=VE[:, c, :, 0:128],
                                                           in0=p[:, 0:512].rearrange("p (h d) -> p h d", h=4),
                                                           in1=pp(f'b_mv{l}').rearrange("p (h d) -> p h d", h=4),
                                                           op=ALU.add), [k, 'prm'], ['VE'])
                inproj_tm(l, (W, wk), 512, c, ev)
            for c in cs_order:
                gate_scalars(l, c, Wg, wgk)
                if backward:
                    gc = ti * 2 + c
                    em.dma('sp', 'stcb', save_ap[gc].rearrange("p (h d) -> p h d", h=4), CBB[:], reads=['CBB'],
                           writes=['cbscr'])
                for h in range(4):
                    k_transposed_scaled(c, h, 5 if backward else 4)
                    if backward:
                        state_update(c, h, CB32, CBB, 'CB32', 'CBB', 7)
                    else:
                        state_update(c, h, CF32, CFB, 'CF32', 'CFB', 6)

        def gelu_T(dst, src_ap, src_keys, dkey, T, tmp, tkey):
            em.op('act', lambda e: e.activation(out=tmp, in_=src_ap, func=AF.Square), src_keys, [tkey])
            em.op('dve', lambda e: e.tensor_scalar(out=tmp, in0=tmp, scalar1=0.044715, scalar2=1.0, op0=ALU.mult,
                                                   op1=ALU.add), [tkey], [tkey])
            em.op('dve', lambda e: e.tensor_tensor(out=tmp, in0=tmp, in1=src_ap, op=ALU.mult), [tkey] + src_keys,
                  [tkey])
            em.op('act', lambda e: e.activation(out=tmp, in_=tmp, func=AF.Sigmoid, scale=GK), [tkey], [tkey])
            em.op('dve', lambda e: e.tensor_tensor(out=dst, in0=tmp, in1=src_ap, op=ALU.mult), [tkey] + src_keys,
                  [dkey])

        def mixer_tile(l, s, src, dst, ntok, ti, cb_scr, want_out=True):
            first, lastt = ti == 0, ti == ntok // TM - 1
            load_tile_halo(src, ntok, ti)
            modulate(l, 1, s, TMH, c0=HALO)
            for c in range(2):
                gc = ti * 2 + c
                em.dma('sp', f'ldcb{c}', CBL[c][:], cb_scr[gc].rearrange("p (h d) -> p h d", h=4), reads=['cbscr'],
                       writes=[f'CBL{c}'])
            inproj_fm(l, ('in', l, 0), TMH, (0, TMH), 0, evac_bias(CST, 'CST', 0, False, False))

            def ev_glu(q, p, k, b):
                em.op('act', lambda e: e.activation(out=PT[0][:, 0:TMH], in_=p[:, 0:TMH], func=AF.Sigmoid, bias=b),
                      [k, 'prm'], ['PT0'])
                em.op('dve', lambda e: e.tensor_tensor(out=CST[:, q, 0:TMH], in0=CST[:, q, 0:TMH], in1=PT[0][:, 0:TMH],
                                                       op=ALU.mult), ['CST', 'PT0'], ['CST'])
                zero_halo(CST, 'CST', q, first, lastt)
            inproj_fm(l, ('in', l, 1), TMH, (0, TMH), 4, ev_glu)
            for q in range(4):
                cw = lambda j: pp(f'conv_w{l}', q * 31 + j)
                em.op('dve', lambda e: e.tensor_scalar(out=CACC[:, q, :], in0=CST[:, q, 1:1 + TM], scalar1=cw(0),
                                                       scalar2=pp(f'conv_b{l}', q), op0=ALU.mult, op1=ALU.add),
                      ['CST', 'prm'], ['CACC'])
                for j in range(1, 31):
                    em.op('dve', lambda e: e.scalar_tensor_tensor(out=CACC[:, q, :], in0=CST[:, q, 1 + j:1 + j + TM],
                                                                  scalar=cw(j), in1=CACC[:, q, :], op0=ALU.mult,
                                                                  op1=ALU.add), ['CST', 'prm', 'CACC'], ['CACC'])
            layer_norm_T(TM, f'cln_g{l}', f'cln_b{l}', 0, lambda i: YB[:, 1, i, :], 'YB', nch=4, src=CACC,
                         src_key='CACC', silu=True, banks=(3, 4))
            inproj_fm(l, ('in', l, 2), TMH, (0, TMH), 8, evac_bias(CST, 'CST', 0, first, lastt))
            inproj_fm(l, ('in', l, 3), TMH, (0, TMH), 12, evac_bias(CST, 'CST', 4, first, lastt))
            for q in range(4):
                qk_conv(l, q, q, [(QT[:, q, :], 'QT')])
            for q in range(4):
                qk_conv(l, 4 + q, 4 + q, [(KTf[:, q, :], 'KTf'), (KTb[:, q, :], 'KTb')])
            inproj_fm(l, ('in', l, 4), TMH, (0, TMH), 16, evac_bias(CST, 'CST', 0, first, lastt))
            em.op('pool', lambda e: e.tensor_copy(out=PWB[:], in_=pp(f'pool_w{l}').rearrange("p (g e) -> p g e", g=4)),
                  ['prm'], ['PWB'])
            for g, win in enumerate((2, 4, 8, 16)):
                lo, hi = win // 2, win - 1 - win // 2
                cur, ck = CST[:, g, :], 'CST'
                k = 1
                nb = 0
                W0 = TMH
                while k < win:
                    nxt = PT[nb % 2]
                    nk = f'PT{nb % 2}'
                    cin_, cink = cur, ck
                    em.op('pool', lambda e: e.tensor_tensor(out=nxt[:, k:W0], in0=cin_[:, k:W0], in1=cin_[:, 0:W0 - k],
                                                            op=ALU.add), [cink], [nk])
                    cur, ck = nxt, nk
                    k *= 2
                    nb += 1
                a0 = HALO + hi
                ic = PT[2]
                em.op('pool', lambda e: e.memset(ic[:, 0:TM], 1.0 / win), [], ['PT2'])
                if first:
                    for t in range(lo):
                        em.op('pool', lambda e: e.memset(ic[:, t:t + 1], 1.0 / (t + hi + 1)), [], ['PT2'])
                if lastt:
                    for t in range(TM - hi, TM):
                        em.op('pool', lambda e: e.memset(ic[:, t:t + 1], 1.0 / (TM - t + lo)), [], ['PT2'])
                em.op('dve', lambda e: e.tensor_tensor(out=T1[:, 0:TM], in0=cur[:, a0:a0 + TM], in1=ic[:, 0:TM],
                                                       op=ALU.mult), [ck, 'PT2'], ['T1'])
                em.op('dve', lambda e: e.tensor_tensor(out=MB[:, g, :], in0=T1[:, 0:TM], in1=CST[:, g, HALO:HALO + TM],
                                                       op=ALU.subtract), ['T1', 'CST'], ['MB'])
                em.op('pe', lambda e: e.matmul(PS[3][:, 0:TM], lhsT=PWB[:, g, :], rhs=MB[:, g, :], start=True, stop=True),
                      ['PWB', 'MB'], ['ps3'])
                em.op('act', lambda e: e.activation(out=YB[:, 3, g, :], in_=PS[3][:, 0:TM], func=AF.Copy,
                                                    scale=pp(f'pool_s{l}', g)), ['ps3', 'prm'], ['YB'])
            def ev_u(q, p, k, b):
                em.op('act', lambda e: e.activation(out=PT[1][:, 0:TM], in_=p[:, 0:TM], func=AF.Identity, bias=b),
                      [k, 'prm'], ['PT1'])
                gelu_T(UT[:, q, :], PT[1][:, 0:TM], ['PT1'], 'UT', TM, PT[0][:, 0:TM], 'PT0')
            inproj_fm(l, ('in', l, 5), TM, (HALO, HALO + TM), 20, ev_u)

            def ev_o(q, p, k, b):
                em.op('act', lambda e: e.activation(out=SO[:, q, :], in_=p[:, 0:TM], func=AF.Sigmoid, bias=b),
                      [k, 'prm'], ['SO'])
            inproj_fm(l, ('in', l, 6), TM, (HALO, HALO + TM), 24, ev_o)
            em.op('pool', lambda e: e.tensor_copy(out=WST[:], in_=pp(f'wsT{l}').rearrange("p (g t) -> p g t", g=4)),
                  ['prm'], ['WST'])
            Wv = wload(('in', l, 7))
            for c in range(2):
                def ev_v(p, k, c=c):
                    em.op('dve', lambda e: e.tensor_tensor(out=VT[:, c, :], in0=p[:, 0:512], in1=pp(f'b_gv{l}'),
                                                           op=ALU.add), [k, 'prm'], ['VT'])
                inproj_tm(l, Wv, 512, c, ev_v)
                gelu_T(VT[:, c, :], VT[:, c, :], ['VT'], 'VT', 512, T2[:, 0:512], 'T2')
                em.op('dve', lambda e: e.bn_stats(out=BST[:, 0, :], in_=VT[:, c, :]), ['VT'], ['BST'])
                em.op('dve', lambda e: e.bn_aggr(out=MV[:, 0, :], in_=BST[:, 0, :]), ['BST'], ['MV'])
                em.op('act', lambda e: e.activation(out=SM[:, 0:1], in_=MV[:, 0, 1:2], func=AF.Sqrt, bias=LN_EPS),
                      ['MV'], ['SM'])
                em.op('dve', lambda e: e.reciprocal(out=SM[:, 0:1], in_=SM[:, 0:1]), ['SM'], ['SM'])
                em.op('dve', lambda e: e.tensor_scalar(out=VT[:, c, :], in0=VT[:, c, :], scalar1=MV[:, 0, 0:1],
                                                       scalar2=SM[:, 0:1], op0=ALU.subtract, op1=ALU.mult),
                      ['VT', 'MV', 'SM'], ['VT'])
                em.op('pool', lambda e: e.tensor_tensor(out=VT[:, c, :], in0=VT[:, c, :], in1=pp(f'gln_g{l}'),
                                                        op=ALU.mult), ['VT', 'prm'], ['VT'])
                em.op('pool', lambda e: e.tensor_tensor(out=VN[:, c, :], in0=VT[:, c, :], in1=pp(f'gln_b{l}'),
                                                        op=ALU.add), ['VT', 'prm'], ['VN'])
            for g in range(4):
                for c in range(2):
                    em.op('pe', lambda e: e.matmul(PS[3][:, c * 128:(c + 1) * 128], lhsT=VN[:, c, g * 128:(g + 1) * 128],
                                                   rhs=WST[:, g, :], start=True, stop=True), ['VN', 'WST'], ['ps3'])
                bsg = pp(f'bs{l}', g * 128, (g + 1) * 128)
                for c in range(2):
                    em.op('dve', lambda e: e.tensor_tensor(out=T1[:, c * 128:(c + 1) * 128],
                                                           in0=PS[3][:, c * 128:(c + 1) * 128], in1=bsg, op=ALU.add),
                          ['ps3', 'prm'], ['T1'])
                em.op('dve', lambda e: e.tensor_tensor(out=YB[:, 0, g, :], in0=T1[:, 0:TM], in1=UT[:, g, :], op=ALU.mult),
                      ['T1', 'UT'], ['YB'])
            Wmv = wload(('in', l, 8))
            for c in range(2):
                def ev_mv(p, k, c=c):
                    em.op('dve', lambda e: e.tensor_tensor(out=VE[:, c, :, 0:128],
                                                           in0=p[:, 0:512].rearrange("p (h d) -> p h d", h=4),
                                                           in1=pp(f'b_mv{l}').rearrange("p (h d) -> p h d", h=4),
                                                           op=ALU.add), [k, 'prm'], ['VE'])
                inproj_tm(l, Wmv, 512, c, ev_mv)
            Wg, wgk = wload(('in', l, 'gt'), 128)
            for c in range(2):
                gate_scalars(l, c, Wg, wgk)
                cs_ = slice(c * 128, (c + 1) * 128)
                for h in range(4):
                    em.op('pe', lambda e: e.matmul(PS[3][:, h * 128:(h + 1) * 128], lhsT=KTb[:, h, cs_], rhs=QT[:, h, cs_],
                                                   start=True, stop=True), ['KTb', 'QT'], ['ps3'])
                for h in range(4):
                    em.op('dve', lambda e: e.scalar_tensor_tensor(out=SF[:, h, :], in0=PS[3][:, h * 128:(h + 1) * 128],
                                                                  scalar=EX[:, 0, h:h + 1], in1=triu, op0=ALU.mult,
                                                                  op1=ALU.mult), ['ps3', 'EX', 'prm'], ['SF'])
                    em.op('dve', lambda e: e.scalar_tensor_tensor(out=SB_[:, h, :], in0=PS[3][:, h * 128:(h + 1) * 128],
                                                                  scalar=EX[:, 1, h:h + 1], in1=tril, op0=ALU.mult,
                                                                  op1=ALU.mult), ['ps3', 'EX', 'prm'], ['SBk'])
                for hp in range(2):
                    for (Sx, sk, Cx, ckey, pb, pk_) in ((SF, 'SF', CFB, 'CFB', PS[5], 'ps5'),
                                                        (SB_, 'SBk', CBL[c], f'CBL{c}', PS[6], 'ps6')):
                        for hh in range(2):
                            h = hp * 2 + hh
                            o_ = pb[:, hh * 130:hh * 130 + 129]
                            em.op('pe', lambda e: e.matmul(o_, lhsT=Sx[:, h, :], rhs=VE[:, c, h, 0:129], start=True,
                                                           stop=False), [sk, 'VE'], [pk_], inc=False)
                            em.op('pe', lambda e: e.matmul(o_, lhsT=QT[:, h, cs_], rhs=Cx[:, h, 0:129], start=False,
                                                           stop=True), ['QT', ckey], [pk_])
                    for d_, (pb, pk_) in enumerate(((PS[5], 'ps5'), (PS[6], 'ps6'))):
                        den = pb[:, 0:260].rearrange("p (h d) -> p h d", h=2)[:, :, 128]
                        eb = EX[:, 2 + d_, hp * 2:hp * 2 + 2]
                        sm = SM[:, 4 + d_ * 2:6 + d_ * 2]
                        em.op('act', lambda e: e.activation(out=sm, in_=den, func=AF.Abs), [pk_], ['SM'])
                        em.op('dve', lambda e: e.tensor_tensor(out=sm, in0=sm, in1=eb, op=ALU.mult), ['SM', 'EX'], ['SM'])
                        em.op('dve', lambda e: e.tensor_scalar(out=sm, in0=sm, scalar1=1.0, scalar2=None, op0=ALU.max),
                              ['SM'], ['SM'])
                        em.op('dve', lambda e: e.reciprocal(out=sm, in_=sm), ['SM'], ['SM'])
                        em.op('dve', lambda e: e.tensor_tensor(out=sm, in0=sm, in1=eb, op=ALU.mult), ['SM', 'EX'], ['SM'])
                    for hh in range(2):
                        h = hp * 2 + hh
                        em.op('act', lambda e: e.activation(out=HS[:, h, :], in_=PS[5][:, hh * 130:hh * 130 + 128],
                                                            func=AF.Copy, scale=SM[:, 4 + hh:5 + hh]), ['ps5', 'SM'],
                              ['HS'])
                        em.op('dve', lambda e: e.scalar_tensor_tensor(out=HS[:, h, :],
                                                                      in0=PS[6][:, hh * 130:hh * 130 + 128],
                                                                      scalar=SM[:, 6 + hh:7 + hh], in1=HS[:, h, :],
                                                                      op0=ALU.mult, op1=ALU.add),
                              ['ps6', 'SM', 'HS'], ['HS'])
                for h in range(4):
                    em.op('dve', lambda e: e.bn_stats(out=BST[:, h, :], in_=HS[:, h, :]), ['HS'], ['BST'])
                    em.op('dve', lambda e: e.bn_aggr(out=MV[:, h, :], in_=BST[:, h, :]), ['BST'], ['MV'])
                em.op('act', lambda e: e.activation(out=SM[:, 8:12], in_=MV[:, :, 1], func=AF.Sqrt, bias=LN_EPS),
                      ['MV'], ['SM'])
                em.op('dve', lambda e: e.reciprocal(out=SM[:, 8:12], in_=SM[:, 8:12]), ['SM'], ['SM'])
                for h in range(4):
                    em.op('dve', lambda e: e.tensor_scalar(out=HN[:, h, :], in0=HS[:, h, :], scalar1=MV[:, h, 0:1],
                                                           scalar2=SM[:, 8 + h:9 + h], op0=ALU.subtract, op1=ALU.mult),
                          ['HS', 'MV', 'SM'], ['HN'])
                    em.op('pe', lambda e: e.transpose(PS[4][:, h * 128:(h + 1) * 128], HN[:, h, :], ident),
                          ['HN', 'prm'], ['ps4'])
                    em.op('dve', lambda e: e.scalar_tensor_tensor(out=YB[:, 2, h, cs_], in0=PS[4][:, h * 128:(h + 1) * 128],
                                                                  scalar=pp(f'mln_g{l}', h), in1=SO[:, h, cs_],
                                                                  op0=ALU.mult, op1=ALU.mult),
                          ['ps4', 'prm', 'SO'], ['YB'])
                for h in range(4):
                    k_transposed_scaled(c, h, 4)
                    state_update(c, h, CF32, CFB, 'CF32', 'CFB', 6)
            if not want_out:
                return
            for b in range(4):
                Wb, wbk = wload(('br', l, b))
                Wgs = [None, None]
                for jj in range(8):
                    if jj % 4 == 0:
                        Wgs = wload(('in', l, 9 + b * 2 + jj // 4))
                    Wg2, wg2k = Wgs
                    pg, pgk = PS[jj % 2], f'ps{jj % 2}'
                    ppj, ppk = PS[2], 'ps2'
                    q = jj % 4
                    for kc in range(8):
                        em.op('pe', lambda e: e.matmul(pg[:, 0:TM], lhsT=Wg2[:, kc * 512 + q * 128:kc * 512 + (q + 1) * 128],
                                                       rhs=HT[:, kc, HALO:HALO + TM], start=(kc == 0), stop=(kc == 7)),
                              [wg2k, 'HT'], [pgk], inc=(kc == 7))
                    for kc in range(4):
                        em.op('pe', lambda e: e.matmul(ppj[:, 0:TM], lhsT=Wb[:, kc * 1024 + jj * 128:kc * 1024 + (jj + 1) * 128],
                                                       rhs=YB[:, b, kc, :], start=(kc == 0), stop=(kc == 3)),
                              [wbk, 'YB'], [ppk], inc=(kc == 3))
                    tt, tk = (PT[0], 'PT0') if jj % 2 == 0 else (PT[1], 'PT1')
                    em.op('act', lambda e: e.activation(out=tt[:, 0:TM], in_=pg[:, 0:TM], func=AF.Sigmoid,
                                                        bias=pp(f'b_fm{l}', 28 + b * 8 + jj)), [pgk, 'prm'], [tk])
                    if b == 0:
                        em.op('dve', lambda e: e.tensor_tensor(out=R[:, jj, 0:TM], in0=tt[:, 0:TM], in1=ppj[:, 0:TM],
                                                               op=ALU.mult), [tk, ppk], ['R'])
                    else:
                        em.op('dve', lambda e: e.tensor_tensor(out=tt[:, 0:TM], in0=tt[:, 0:TM], in1=ppj[:, 0:TM],
                                                               op=ALU.mult), [tk, ppk], [tk])
                        em.op('pool', lambda e: e.tensor_tensor(out=R[:, jj, 0:TM], in0=R[:, jj, 0:TM], in1=tt[:, 0:TM],
                                                                op=ALU.add), ['R', tk], ['R'])
            em.op('act', lambda e: e.activation(out=MB[:], in_=R[:, :, 0:TM], func=AF.Copy), ['R'], ['MB'])
            for j2 in range(2):
                Wo, wok = wload(('wo', l, j2))
                for q in range(4):
                    i = j2 * 4 + q
                    p, k = PS[q % 2], f'ps{q % 2}'
                    for kc in range(8):
                        em.op('pe', lambda e: e.matmul(p[:, 0:TM], lhsT=Wo[:, kc * 512 + q * 128:kc * 512 + (q + 1) * 128],
                                                       rhs=MB[:, kc, :], start=(kc == 0), stop=(kc == 7)),
                              [wok, 'MB'], [k], inc=(kc == 7))
                    em.op('dve', lambda e: e.scalar_tensor_tensor(out=R[:, i, 0:TM], in0=p[:, 0:TM],
                                                                  scalar=mod(l, 1, 2, i, s), in1=XA[:, i, 0:TM],
                                                                  op0=ALU.mult, op1=ALU.add), [k, 'mods', 'XA'], ['R'])
            layer_norm_T(TM, f'ln_g{l}', f'ln_b{l}', 8, lambda i: XT[:, i, 0:TM], 'XT', banks=(3, 4))
            t0 = ti * TM
            em.dma('sp', 'stx', dst[:, t0:t0 + TM].rearrange("(c p) t -> p c t", p=128), XT[:, :, 0:TM], reads=['XT'],
                   writes=['xs2'])

        def ffn_stage(l_prev, l_next, s, src, dst, ntok, T, add_pos=False):
            for ti in range(ntok // T):
                t0 = ti * T
                em.dma('sp', 'ldx', XT[:, :, 0:T], src[:, t0:t0 + T].rearrange("(c p) t -> p c t", p=128),
                       reads=['xs', 'xs2'], writes=['XT'])
                if add_pos:
                    nr = T // 64
                    r0 = t0 // 64
                    for c in range(2):
                        for sc_ in range(2):
                            ch = sc_ * 2 + c
                            ch2 = 4 + sc_ * 2 + c
                            for r in range(nr):
                                em.op('dve', lambda e: e.tensor_scalar(out=XT[:, ch, r * 64:(r + 1) * 64],
                                                                       in0=XT[:, ch, r * 64:(r + 1) * 64],
                                                                       scalar1=ptab[:, c, sc_, r0 + r:r0 + r + 1],
                                                                       scalar2=None, op0=ALU.add), ['XT', 'ptab'], ['XT'])
                                em.op('pool', lambda e: e.tensor_tensor(out=XT[:, ch2, r * 64:(r + 1) * 64],
                                                                        in0=XT[:, ch2, r * 64:(r + 1) * 64],
                                                                        in1=ctab[:, c, sc_, :], op=ALU.add),
                                      ['XT', 'ptab'], ['XT'])
                if l_prev is not None:
                    ffn(l_prev, 1, s, T)
                if l_next is not None:
                    ffn(l_next, 0, s, T)
                em.dma('sp', 'stx', dst[:, t0:t0 + T].rearrange("(c p) t -> p c t", p=128), XT[:, :, 0:T], reads=['XT'],
                       writes=['xs', 'xs2'])

        xa_, xb_ = xs
        ca_, cb_ = cs
        for l in range(nl):
            if l > 0:
                em.new_epoch(['pe', 'act', 'dve'])
                em.depoch += 1
            last = (l == last_layer)
            em.dma('sp', 'prm', prl[:], prl_d[l], writes=['prm'])
            ffn_stage(l - 1 if l > 0 else None, l, 1, cin if l == 0 else cb_, ca_, NCTX, NCTX)
            ffn_stage(l - 1 if l > 0 else None, l, 0, xin if l == 0 else xb_, xa_, N, TF, add_pos=(l == 0))
            if stop == ('A', l):
                break
            em.op('dve', lambda e: e.memset(CB32[:], 0.0), [], ['CB32'])
            em.op('dve', lambda e: e.memset(CBB[:], 0.0), [], ['CBB'])
            for ti in reversed(range(NCTX // TM)):
                state_pass(l, 1, ca_, NCTX, ti, True, cb_c)
            for ti in reversed(range(NMT)):
                state_pass(l, 0, xa_, N, ti, True, cb_x)
            em.op('dve', lambda e: e.memset(CF32[:], 0.0), [], ['CF32'])
            em.op('dve', lambda e: e.memset(CFB[:], 0.0), [], ['CFB'])
            for ti in range(NCTX // TM):
                if last:
                    state_pass(l, 1, ca_, NCTX, ti, False, None)
                else:
                    mixer_tile(l, 1, ca_, cb_, NCTX, ti, cb_c)
            for ti in range(NMT):
                if ti == NMT // 2 and NMT >= 16:
                    em.new_epoch(['dve'])
                mixer_tile(l, 0, xa_, xb_, N, ti, cb_x)
        fin_src = xb_
        if stop is None:
            ffn_stage(nl - 1, None, 0, xb_, yout, N, TF)
        else:
            fin_src = xa_ if stop[0] == 'A' else xb_
            for ti in range(N // TF):
                t0 = ti * TF
                em.dma('sp', 'ldx', XT[:, :, 0:TF], fin_src[:, t0:t0 + TF].rearrange("(c p) t -> p c t", p=128),
                       reads=['xs', 'xs2'], writes=['XT'])
                em.dma('sp', 'stx', yout[:, t0:t0 + TF].rearrange("(c p) t -> p c t", p=128), XT[:, :, 0:TF],
                       reads=['XT'], writes=['xs', 'xs2'])
        em.deps('sp', ['xs', 'xs2'], [])
        build.last_counts = dict(em.cnt)
    return nc


def kernel(x, c, ctx, c_ctx, w_ada, b_ada, ln_g, ln_b, ffn_w_in, ffn_w_out, w_in, b_in,
           gmlp_ln_g, gmlp_ln_b, gmlp_ws, gmlp_bs, conv_w, conv_b, conv_ln_g, conv_ln_b,
           qk_conv_w, mlstm_ln_g, pool_w, pool_scale, w_branch, w_out, _nl=None, _stop=None):
    x = np.asarray(x, np.float32)
    B, N, _ = x.shape
    nl = _nl or DEPTH
    f = lambda a: np.ascontiguousarray(np.asarray(a, np.float32)[:nl])
    packs = [pack_params(nl, np.asarray(c)[b], np.asarray(c_ctx), *[np.asarray(a, np.float32) for a in (
        b_ada, ln_g, ln_b, b_in, gmlp_ln_g, gmlp_ln_b, gmlp_ws, gmlp_bs, conv_w, conv_b, conv_ln_g, conv_ln_b,
        qk_conv_w, mlstm_ln_g, pool_w, pool_scale)]) for b in range(B)]
    prm_off = packs[0][0].off
    prl_off = packs[0][1][0].off
    prms = [p[0].get() for p in packs]
    prls = [np.stack([q.get() for q in p[1]], axis=0) for p in packs]
    nc = build(N, nl, prm_off, prms[0].shape[1], prl_off, prls[0].shape[2], stop=_stop)
    shared = {"w_ada": f(w_ada), "ffn_w_in": f(ffn_w_in), "ffn_w_out": f(ffn_w_out), "w_in": f(w_in),
              "w_branch": f(w_branch), "w_out": f(w_out)}
    xT = [np.ascontiguousarray(x[b].T) for b in range(B)]
    cT = [np.ascontiguousarray(np.asarray(ctx, np.float32)[b].T) for b in range(B)]
    ncores = 2
    in_maps = []
    for i in range(ncores):
        b = i % B
        m = {"xT": xT[b], "ctxT": cT[b], "prm": prms[b], "prl": prls[b]}
        m.update(shared)
        in_maps.append(m)
    res = run_bass_kernel_spmd(nc, in_maps, core_ids=list(range(ncores)))
    out = np.stack([np.ascontiguousarray(res.results[b]["yT"].T) for b in range(B)], axis=0)
    return out.astype(np.float32)
```

```python
import math
from contextlib import ExitStack

import numpy as np
import concourse.bass as bass
import concourse.mybir as mybir
from concourse.bass_utils import run_bass_kernel_spmd

F32 = mybir.dt.float32
BF16 = mybir.dt.bfloat16
AF = mybir.ActivationFunctionType
ALU = mybir.AluOpType

D = 1024
DC = 8
DFF = 2816
FC = 22
MIX = 512
NCTX = 256
DEPTH = 4
ALPHA = (2 * DEPTH) ** 0.25
LN_EPS = 1e-6
DH = 128
LNS = math.log(DH ** -0.5)
OFF_A, OFF_B, OFF_C = 0, 1024, 2048
OFF_GATES = 3584
OFF_O = 3600
OFF_D = 4112
OFF_G = 4624
IN_COLS = 8720
HALO = 16
TM = 256
TMH = TM + 2 * HALO
TF = 256
GK = 2.0 * math.sqrt(2.0 / math.pi)


class Em:
    def __init__(self, nc, es):
        self.nc = nc
        self.es = es
        self.engs = {'pe': nc.tensor, 'act': nc.scalar, 'dve': nc.vector, 'pool': nc.gpsimd, 'sp': nc.sync}
        self.sems = {}
        self.cnt = {}
        self.cur = {}
        self.epoch = 0
        self.seen = {k: {} for k in self.engs}
        self.lastw = {}
        self.readers = {}
        self.depoch = 0
        self.new_epoch()

    def _mksem(self, name):
        return self.es.enter_context(self.nc.semaphore(name))

    def new_epoch(self, engs=None):
        self.epoch += 1
        for k in (engs or self.engs):
            if k == 'sp' and 'sp' in self.cur:
                continue
            key = f"{k}{self.epoch}"
            self.sems[key] = self._mksem("s_" + key)
            self.cnt[key] = 0
            self.cur[k] = key

    def dsem(self, slot):
        if slot not in self.sems:
            self.sems[slot] = self._mksem("d_" + slot)
            self.cnt[slot] = 0
        return self.sems[slot]

    def _wait(self, e, semkey, val):
        if self.seen[e].get(semkey, 0) >= val:
            return
        self.engs[e].wait_ge(self.sems[semkey], val)
        self.seen[e][semkey] = val

    def deps(self, e, reads, writes, skip_self=False):
        mine = self.cur[e]
        for k in reads:
            w = self.lastw.get(k)
            if w and not (skip_self and w[0] == mine):
                self._wait(e, *w)
        for k in writes:
            w = self.lastw.get(k)
            if w and not (skip_self and w[0] == mine):
                self._wait(e, *w)
            for sk, v in self.readers.get(k, {}).items():
                if not (skip_self and sk == mine):
                    self._wait(e, sk, v)

    def done(self, semkey, val, reads, writes):
        for k in reads:
            d = self.readers.setdefault(k, {})
            d[semkey] = max(d.get(semkey, 0), val)
        for k in writes:
            self.lastw[k] = (semkey, val)
            self.readers[k] = {}

    @staticmethod
    def _norm(reads, writes):
        r2 = [k for k in reads if not k.startswith('ps')]
        w2 = [k[:3] if k.startswith('ps') else k for k in writes]
        w2 += [k[:3] for k in reads if k.startswith('ps')]
        return r2, list(dict.fromkeys(w2))

    def op(self, e, fn, reads=(), writes=(), inc=True):
        reads, writes = self._norm(list(reads), list(writes))
        self.deps(e, reads, writes, e == 'pe')
        ins = fn(self.engs[e])
        key = self.cur[e]
        if inc:
            self.cnt[key] += 1
            ins.then_inc(self.sems[key], 1)
            self.done(key, self.cnt[key], reads, writes)
        else:
            self.done(key, self.cnt[key] + 1, reads, writes)
        return ins

    def barrier(self):
        for e in self.engs:
            for key, v in self.cnt.items():
                if v > 0:
                    self._wait(e, key, v)

    def dma(self, e, slot, out, in_, reads=(), writes=()):
        self.deps(e, reads, writes)
        sem = self.dsem(slot)
        ins = self.engs[e].dma_start(out=out, in_=in_)
        self.cnt[slot] += 16
        ins.then_inc(sem, 16)
        self.done(slot, self.cnt[slot], reads, writes)
        return ins


class Packer:
    def __init__(self):
        self.off = {}
        self.parts = []
        self.n = 0

    def add(self, name, arr):
        arr = np.ascontiguousarray(arr, dtype=np.float32).reshape(128, -1)
        self.off[name] = (self.n, arr.shape[1])
        self.parts.append(arr)
        self.n += arr.shape[1]

    def get(self):
        return np.concatenate(self.parts, axis=1)


def fm(v):
    v = np.asarray(v, np.float32)
    n = v.shape[-1] // 128
    return np.moveaxis(v.reshape(v.shape[:-1] + (n, 128)), -1, 0)


def bc(v):
    v = np.asarray(v, np.float32)
    return np.broadcast_to(v[None], (128,) + v.shape)


def pack_params(nl, c_b, c_ctx, b_ada, ln_g, ln_b, b_in, gmlp_ln_g, gmlp_ln_b, gmlp_ws, gmlp_bs, conv_w, conv_b,
                conv_ln_g, conv_ln_b, qk_conv_w, mlstm_ln_g, pool_w, pool_scale):
    pk = Packer()
    pk.add('cvec', np.stack([fm(c_b), fm(c_ctx)], axis=-1))
    ii = np.arange(128, dtype=np.float32)
    pk.add('ident', np.eye(128, dtype=np.float32))
    pk.add('triu', (ii[:, None] <= ii[None, :]).astype(np.float32))
    pk.add('tril', (ii[:, None] >= ii[None, :]).astype(np.float32))
    pk.add('ones', np.ones((128, 128), np.float32))
    pk.add('pidx', ii[:, None])
    pk.add('ridx', bc(np.arange(128, dtype=np.float32)))
    pk.add('cidx', bc(np.arange(64, dtype=np.float32)))
    for l in range(nl):
        pk.add(f'b_ada{l}', fm(b_ada[l]))
        pk.add(f'ln_g{l}', fm(ln_g[l]))
        pk.add(f'ln_b{l}', fm(ln_b[l]))
    pls = []
    for l in range(nl):
        pl = Packer()
        bi = b_in[l]
        fmcols = np.concatenate([bi[1024:2048], bi[2048:3072], bi[OFF_D:OFF_D + 512], bi[0:512],
                                 bi[OFF_O:OFF_O + 512], bi[OFF_G:OFF_G + 4096]])
        pl.add('b_fm', fm(fmcols))
        pl.add('b_gv', bc(bi[512:1024]))
        pl.add('b_mv', bc(bi[3072:3584]))
        pl.add('b_gt', bc(bi[OFF_GATES:OFF_GATES + 16]))
        pl.add('gln_g', bc(gmlp_ln_g[l]))
        pl.add('gln_b', bc(gmlp_ln_b[l]))
        pl.add('wsT', np.transpose(gmlp_ws[l], (2, 0, 1)))
        pl.add('bs', bc(gmlp_bs[l]))
        pl.add('conv_w', np.transpose(fm(conv_w[l]), (0, 2, 1)))
        pl.add('conv_b', fm(conv_b[l]))
        pl.add('cln_g', fm(conv_ln_g[l]))
        pl.add('cln_b', fm(conv_ln_b[l]))
        pl.add('qkw', np.transpose(fm(qk_conv_w[l]), (0, 2, 1)))
        pl.add('mln_g', fm(mlstm_ln_g[l]))
        pl.add('pool_w', np.transpose(pool_w[l], (1, 0, 2)))
        pl.add('pool_s', fm(pool_scale[l]))
        pls.append(pl)
    return pk, pls


def build(N, nl, prm_off, nprm, prl_off, nprl, stop=None):
    nc = bass.Bass("TRN2", target_bir_lowering=False)
    last_layer = nl - 1
    NFT = N // TF
    NMT = N // TM
    xin = nc.dram_tensor("xT", [D, N], F32, kind="ExternalInput").ap()
    cin = nc.dram_tensor("ctxT", [D, NCTX], F32, kind="ExternalInput").ap()
    prm_d = nc.dram_tensor("prm", [128, nprm], F32, kind="ExternalInput").ap()
    prl_d = nc.dram_tensor("prl", [nl, 128, nprl], F32, kind="ExternalInput").ap()
    w_ada = nc.dram_tensor("w_ada", [nl, D, 9 * D], F32, kind="ExternalInput").ap()
    ffn_w_in = nc.dram_tensor("ffn_w_in", [nl, 2, D, 2 * DFF], F32, kind="ExternalInput").ap()
    ffn_w_out = nc.dram_tensor("ffn_w_out", [nl, 2, DFF, D], F32, kind="ExternalInput").ap()
    w_in = nc.dram_tensor("w_in", [nl, D, IN_COLS], F32, kind="ExternalInput").ap()
    w_branch = nc.dram_tensor("w_branch", [nl, 4, MIX, D], F32, kind="ExternalInput").ap()
    w_out = nc.dram_tensor("w_out", [nl, D, D], F32, kind="ExternalInput").ap()
    yout = nc.dram_tensor("yT", [D, N], F32, kind="ExternalOutput").ap()
    xs = [nc.dram_tensor(f"xs{i}", [D, N], F32).ap() for i in range(2)]
    cs = [nc.dram_tensor(f"cs{i}", [D, NCTX], F32).ap() for i in range(2)]
    NCH = N // 128
    cb_x = nc.dram_tensor("cb_x", [NCH, 128, 4 * 130], BF16).ap()
    cb_c = nc.dram_tensor("cb_c", [NCTX // 128, 128, 4 * 130], BF16).ap()
    UPL = 2 * 11 + 2 * 8 + 18 + 4 + 2
    wq = nc.dram_tensor("wq", [nl * UPL, 128, 4096], BF16).ap()

    es = ExitStack()
    with es:
        em = Em(nc, es)
        _n = [0]

        def sb(shape, dt, name=None):
            _n[0] += 1
            return es.enter_context(nc.sbuf_tensor(name or f"t{_n[0]}", shape, dt))

        PS = [es.enter_context(nc.psum_tensor(f"ps{i}", [128, 512], F32)) for i in range(8)]
        prm = sb([128, nprm], F32, "prm_sb")
        prl = sb([128, nprl], F32, "prl_sb")

        def pp(name, a=None, b=None):
            import re as _re
            m = _re.match(r"^(.*?)(\d+)$", name)
            if name in prm_off:
                o, n = prm_off[name]
                buf = prm
            else:
                base = m.group(1) if (m and m.group(1) in prl_off) else name
                o, n = prl_off[base]
                buf = prl
            if a is None:
                return buf[:, o:o + n]
            return buf[:, o + a:o + (b if b is not None else a + 1)]

        em.dma('sp', 'prm', prm[:], prm_d, writes=['prm'])

        ident = pp('ident')
        triu = pp('triu')
        tril = pp('tril')
        ones = pp('ones')

        sc = sb([128, 8, 2], F32, "sc")
        mods = sb([128, nl, 72, 2], F32, "mods")
        es_pro = ExitStack()
        stg = [es_pro.enter_context(nc.sbuf_tensor(f"stg{i}", [128, 4096], F32)) for i in range(2)]
        stb = [es_pro.enter_context(nc.sbuf_tensor(f"stb{i}", [128, 4096], BF16)) for i in range(2)]
        ucount = [0]
        cast_eng = ['act', 'dve']

        def prologue_unit(uidx, srcs):
            i = ucount[0] % 2
            ucount[0] += 1
            tot = 0
            for (o, a, b, ap) in srcs:
                dst = stg[i][:, o:o + a * b].rearrange("p (a b) -> p a b", a=a, b=b)
                em.dma('sp', f'pl{i}', dst, ap, writes=[f'stg{i}'])
                tot = max(tot, o + a * b)
            ce = cast_eng[uidx % 2]
            if ce == 'act':
                em.op('act', lambda e: e.activation(out=stb[i][:, 0:tot], in_=stg[i][:, 0:tot], func=AF.Copy),
                      [f'stg{i}'], [f'stb{i}'])
            else:
                em.op(ce, lambda e: e.tensor_copy(out=stb[i][:, 0:tot], in_=stg[i][:, 0:tot]),
                      [f'stg{i}'], [f'stb{i}'])
            em.dma('pool', f'plst{i}', wq[uidx, :, 0:tot], stb[i][:, 0:tot], reads=[f'stb{i}'], writes=[f'wq{i}'])

        def kc_ap(w2d, c0, ncols):
            return w2d[:, c0:c0 + ncols].rearrange("(kc p) c -> p kc c", p=128)

        UNIT = {}
        u = 0
        for l in range(nl):
            for f in range(2):
                for j in range(11):
                    w2 = ffn_w_in[l, f]
                    prologue_unit(u, [(0, 8, 256, kc_ap(w2, j * 256, 256)),
                                      (2048, 8, 256, kc_ap(w2, DFF + j * 256, 256))])
                    UNIT[('fi', l, f, j)] = u
                    u += 1
                for i in range(8):
                    prologue_unit(u, [(0, 22, 128, kc_ap(ffn_w_out[l, f], i * 128, 128))])
                    UNIT[('fo', l, f, i)] = u
                    u += 1
            incols = [1024, 1536, 2048, 2560, OFF_D, 0, OFF_O, 512, 3072] + [OFF_G + 512 * i for i in range(8)]
            for j, c0 in enumerate(incols):
                prologue_unit(u, [(0, 8, 512, kc_ap(w_in[l], c0, 512))])
                UNIT[('in', l, j)] = u
                u += 1
            prologue_unit(u, [(0, 8, 16, kc_ap(w_in[l], OFF_GATES, 16))])
            UNIT[('in', l, 'gt')] = u
            u += 1
            for b in range(4):
                prologue_unit(u, [(0, 4, 1024, kc_ap(w_branch[l, b], 0, 1024))])
                UNIT[('br', l, b)] = u
                u += 1
            for j in range(2):
                prologue_unit(u, [(0, 8, 512, kc_ap(w_out[l], j * 512, 512))])
                UNIT[('wo', l, j)] = u
                u += 1
        assert u == nl * UPL, (u, nl * UPL)

        cv = pp('cvec').rearrange("p (k t) -> p k t", k=8, t=2)
        em.op('act', lambda e: e.activation(out=sc[:], in_=cv, func=AF.Silu), ['prm'], ['sc'])
        for l in range(nl):
            for og in range(18):
                i = ucount[0] % 2
                ucount[0] += 1
                em.dma('sp', f'pl{i}', stg[i][:].rearrange("p (a b) -> p a b", a=8, b=512),
                       kc_ap(w_ada[l], og * 512, 512), writes=[f'stg{i}'])
                for q in range(4):
                    oc = og * 4 + q
                    for kc in range(8):
                        em.op('pe', lambda e: e.matmul(PS[0][:, oc * 2:oc * 2 + 2],
                                                       lhsT=stg[i][:, kc * 512 + q * 128: kc * 512 + (q + 1) * 128],
                                                       rhs=sc[:, kc, :], start=(kc == 0), stop=(kc == 7)),
                              [f'stg{i}', 'sc'], ['ps0'], inc=(kc == 7))
            ba = pp(f'b_ada{l}')
            for t in range(2):
                em.op('dve', lambda e: e.tensor_tensor(out=mods[:, l, :, t],
                                                       in0=PS[0][:, 0:144].rearrange("p (c t) -> p c t", t=2)[:, :, t],
                                                       in1=ba, op=ALU.add), ['ps0', 'prm'], ['mods'])
        for l in range(nl):
            for j in range(3):
                r = 1.0 if j == 1 else 0.5
                em.op('dve', lambda e: e.tensor_scalar(out=mods[:, l, (j * 3 + 1) * 8:(j * 3 + 2) * 8, :],
                                                       in0=mods[:, l, (j * 3 + 1) * 8:(j * 3 + 2) * 8, :],
                                                       scalar1=1.0, scalar2=None, op0=ALU.add), ['mods'], ['mods'])
                em.op('dve', lambda e: e.tensor_scalar(out=mods[:, l, (j * 3 + 2) * 8:(j * 3 + 3) * 8, :],
                                                       in0=mods[:, l, (j * 3 + 2) * 8:(j * 3 + 3) * 8, :],
                                                       scalar1=r, scalar2=None, op0=ALU.mult), ['mods'], ['mods'])

        em.barrier()
        es_pro.close()

        def mod(l, j, t, i, s):
            c = (j * 3 + t) * 8 + i
            return mods[:, l, c, s:s + 1]

        RING = 4
        ring = [sb([128, 4096], BF16, f"ring{i}") for i in range(RING)]
        rcount = [0]

        def wload(key, ncols=4096):
            i = rcount[0] % RING
            rcount[0] += 1
            em.dma('sp', f'wr{i}_{em.depoch}', ring[i][:, 0:ncols], wq[UNIT[key], :, 0:ncols], reads=['wq0', 'wq1'], writes=[f'ring{i}'])
            return ring[i], f'ring{i}'

        ptab = sb([128, 2, 2, 128], F32, "ptab")
        ctab = sb([128, 2, 2, 64], F32, "ctab")
        frq = sb([128, 2], F32, "frq")
        ang = sb([128, 128], F32, "ang")
        angi = sb([128, 128], mybir.dt.int32, "angi")
        angf = sb([128, 128], F32, "angf")
        for c in range(2):
            em.op('dve', lambda e: e.tensor_scalar(out=frq[:, c:c + 1], in0=pp('pidx'), scalar1=float(c * 128),
                                                   scalar2=None, op0=ALU.add), ['prm'], ['frq'])
        em.op('act', lambda e: e.activation(out=frq[:], in_=frq[:], func=AF.Exp, scale=-math.log(10000.0) / 256.0),
              ['frq'], ['frq'])

        def sincos(dst, idx_ap, n, c, phase):
            em.op('dve', lambda e: e.tensor_scalar(out=ang[:, 0:n], in0=idx_ap, scalar1=frq[:, c:c + 1],
                                                   scalar2=1.0 / (2 * math.pi), op0=ALU.mult, op1=ALU.mult),
                  ['prm', 'frq'], ['ang'])
            if phase:
                em.op('dve', lambda e: e.tensor_scalar(out=ang[:, 0:n], in0=ang[:, 0:n], scalar1=phase,
                                                       scalar2=None, op0=ALU.add), ['ang'], ['ang'])
            em.op('dve', lambda e: e.tensor_copy(out=angi[:, 0:n], in_=ang[:, 0:n]), ['ang'], ['angi'])
            em.op('dve', lambda e: e.tensor_copy(out=angf[:, 0:n], in_=angi[:, 0:n]), ['angi'], ['angf'])
            em.op('dve', lambda e: e.tensor_tensor(out=ang[:, 0:n], in0=ang[:, 0:n], in1=angf[:, 0:n],
                                                   op=ALU.subtract), ['ang', 'angf'], ['ang'])
            em.op('dve', lambda e: e.tensor_scalar(out=angf[:, 0:n], in0=ang[:, 0:n], scalar1=0.5, scalar2=None,
                                                   op0=ALU.is_gt), ['ang'], ['angf'])
            em.op('dve', lambda e: e.tensor_tensor(out=ang[:, 0:n], in0=ang[:, 0:n], in1=angf[:, 0:n],
                                                   op=ALU.subtract), ['ang', 'angf'], ['ang'])
            em.op('dve', lambda e: e.tensor_scalar(out=angf[:, 0:n], in0=ang[:, 0:n], scalar1=-0.5, scalar2=None,
                                                   op0=ALU.is_lt), ['ang'], ['angf'])
            em.op('dve', lambda e: e.tensor_tensor(out=ang[:, 0:n], in0=ang[:, 0:n], in1=angf[:, 0:n],
                                                   op=ALU.add), ['ang', 'angf'], ['ang'])
            em.op('act', lambda e: e.activation(out=dst, in_=ang[:, 0:n], func=AF.Sin, scale=2 * math.pi),
                  ['ang'], ['ptab'])

        for c in range(2):
            sincos(ptab[:, c, 0, :], pp('ridx'), 128, c, 0.0)
            sincos(ptab[:, c, 1, :], pp('ridx'), 128, c, 0.25)
            sincos(ctab[:, c, 0, :], pp('cidx'), 64, c, 0.0)
            sincos(ctab[:, c, 1, :], pp('cidx'), 64, c, 0.25)

        XT = sb([128, 8, TMH], F32, "XT")
        XTB = sb([128, 8, TMH], F32, "XTb")
        XA = sb([128, 8, TMH], F32, "XA")
        HT = sb([128, 8, TMH], BF16, "HT")
        G = sb([128, FC, TF], BF16, "G")
        R = sb([128, 8, TF], F32, "R")
        SQ = sb([128, 8, TF], F32, "SQ")
        T1 = sb([128, 512], F32, "T1")
        T2 = sb([128, 512], F32, "T2")
        MEAN = sb([128, 512], F32, "MEAN")
        RSTD = sb([128, 512], F32, "RSTD")

        def layer_norm_T(T, gname, bname, gi, dst, key_dst, nch=8, src=R, src_key='R', silu=False, banks=(6, 7), tmp=None):
            nf = float(nch * 128)
            TT_, ttk = tmp if tmp is not None else (T1, 'T1')
            BK7 = ['ps7', 'ps7r0', 'ps7r1', 'ps7r2', 'ps7c']
            pa, pb_ = PS[banks[0]], PS[banks[1]]
            ka = BK7 if banks[0] == 7 else [f'ps{banks[0]}']
            kb = BK7 if banks[1] == 7 else [f'ps{banks[1]}']
            em.op('act', lambda e: e.activation(out=SQ[:, 0:nch, 0:T], in_=src[:, 0:nch, 0:T], func=AF.Square),
                  [src_key], ['SQ'])
            for i in range(nch):
                em.op('pe', lambda e: e.matmul(pa[:, 0:T], lhsT=ones, rhs=src[:, i, 0:T], start=(i == 0),
                                               stop=(i == nch - 1)), [src_key, 'prm'], ka, inc=(i == nch - 1))
            for i in range(nch):
                em.op('pe', lambda e: e.matmul(pb_[:, 0:T], lhsT=ones, rhs=SQ[:, i, 0:T], start=(i == 0),
                                               stop=(i == nch - 1)), ['SQ', 'prm'], kb, inc=(i == nch - 1))
            em.op('act', lambda e: e.activation(out=MEAN[:, 0:T], in_=pa[:, 0:T], func=AF.Copy, scale=1.0 / nf),
                  ka, ['MEAN'])
            em.op('dve', lambda e: e.tensor_tensor(out=TT_[:, 0:T], in0=MEAN[:, 0:T], in1=MEAN[:, 0:T], op=ALU.mult),
                  ['MEAN'], [ttk])
            em.op('dve', lambda e: e.scalar_tensor_tensor(out=TT_[:, 0:T], in0=pb_[:, 0:T], scalar=1.0 / nf,
                                                          in1=TT_[:, 0:T], op0=ALU.mult, op1=ALU.subtract),
                  kb + [ttk], [ttk])
            em.op('act', lambda e: e.activation(out=TT_[:, 0:T], in_=TT_[:, 0:T], func=AF.Sqrt, bias=LN_EPS),
                  [ttk], [ttk])
            em.op('dve', lambda e: e.reciprocal(out=RSTD[:, 0:T], in_=TT_[:, 0:T]), [ttk], ['RSTD'])
            for i in range(nch):
                em.op('dve', lambda e: e.tensor_tensor(out=SQ[:, i, 0:T], in0=src[:, i, 0:T], in1=MEAN[:, 0:T],
                                                       op=ALU.subtract), [src_key, 'MEAN'], ['SQ'])
                em.op('dve', lambda e: e.tensor_tensor(out=SQ[:, i, 0:T], in0=SQ[:, i, 0:T], in1=RSTD[:, 0:T],
                                                       op=ALU.mult), ['SQ', 'RSTD'], ['SQ'])
                em.op('act', lambda e: e.activation(out=dst(i), in_=SQ[:, i, 0:T],
                                                    func=AF.Silu if silu else AF.Identity,
                                                    scale=pp(gname, gi + i), bias=pp(bname, gi + i)),
                      ['SQ', 'prm'], [key_dst])

        def modulate(l, j, s, T, c0=0):
            for i in range(8):
                em.op('dve', lambda e: e.tensor_scalar(out=HT[:, i, 0:T], in0=XT[:, i, 0:T], scalar1=mod(l, j, 1, i, s),
                                                       scalar2=mod(l, j, 0, i, s), op0=ALU.mult, op1=ALU.add),
                      ['XT', 'mods'], ['HT'])
            em.op('pool', lambda e: e.tensor_scalar(out=XA[:, :, 0:T - 2 * c0], in0=XT[:, :, c0:T - c0], scalar1=ALPHA,
                                                    scalar2=0.0, op0=ALU.mult, op1=ALU.add), ['XT'], ['XA'])

        def ffn(l, f, s, T):
            j = 0 if f == 0 else 2
            modulate(l, j, s, T)
            for uu in range(11):
                W, wk = wload(('fi', l, f, uu))
                for q in range(2):
                    jj = uu * 2 + q
                    p1, p2 = PS[(jj % 2) * 2], PS[(jj % 2) * 2 + 1]
                    k1, k2 = f'ps{(jj % 2) * 2}', f'ps{(jj % 2) * 2 + 1}'
                    for kc in range(8):
                        em.op('pe', lambda e: e.matmul(p1[:, 0:T], lhsT=W[:, kc * 256 + q * 128:kc * 256 + (q + 1) * 128],
                                                       rhs=HT[:, kc, 0:T], start=(kc == 0), stop=(kc == 7)),
                              [wk, 'HT'], [k1], inc=(kc == 7))
                    for kc in range(8):
                        em.op('pe', lambda e: e.matmul(p2[:, 0:T],
                                                       lhsT=W[:, 2048 + kc * 256 + q * 128:2048 + kc * 256 + (q + 1) * 128],
                                                       rhs=HT[:, kc, 0:T], start=(kc == 0), stop=(kc == 7)),
                              [wk, 'HT'], [k2], inc=(kc == 7))
                    tt, tk = (T1, 'T1') if jj % 2 == 0 else (T2, 'T2')
                    em.op('act', lambda e: e.activation(out=tt[:, 0:T], in_=p1[:, 0:T], func=AF.Silu), [k1], [tk])
                    em.op('dve', lambda e: e.tensor_tensor(out=G[:, jj, 0:T], in0=tt[:, 0:T], in1=p2[:, 0:T],
                                                           op=ALU.mult), [tk, k2], ['G'])
            for i in range(8):
                W, wk = wload(('fo', l, f, i), 22 * 128)
                p, k = PS[4 + (i % 2)], f'ps{4 + (i % 2)}'
                for jj in range(FC):
                    em.op('pe', lambda e: e.matmul(p[:, 0:T], lhsT=W[:, jj * 128:(jj + 1) * 128], rhs=G[:, jj, 0:T],
                                                   start=(jj == 0), stop=(jj == FC - 1)), [wk, 'G'], [k],
                          inc=(jj == FC - 1))
                em.op('dve', lambda e: e.scalar_tensor_tensor(out=R[:, i, 0:T], in0=p[:, 0:T], scalar=mod(l, j, 2, i, s),
                                                              in1=XA[:, i, 0:T], op0=ALU.mult, op1=ALU.add),
                      [k, 'mods', 'XA'], ['R'])
            layer_norm_T(T, f'ln_g{l}', f'ln_b{l}', j * 8, lambda i: XT[:, i, 0:T], 'XT')

        CST = sb([128, 8, TMH], F32, "CST")
        CACC = sb([128, 4, TM], F32, "CACC")
        CA = sb([128, 4, TMH], F32, "CA")
        QT = sb([128, 4, TM], BF16, "QT")
        KTb = sb([128, 4, TM], BF16, "KTb")
        KTf = sb([128, 4, TM], F32, "KTf")
        UT = sb([128, 4, TM], F32, "UT")
        SO = sb([128, 4, TM], F32, "SO")
        VN = sb([128, 2, 512], BF16, "VN")
        VT = sb([128, 2, 512], F32, "VT")
        VE = sb([128, 2, 4, 130], BF16, "VE")
        YB = sb([128, 4, 4, TM], BF16, "YB")
        MB = sb([128, 8, TM], BF16, "MB")
        GT = sb([128, 16], F32, "GT")
        EG = sb([128, 16], F32, "EG")
        LF = sb([128, 8], F32, "LF")
        ARG = sb([128, 8, 4], F32, "ARG")
        EX = sb([128, 8, 4], F32, "EX")
        SF = sb([128, 4, 128], BF16, "SF")
        SB_ = sb([128, 4, 128], BF16, "SBk")
        KW = sb([128, 4, 128], BF16, "KW")
        CF32 = sb([128, 4, 130], F32, "CF32")
        CFB = sb([128, 4, 130], BF16, "CFB")
        CB32 = sb([128, 4, 130], F32, "CB32")
        CBB = sb([128, 4, 130], BF16, "CBB")
        CBL = [sb([128, 4, 130], BF16, f"CBL{i}") for i in range(2)]
        HS = sb([128, 4, 128], F32, "HS")
        HN = sb([128, 4, 128], F32, "HN")
        BST = sb([128, 4, 6], F32, "BST")
        MV = sb([128, 4, 2], F32, "MV")
        SM = sb([128, 16], F32, "SM")
        WST = sb([128, 4, 128], BF16, "WST")
        PWB = sb([128, 4, 128], BF16, "PWB")
        PT = [sb([128, TMH], F32, f"PT{i}") for i in range(3)]
        em.op('pool', lambda e: e.memset(VE[:], 1.0), [], ['VE'])

        def inproj_fm(l, ukey, T, cols, bias_c0, evac):
            W, wk = wload(ukey)
            for q in range(4):
                p, k = PS[q % 3], f'ps{q % 3}'
                for kc in range(8):
                    em.op('pe', lambda e: e.matmul(p[:, 0:T], lhsT=W[:, kc * 512 + q * 128:kc * 512 + (q + 1) * 128],
                                                   rhs=HT[:, kc, cols[0]:cols[1]], start=(kc == 0), stop=(kc == 7)),
                          [wk, 'HT'], [k], inc=(kc == 7))
                evac(q, p, k, pp(f'b_fm{l}', bias_c0 + q))

        def inproj_tm(l, ukey, ncols, c, evac, wcols=512):
            W, wk = ukey
            p, k = PS[c % 2], f'ps{c % 2}'
            t0 = HALO + c * 128
            for kc in range(8):
                em.op('pe', lambda e: e.matmul(p[:, 0:ncols], lhsT=HT[:, kc, t0:t0 + 128],
                                               rhs=W[:, kc * wcols:kc * wcols + ncols], start=(kc == 0), stop=(kc == 7)),
                      [wk, 'HT'], [k], inc=(kc == 7))
            evac(p, k)

        def zero_halo(buf, key, q, first, lastt):
            if first:
                em.op('pool', lambda e: e.memset(buf[:, q, 0:HALO], 0.0), [], [key])
            if lastt:
                em.op('pool', lambda e: e.memset(buf[:, q, HALO + TM:TMH], 0.0), [], [key])

        def load_tile_halo(src, ntok, ti):
            t0 = ti * TM
            lo = max(t0 - HALO, 0)
            hi = min(t0 + TM + HALO, ntok)
            if lo > t0 - HALO:
                em.op('pool', lambda e: e.memset(XT[:, :, 0:HALO], 0.0), [], ['XT'])
            if hi < t0 + TM + HALO:
                em.op('pool', lambda e: e.memset(XT[:, :, HALO + TM:TMH], 0.0), [], ['XT'])
            em.dma('sp', 'ldx', XT[:, :, lo - (t0 - HALO):hi - (t0 - HALO)],
                   src[:, lo:hi].rearrange("(c p) t -> p c t", p=128), reads=['xs'], writes=['XT'])

        def gate_scalars(l, c, W, wk):
            def ev(p, k):
                em.op('dve', lambda e: e.tensor_tensor(out=GT[:], in0=p[:, 0:16], in1=pp(f'b_gt{l}'), op=ALU.add),
                      [k, 'prm'], ['GT'])
            inproj_tm(l, (W, wk), 16, c, ev, wcols=16)
            em.op('act', lambda e: e.activation(out=EG[:], in_=GT[:], func=AF.Exp, scale=-1.0), ['GT'], ['EG'])
            em.op('act', lambda e: e.activation(out=EG[:], in_=EG[:], func=AF.Ln, bias=1.0), ['EG'], ['EG'])
            em.op('dve', lambda e: e.tensor_scalar(out=LF[:, 0:4], in0=EG[:, 4:8], scalar1=-1.0, scalar2=None,
                                                   op0=ALU.mult), ['EG'], ['LF'])
            em.op('dve', lambda e: e.tensor_scalar(out=LF[:, 4:8], in0=EG[:, 12:16], scalar1=-1.0, scalar2=None,
                                                   op0=ALU.mult), ['EG'], ['LF'])
            pc = PS[7]
            for n, m in enumerate([triu, tril, ones]):
                em.op('pe', lambda e: e.matmul(pc[:, 400 + n * 8:400 + n * 8 + 8], lhsT=m, rhs=LF[:], start=True,
                                               stop=True), ['LF', 'prm'], ['ps7c'])
            bf, bb = pc[:, 400:404], pc[:, 412:416]
            totf, totb = pc[:, 416:420], pc[:, 420:424]
            em.op('dve', lambda e: e.scalar_tensor_tensor(out=ARG[:, 0, :], in0=GT[:, 0:4], scalar=LNS, in1=bf,
                                                          op0=ALU.add, op1=ALU.subtract), ['GT', 'ps7c'], ['ARG'])
            em.op('dve', lambda e: e.scalar_tensor_tensor(out=ARG[:, 1, :], in0=GT[:, 8:12], scalar=LNS, in1=bb,
                                                          op0=ALU.add, op1=ALU.subtract), ['GT', 'ps7c'], ['ARG'])
            em.op('dve', lambda e: e.tensor_copy(out=ARG[:, 2, :], in_=bf), ['ps7c'], ['ARG'])
            em.op('dve', lambda e: e.tensor_copy(out=ARG[:, 3, :], in_=bb), ['ps7c'], ['ARG'])
            em.op('dve', lambda e: e.tensor_tensor(out=ARG[:, 4, :], in0=ARG[:, 0, :], in1=totf, op=ALU.add),
                  ['ARG', 'ps7c'], ['ARG'])
            em.op('dve', lambda e: e.tensor_tensor(out=ARG[:, 5, :], in0=ARG[:, 1, :], in1=totb, op=ALU.add),
                  ['ARG', 'ps7c'], ['ARG'])
            em.op('dve', lambda e: e.tensor_copy(out=ARG[:, 6, :], in_=totf), ['ps7c'], ['ARG'])
            em.op('dve', lambda e: e.tensor_copy(out=ARG[:, 7, :], in_=totb), ['ps7c'], ['ARG'])
            em.op('act', lambda e: e.activation(out=EX[:], in_=ARG[:], func=AF.Exp), ['ARG'], ['EX'])

        def qk_conv(l, q, qi, dst_list):
            w = lambda j: pp(f'qkw{l}', qi * 3 + j)
            em.op('dve', lambda e: e.tensor_scalar(out=T1[:, 0:TM], in0=CST[:, q, HALO - 1:HALO - 1 + TM], scalar1=w(0),
                                                   scalar2=None, op0=ALU.mult), ['CST', 'prm'], ['T1'])
            em.op('dve', lambda e: e.scalar_tensor_tensor(out=T1[:, 0:TM], in0=CST[:, q, HALO:HALO + TM], scalar=w(1),
                                                          in1=T1[:, 0:TM], op0=ALU.mult, op1=ALU.add),
                  ['CST', 'prm', 'T1'], ['T1'])
            em.op('dve', lambda e: e.scalar_tensor_tensor(out=T1[:, 0:TM], in0=CST[:, q, HALO + 1:HALO + 1 + TM],
                                                          scalar=w(2), in1=T1[:, 0:TM], op0=ALU.mult, op1=ALU.add),
                  ['CST', 'prm', 'T1'], ['T1'])
            for dst, key in dst_list:
                em.op('act', lambda e: e.activation(out=dst, in_=T1[:, 0:TM], func=AF.Silu), ['T1'], [key])

        def evac_bias(buf, key, qoff, first, lastt):
            def ev(q, p, k, b):
                em.op('act', lambda e: e.activation(out=buf[:, qoff + q, 0:TMH], in_=p[:, 0:TMH], func=AF.Identity,
                                                    bias=b), [k, 'prm'], [key])
                zero_halo(buf, key, qoff + q, first, lastt)
            return ev

        def k_transposed_scaled(c, h, grp):
            em.op('pe', lambda e: e.transpose(PS[4][:, h * 128:(h + 1) * 128], KTf[:, h, c * 128:(c + 1) * 128], ident),
                  ['KTf', 'prm'], ['ps4'])
            em.op('dve', lambda e: e.tensor_scalar(out=KW[:, h, :], in0=PS[4][:, h * 128:(h + 1) * 128],
                                                   scalar1=EX[:, grp, h:h + 1], scalar2=None, op0=ALU.mult),
                  ['ps4', 'EX'], ['KW'])

        def state_update(c, h, C32, Cb, key32, keyb, grp_dec):
            reg = h % 3
            pr = PS[7][:, reg * 130:reg * 130 + 129]
            em.op('pe', lambda e: e.matmul(pr, lhsT=KW[:, h, :], rhs=VE[:, c, h, 0:129], start=True, stop=True),
                  ['KW', 'VE'], [f'ps7r{reg}'])
            em.op('dve', lambda e: e.scalar_tensor_tensor(out=C32[:, h, 0:129], in0=C32[:, h, 0:129],
                                                          scalar=EX[:, grp_dec, h:h + 1], in1=pr, op0=ALU.mult,
                                                          op1=ALU.add), [key32, 'EX', f'ps7r{reg}'], [key32])
            em.op('pool', lambda e: e.tensor_copy(out=Cb[:, h, 0:129], in_=C32[:, h, 0:129]), [key32], [keyb])

        def state_pass(l, s, src, ntok, ti, backward, save_ap):
            first, lastt = ti == 0, ti == ntok // TM - 1
            load_tile_halo(src, ntok, ti)
            modulate(l, 1, s, TMH, c0=HALO)
            inproj_fm(l, ('in', l, 3), TMH, (0, TMH), 12, evac_bias(CST, 'CST', 4, first, lastt))
            for q in range(4):
                qk_conv(l, 4 + q, 4 + q, [(KTf[:, q, :], 'KTf')])
            Wg, wgk = wload(('in', l, 'gt'), 128)
            cs_order = [1, 0] if backward else [0, 1]
            Wv_ = wload(('in', l, 8))
            for c in cs_order:
                W, wk = Wv_

                def ev(p, k, c=c):
                    em.op('dve', lambda e: e.tensor_tensor(out=VE[:, c, :, 0:128],
                                                           in0=p[:, 0:512].rearrange("p (h d) -> p h d", h=4),
                                                           in1=pp(f'b_mv{l}').rearrange("p (h d) -> p h d", h=4),
                                                           op=ALU.add), [k, 'prm'], ['VE'])
                inproj_tm(l, (W, wk), 512, c, ev)
            for c in cs_order:
                gate_scalars(l, c, Wg, wgk)
                if backward:
                    gc = ti * 2 + c
                    em.dma('sp', 'stcb', save_ap[gc].rearrange("p (h d) -> p h d", h=4), CBB[:], reads=['CBB'],
                           writes=['cbscr'])
                for h in range(4):
                    k_transposed_scaled(c, h, 5 if backward else 4)
                    if backward:
                        state_update(c, h, CB32, CBB, 'CB32', 'CBB', 7)
                    else:
                        state_update(c, h, CF32, CFB, 'CF32', 'CFB', 6)

        def gelu_T(dst, src_ap, src_keys, dkey, T, tmp, tkey):
            em.op('act', lambda e: e.activation(out=tmp, in_=src_ap, func=AF.Square), src_keys, [tkey])
            em.op('dve', lambda e: e.tensor_scalar(out=tmp, in0=tmp, scalar1=0.044715, scalar2=1.0, op0=ALU.mult,
                                                   op1=ALU.add), [tkey], [tkey])
            em.op('dve', lambda e: e.tensor_tensor(out=tmp, in0=tmp, in1=src_ap, op=ALU.mult), [tkey] + src_keys,
                  [tkey])
            em.op('act', lambda e: e.activation(out=tmp, in_=tmp, func=AF.Sigmoid, scale=GK), [tkey], [tkey])
            em.op('dve', lambda e: e.tensor_tensor(out=dst, in0=tmp, in1=src_ap, op=ALU.mult), [tkey] + src_keys,
                  [dkey])

        def mixer_tile(l, s, src, dst, ntok, ti, cb_scr, want_out=True):
            first, lastt = ti == 0, ti == ntok // TM - 1
            load_tile_halo(src, ntok, ti)
            modulate(l, 1, s, TMH, c0=HALO)
            for c in range(2):
                gc = ti * 2 + c
                em.dma('sp', f'ldcb{c}', CBL[c][:], cb_scr[gc].rearrange("p (h d) -> p h d", h=4), reads=['cbscr'],
                       writes=[f'CBL{c}'])
            inproj_fm(l, ('in', l, 0), TMH, (0, TMH), 0, evac_bias(CA, 'CA', 0, False, False))

            def ev_glu(q, p, k, b):
                em.op('act', lambda e: e.activation(out=PT[0][:, 0:TMH], in_=p[:, 0:TMH], func=AF.Sigmoid, bias=b),
                      [k, 'prm'], ['PT0'])
                em.op('dve', lambda e: e.tensor_tensor(out=CA[:, q, 0:TMH], in0=CA[:, q, 0:TMH], in1=PT[0][:, 0:TMH],
                                                       op=ALU.mult), ['CA', 'PT0'], ['CA'])
                zero_halo(CA, 'CA', q, first, lastt)
            inproj_fm(l, ('in', l, 1), TMH, (0, TMH), 4, ev_glu)
            conv_thunks = []
            for q in range(4):
                def t0_(q=q):
                    em.op('dve', lambda e: e.tensor_scalar(out=CACC[:, q, :], in0=CA[:, q, 1:1 + TM],
                                                           scalar1=pp(f'conv_w{l}', q * 31),
                                                           scalar2=pp(f'conv_b{l}', q), op0=ALU.mult, op1=ALU.add),
                          ['CA', 'prm'], ['CACC'])
                conv_thunks.append(t0_)
                for j in range(1, 31):
                    def tj_(q=q, j=j):
                        em.op('dve', lambda e: e.scalar_tensor_tensor(out=CACC[:, q, :], in0=CA[:, q, 1 + j:1 + j + TM],
                                                                      scalar=pp(f'conv_w{l}', q * 31 + j),
                                                                      in1=CACC[:, q, :], op0=ALU.mult, op1=ALU.add),
                              ['CA', 'prm', 'CACC'], ['CACC'])
                    conv_thunks.append(tj_)
            inproj_fm(l, ('in', l, 2), TMH, (0, TMH), 8, evac_bias(CST, 'CST', 0, first, lastt))
            inproj_fm(l, ('in', l, 3), TMH, (0, TMH), 12, evac_bias(CST, 'CST', 4, first, lastt))
            for q in range(4):
                qk_conv(l, q, q, [(QT[:, q, :], 'QT')])
            for q in range(4):
                qk_conv(l, 4 + q, 4 + q, [(KTf[:, q, :], 'KTf'), (KTb[:, q, :], 'KTb')])
            inproj_fm(l, ('in', l, 4), TMH, (0, TMH), 16, evac_bias(CST, 'CST', 0, first, lastt))
            em.op('pool', lambda e: e.tensor_copy(out=PWB[:], in_=pp(f'pool_w{l}').rearrange("p (g e) -> p g e", g=4)),
                  ['prm'], ['PWB'])
            for g, win in enumerate((2, 4, 8, 16)):
                lo, hi = win // 2, win - 1 - win // 2
                cur, ck = CST[:, g, :], 'CST'
                k = 1
                nb = 0
                W0 = TMH
                while k < win:
                    nxt = PT[nb % 2]
                    nk = f'PT{nb % 2}'
                    cin_, cink = cur, ck
                    em.op('pool', lambda e: e.tensor_tensor(out=nxt[:, k:W0], in0=cin_[:, k:W0], in1=cin_[:, 0:W0 - k],
                                                            op=ALU.add), [cink], [nk])
                    cur, ck = nxt, nk
                    k *= 2
                    nb += 1
                a0 = HALO + hi
                ic = PT[2]
                em.op('pool', lambda e: e.memset(ic[:, 0:TM], 1.0 / win), [], ['PT2'])
                if first:
                    for t in range(lo):
                        em.op('pool', lambda e: e.memset(ic[:, t:t + 1], 1.0 / (t + hi + 1)), [], ['PT2'])
                if lastt:
                    for t in range(TM - hi, TM):
                        em.op('pool', lambda e: e.memset(ic[:, t:t + 1], 1.0 / (TM - t + lo)), [], ['PT2'])
                em.op('dve', lambda e: e.tensor_tensor(out=T1[:, 0:TM], in0=cur[:, a0:a0 + TM], in1=ic[:, 0:TM],
                                                       op=ALU.mult), [ck, 'PT2'], ['T1'])
                em.op('dve', lambda e: e.tensor_tensor(out=MB[:, g, :], in0=T1[:, 0:TM], in1=CST[:, g, HALO:HALO + TM],
                                                       op=ALU.subtract), ['T1', 'CST'], ['MB'])
                em.op('pe', lambda e: e.matmul(PS[3][:, 0:TM], lhsT=PWB[:, g, :], rhs=MB[:, g, :], start=True, stop=True),
                      ['PWB', 'MB'], ['ps3'])
                em.op('act', lambda e: e.activation(out=YB[:, 3, g, :], in_=PS[3][:, 0:TM], func=AF.Copy,
                                                    scale=pp(f'pool_s{l}', g)), ['ps3', 'prm'], ['YB'])
            def ev_u(q, p, k, b):
                em.op('act', lambda e: e.activation(out=PT[1][:, 0:TM], in_=p[:, 0:TM], func=AF.Identity, bias=b),
                      [k, 'prm'], ['PT1'])
                gelu_T(UT[:, q, :], PT[1][:, 0:TM], ['PT1'], 'UT', TM, PT[0][:, 0:TM], 'PT0')
            inproj_fm(l, ('in', l, 5), TM, (HALO, HALO + TM), 20, ev_u)

            def ev_o(q, p, k, b):
                em.op('act', lambda e: e.activation(out=SO[:, q, :], in_=p[:, 0:TM], func=AF.Sigmoid, bias=b),
                      [k, 'prm'], ['SO'])
            inproj_fm(l, ('in', l, 6), TM, (HALO, HALO + TM), 24, ev_o)
            em.op('pool', lambda e: e.tensor_copy(out=WST[:], in_=pp(f'wsT{l}').rearrange("p (g t) -> p g t", g=4)),
                  ['prm'], ['WST'])
            Wv = wload(('in', l, 7))
            for c in range(2):
                def ev_v(p, k, c=c):
                    em.op('dve', lambda e: e.tensor_tensor(out=VT[:, c, :], in0=p[:, 0:512], in1=pp(f'b_gv{l}'),
                                                           op=ALU.add), [k, 'prm'], ['VT'])
                inproj_tm(l, Wv, 512, c, ev_v)
                gelu_T(VT[:, c, :], VT[:, c, :], ['VT'], 'VT', 512, T2[:, 0:512], 'T2')
                em.op('dve', lambda e: e.bn_stats(out=BST[:, 0, :], in_=VT[:, c, :]), ['VT'], ['BST'])
                em.op('dve', lambda e: e.bn_aggr(out=MV[:, 0, :], in_=BST[:, 0, :]), ['BST'], ['MV'])
                em.op('act', lambda e: e.activation(out=SM[:, 0:1], in_=MV[:, 0, 1:2], func=AF.Sqrt, bias=LN_EPS),
                      ['MV'], ['SM'])
                em.op('dve', lambda e: e.reciprocal(out=SM[:, 0:1], in_=SM[:, 0:1]), ['SM'], ['SM'])
                em.op('dve', lambda e: e.tensor_scalar(out=VT[:, c, :], in0=VT[:, c, :], scalar1=MV[:, 0, 0:1],
                                                       scalar2=SM[:, 0:1], op0=ALU.subtract, op1=ALU.mult),
                      ['VT', 'MV', 'SM'], ['VT'])
                em.op('pool', lambda e: e.tensor_tensor(out=VT[:, c, :], in0=VT[:, c, :], in1=pp(f'gln_g{l}'),
                                                        op=ALU.mult), ['VT', 'prm'], ['VT'])
                em.op('pool', lambda e: e.tensor_tensor(out=VN[:, c, :], in0=VT[:, c, :], in1=pp(f'gln_b{l}'),
                                                        op=ALU.add), ['VT', 'prm'], ['VN'])
            for g in range(4):
                for c in range(2):
                    em.op('pe', lambda e: e.matmul(PS[3][:, c * 128:(c + 1) * 128], lhsT=VN[:, c, g * 128:(g + 1) * 128],
                                                   rhs=WST[:, g, :], start=True, stop=True), ['VN', 'WST'], ['ps3'])
                bsg = pp(f'bs{l}', g * 128, (g + 1) * 128)
                for c in range(2):
                    em.op('dve', lambda e: e.tensor_tensor(out=T1[:, c * 128:(c + 1) * 128],
                                                           in0=PS[3][:, c * 128:(c + 1) * 128], in1=bsg, op=ALU.add),
                          ['ps3', 'prm'], ['T1'])
                em.op('dve', lambda e: e.tensor_tensor(out=YB[:, 0, g, :], in0=T1[:, 0:TM], in1=UT[:, g, :], op=ALU.mult),
                      ['T1', 'UT'], ['YB'])
            Wmv = wload(('in', l, 8))
            for c in range(2):
                def ev_mv(p, k, c=c):
                    em.op('dve', lambda e: e.tensor_tensor(out=VE[:, c, :, 0:128],
                                                           in0=p[:, 0:512].rearrange("p (h d) -> p h d", h=4),
                                                           in1=pp(f'b_mv{l}').rearrange("p (h d) -> p h d", h=4),
                                                           op=ALU.add), [k, 'prm'], ['VE'])
                inproj_tm(l, Wmv, 512, c, ev_mv)
            Wg, wgk = wload(('in', l, 'gt'), 128)
            for c in range(2):
                gate_scalars(l, c, Wg, wgk)
                cs_ = slice(c * 128, (c + 1) * 128)
                for h in range(4):
                    em.op('pe', lambda e: e.matmul(PS[3][:, h * 128:(h + 1) * 128], lhsT=KTb[:, h, cs_], rhs=QT[:, h, cs_],
                                                   start=True, stop=True), ['KTb', 'QT'], ['ps3'])
                for h in range(4):
                    em.op('dve', lambda e: e.scalar_tensor_tensor(out=SF[:, h, :], in0=PS[3][:, h * 128:(h + 1) * 128],
                                                                  scalar=EX[:, 0, h:h + 1], in1=triu, op0=ALU.mult,
                                                                  op1=ALU.mult), ['ps3', 'EX', 'prm'], ['SF'])
                    em.op('dve', lambda e: e.scalar_tensor_tensor(out=SB_[:, h, :], in0=PS[3][:, h * 128:(h + 1) * 128],
                                                                  scalar=EX[:, 1, h:h + 1], in1=tril, op0=ALU.mult,
                                                                  op1=ALU.mult), ['ps3', 'EX', 'prm'], ['SBk'])
                for hp in range(2):
                    for (Sx, sk, Cx, ckey, pb, pk_) in ((SF, 'SF', CFB, 'CFB', PS[5], 'ps5'),
                                                        (SB_, 'SBk', CBL[c], f'CBL{c}', PS[6], 'ps6')):
                        for hh in range(2):
                            h = hp * 2 + hh
                            o_ = pb[:, hh * 130:hh * 130 + 129]
                            em.op('pe', lambda e: e.matmul(o_, lhsT=Sx[:, h, :], rhs=VE[:, c, h, 0:129], start=True,
                                                           stop=False), [sk, 'VE'], [pk_], inc=False)
                            em.op('pe', lambda e: e.matmul(o_, lhsT=QT[:, h, cs_], rhs=Cx[:, h, 0:129], start=False,
                                                           stop=True), ['QT', ckey], [pk_])
                    for d_, (pb, pk_) in enumerate(((PS[5], 'ps5'), (PS[6], 'ps6'))):
                        den = pb[:, 0:260].rearrange("p (h d) -> p h d", h=2)[:, :, 128]
                        eb = EX[:, 2 + d_, hp * 2:hp * 2 + 2]
                        sm = SM[:, 4 + d_ * 2:6 + d_ * 2]
                        em.op('act', lambda e: e.activation(out=sm, in_=den, func=AF.Abs), [pk_], ['SM'])
                        em.op('dve', lambda e: e.tensor_tensor(out=sm, in0=sm, in1=eb, op=ALU.mult), ['SM', 'EX'], ['SM'])
                        em.op('dve', lambda e: e.tensor_scalar(out=sm, in0=sm, scalar1=1.0, scalar2=None, op0=ALU.max),
                              ['SM'], ['SM'])
                        em.op('dve', lambda e: e.reciprocal(out=sm, in_=sm), ['SM'], ['SM'])
                        em.op('dve', lambda e: e.tensor_tensor(out=sm, in0=sm, in1=eb, op=ALU.mult), ['SM', 'EX'], ['SM'])
                    for hh in range(2):
                        h = hp * 2 + hh
                        em.op('act', lambda e: e.activation(out=HS[:, h, :], in_=PS[5][:, hh * 130:hh * 130 + 128],
                                                            func=AF.Copy, scale=SM[:, 4 + hh:5 + hh]), ['ps5', 'SM'],
                              ['HS'])
                        em.op('dve', lambda e: e.scalar_tensor_tensor(out=HS[:, h, :],
                                                                      in0=PS[6][:, hh * 130:hh * 130 + 128],
                                                                      scalar=SM[:, 6 + hh:7 + hh], in1=HS[:, h, :],
                                                                      op0=ALU.mult, op1=ALU.add),
                              ['ps6', 'SM', 'HS'], ['HS'])
                for h in range(4):
                    em.op('dve', lambda e: e.bn_stats(out=BST[:, h, :], in_=HS[:, h, :]), ['HS'], ['BST'])
                    em.op('dve', lambda e: e.bn_aggr(out=MV[:, h, :], in_=BST[:, h, :]), ['BST'], ['MV'])
                em.op('act', lambda e: e.activation(out=SM[:, 8:12], in_=MV[:, :, 1], func=AF.Sqrt, bias=LN_EPS),
                      ['MV'], ['SM'])
                em.op('dve', lambda e: e.reciprocal(out=SM[:, 8:12], in_=SM[:, 8:12]), ['SM'], ['SM'])
                for h in range(4):
                    em.op('dve', lambda e: e.tensor_scalar(out=HN[:, h, :], in0=HS[:, h, :], scalar1=MV[:, h, 0:1],
                                                           scalar2=SM[:, 8 + h:9 + h], op0=ALU.subtract, op1=ALU.mult),
                          ['HS', 'MV', 'SM'], ['HN'])
                    em.op('pe', lambda e: e.transpose(PS[4][:, h * 128:(h + 1) * 128], HN[:, h, :], ident),
                          ['HN', 'prm'], ['ps4'])
                    em.op('dve', lambda e: e.scalar_tensor_tensor(out=YB[:, 2, h, cs_], in0=PS[4][:, h * 128:(h + 1) * 128],
                                                                  scalar=pp(f'mln_g{l}', h), in1=SO[:, h, cs_],
                                                                  op0=ALU.mult, op1=ALU.mult),
                          ['ps4', 'prm', 'SO'], ['YB'])
                for h in range(4):
                    k_transposed_scaled(c, h, 4)
                    state_update(c, h, CF32, CFB, 'CF32', 'CFB', 6)
            if not want_out:
                return
            if not want_out:
                for th in conv_thunks:
                    pass
            for bi, b in enumerate([0, 2, 3, 1]):
                if b == 1:
                    while conv_thunks:
                        conv_thunks.pop(0)()
                    layer_norm_T(TM, f'cln_g{l}', f'cln_b{l}', 0, lambda i: YB[:, 1, i, :], 'YB', nch=4, src=CACC,
                                 src_key='CACC', silu=True, banks=(3, 4))
                Wb, wbk = wload(('br', l, b))
                Wgs = [None, None]
                for jj in range(8):
                    if jj % 4 == 0:
                        Wgs = wload(('in', l, 9 + b * 2 + jj // 4))
                    Wg2, wg2k = Wgs
                    pg, pgk = PS[jj % 2], f'ps{jj % 2}'
                    ppj, ppk = PS[2], 'ps2'
                    q = jj % 4
                    for kc in range(8):
                        em.op('pe', lambda e: e.matmul(pg[:, 0:TM], lhsT=Wg2[:, kc * 512 + q * 128:kc * 512 + (q + 1) * 128],
                                                       rhs=HT[:, kc, HALO:HALO + TM], start=(kc == 0), stop=(kc == 7)),
                              [wg2k, 'HT'], [pgk], inc=(kc == 7))
                    for kc in range(4):
                        em.op('pe', lambda e: e.matmul(ppj[:, 0:TM], lhsT=Wb[:, kc * 1024 + jj * 128:kc * 1024 + (jj + 1) * 128],
                                                       rhs=YB[:, b, kc, :], start=(kc == 0), stop=(kc == 3)),
                              [wbk, 'YB'], [ppk], inc=(kc == 3))
                    tt, tk = (PT[0], 'PT0') if jj % 2 == 0 else (PT[1], 'PT1')
                    em.op('act', lambda e: e.activation(out=tt[:, 0:TM], in_=pg[:, 0:TM], func=AF.Sigmoid,
                                                        bias=pp(f'b_fm{l}', 28 + b * 8 + jj)), [pgk, 'prm'], [tk])
                    if bi == 0:
                        em.op('dve', lambda e: e.tensor_tensor(out=R[:, jj, 0:TM], in0=tt[:, 0:TM], in1=ppj[:, 0:TM],
                                                               op=ALU.mult), [tk, ppk], ['R'])
                    else:
                        em.op('dve', lambda e: e.tensor_tensor(out=tt[:, 0:TM], in0=tt[:, 0:TM], in1=ppj[:, 0:TM],
                                                               op=ALU.mult), [tk, ppk], [tk])
                        em.op('pool', lambda e: e.tensor_tensor(out=R[:, jj, 0:TM], in0=R[:, jj, 0:TM], in1=tt[:, 0:TM],
                                                                op=ALU.add), ['R', tk], ['R'])
                    for _ in range(4):
                        if conv_thunks and b != 1:
                            conv_thunks.pop(0)()
            em.op('act', lambda e: e.activation(out=MB[:], in_=R[:, :, 0:TM], func=AF.Copy), ['R'], ['MB'])
            for j2 in range(2):
                Wo, wok = wload(('wo', l, j2))
                for q in range(4):
                    i = j2 * 4 + q
                    p, k = PS[q % 2], f'ps{q % 2}'
                    for kc in range(8):
                        em.op('pe', lambda e: e.matmul(p[:, 0:TM], lhsT=Wo[:, kc * 512 + q * 128:kc * 512 + (q + 1) * 128],
                                                       rhs=MB[:, kc, :], start=(kc == 0), stop=(kc == 7)),
                              [wok, 'MB'], [k], inc=(kc == 7))
                    em.op('dve', lambda e: e.scalar_tensor_tensor(out=R[:, i, 0:TM], in0=p[:, 0:TM],
                                                                  scalar=mod(l, 1, 2, i, s), in1=XA[:, i, 0:TM],
                                                                  op0=ALU.mult, op1=ALU.add), [k, 'mods', 'XA'], ['R'])
            layer_norm_T(TM, f'ln_g{l}', f'ln_b{l}', 8, lambda i: XT[:, i, 0:TM], 'XT', banks=(3, 4))
            t0 = ti * TM
            em.dma('sp', 'stx', dst[:, t0:t0 + TM].rearrange("(c p) t -> p c t", p=128), XT[:, :, 0:TM], reads=['XT'],
                   writes=['xs2'])

        def ffn_stage(l_prev, l_next, s, src, dst, ntok, T, add_pos=False):
            for ti in range(ntok // T):
                t0 = ti * T
                em.dma('sp', 'ldx', XT[:, :, 0:T], src[:, t0:t0 + T].rearrange("(c p) t -> p c t", p=128),
                       reads=['xs', 'xs2'], writes=['XT'])
                if add_pos:
                    nr = T // 64
                    r0 = t0 // 64
                    for c in range(2):
                        for sc_ in range(2):
                            ch = sc_ * 2 + c
                            ch2 = 4 + sc_ * 2 + c
                            for r in range(nr):
                                em.op('dve', lambda e: e.tensor_scalar(out=XT[:, ch, r * 64:(r + 1) * 64],
                                                                       in0=XT[:, ch, r * 64:(r + 1) * 64],
                                                                       scalar1=ptab[:, c, sc_, r0 + r:r0 + r + 1],
                                                                       scalar2=None, op0=ALU.add), ['XT', 'ptab'], ['XT'])
                                em.op('pool', lambda e: e.tensor_tensor(out=XT[:, ch2, r * 64:(r + 1) * 64],
                                                                        in0=XT[:, ch2, r * 64:(r + 1) * 64],
                                                                        in1=ctab[:, c, sc_, :], op=ALU.add),
                                      ['XT', 'ptab'], ['XT'])
                if l_prev is not None:
                    ffn(l_prev, 1, s, T)
                if l_next is not None:
                    ffn(l_next, 0, s, T)
                em.dma('sp', 'stx', dst[:, t0:t0 + T].rearrange("(c p) t -> p c t", p=128), XT[:, :, 0:T], reads=['XT'],
                       writes=['xs', 'xs2'])

        XBUF = [(XT, 'XT'), (XTB, 'XTb')]

        class FJ:
            def __init__(self, l, f, s, T, X, pre=None, post=None):
                self.l, self.f, self.s, self.T, self.X, self.pre, self.post = l, f, s, T, X, pre, post
                self.j = 0 if f == 0 else 2

            def mod_(self):
                X, xk = self.X
                l, j, s, T = self.l, self.j, self.s, self.T
                for i in range(8):
                    em.op('dve', lambda e: e.tensor_scalar(out=HT[:, i, 0:T], in0=X[:, i, 0:T], scalar1=mod(l, j, 1, i, s),
                                                           scalar2=mod(l, j, 0, i, s), op0=ALU.mult, op1=ALU.add),
                          [xk, 'mods'], ['HT'])

            def xa_(self):
                X, xk = self.X
                T = self.T
                em.op('pool', lambda e: e.tensor_scalar(out=XA[:, :, 0:T], in0=X[:, :, 0:T], scalar1=ALPHA, scalar2=0.0,
                                                        op0=ALU.mult, op1=ALU.add), [xk], ['XA'])

            def win_(self):
                l, f, T = self.l, self.f, self.T
                for uu in range(11):
                    W, wk = wload(('fi', l, f, uu))
                    for q in range(2):
                        jj = uu * 2 + q
                        p1, p2 = PS[(jj % 2) * 2], PS[(jj % 2) * 2 + 1]
                        k1, k2 = f'ps{(jj % 2) * 2}', f'ps{(jj % 2) * 2 + 1}'
                        for kc in range(8):
                            em.op('pe', lambda e: e.matmul(p1[:, 0:T], lhsT=W[:, kc * 256 + q * 128:kc * 256 + (q + 1) * 128],
                                                           rhs=HT[:, kc, 0:T], start=(kc == 0), stop=(kc == 7)),
                                  [wk, 'HT'], [k1], inc=(kc == 7))
                        for kc in range(8):
                            em.op('pe', lambda e: e.matmul(p2[:, 0:T],
                                                           lhsT=W[:, 2048 + kc * 256 + q * 128:2048 + kc * 256 + (q + 1) * 128],
                                                           rhs=HT[:, kc, 0:T], start=(kc == 0), stop=(kc == 7)),
                                  [wk, 'HT'], [k2], inc=(kc == 7))
                        tt, tk = (T1, 'T1') if jj % 2 == 0 else (T2, 'T2')
                        em.op('act', lambda e: e.activation(out=tt[:, 0:T], in_=p1[:, 0:T], func=AF.Silu), [k1], [tk])
                        em.op('dve', lambda e: e.tensor_tensor(out=G[:, jj, 0:T], in0=tt[:, 0:T], in1=p2[:, 0:T],
                                                               op=ALU.mult), [tk, k2], ['G'])

            def wout_(self):
                l, f, s, T, j = self.l, self.f, self.s, self.T, self.j
                for i in range(8):
                    W, wk = wload(('fo', l, f, i), 22 * 128)
                    p, k = PS[4 + (i % 2)], f'ps{4 + (i % 2)}'
                    for jj in range(FC):
                        em.op('pe', lambda e: e.matmul(p[:, 0:T], lhsT=W[:, jj * 128:(jj + 1) * 128], rhs=G[:, jj, 0:T],
                                                       start=(jj == 0), stop=(jj == FC - 1)), [wk, 'G'], [k],
                              inc=(jj == FC - 1))
                    em.op('dve', lambda e: e.scalar_tensor_tensor(out=R[:, i, 0:T], in0=p[:, 0:T], scalar=mod(l, j, 2, i, s),
                                                                  in1=XA[:, i, 0:T], op0=ALU.mult, op1=ALU.add),
                          [k, 'mods', 'XA'], ['R'])

            def ln_(self):
                X, xk = self.X
                T = self.T
                layer_norm_T(T, f'ln_g{self.l}', f'ln_b{self.l}', self.j * 8, lambda i: X[:, i, 0:T], xk,
                             tmp=(PT[2], 'PT2'))

        def ffn_pipe(jobs):
            n = len(jobs)
            if jobs[0].pre:
                jobs[0].pre()
            jobs[0].mod_()
            jobs[0].xa_()
            for k in range(n):
                jobs[k].win_()
                same = (k + 1 < n) and (jobs[k + 1].X[1] == jobs[k].X[1])
                if k + 1 < n and not same:
                    if jobs[k + 1].pre:
                        jobs[k + 1].pre()
                    jobs[k + 1].mod_()
                jobs[k].wout_()
                if k + 1 < n and not same:
                    jobs[k + 1].xa_()
                jobs[k].ln_()
                if jobs[k].post:
                    jobs[k].post()
                if same:
                    if jobs[k + 1].pre:
                        jobs[k + 1].pre()
                    jobs[k + 1].mod_()
                    jobs[k + 1].xa_()

        def stage_jobs(ls, tiles):
            jobs = []

            def mk(tile, X):
                s_, src, dst, t0, T, add_pos = tile
                Xb, xk = X

                def pre():
                    em.dma('sp', 'ldx', Xb[:, :, 0:T], src[:, t0:t0 + T].rearrange("(c p) t -> p c t", p=128),
                           reads=['xs2'], writes=[xk])
                    if add_pos:
                        nr = T // 64
                        r0 = t0 // 64
                        for c in range(2):
                            for sc_ in range(2):
                                ch = sc_ * 2 + c
                                ch2 = 4 + sc_ * 2 + c
                                for r in range(nr):
                                    em.op('dve', lambda e: e.tensor_scalar(out=Xb[:, ch, r * 64:(r + 1) * 64],
                                                                           in0=Xb[:, ch, r * 64:(r + 1) * 64],
                                                                           scalar1=ptab[:, c, sc_, r0 + r:r0 + r + 1],
                                                                           scalar2=None, op0=ALU.add), [xk, 'ptab'], [xk])
                                    em.op('pool', lambda e: e.tensor_tensor(out=Xb[:, ch2, r * 64:(r + 1) * 64],
                                                                            in0=Xb[:, ch2, r * 64:(r + 1) * 64],
                                                                            in1=ctab[:, c, sc_, :], op=ALU.add),
                                          [xk, 'ptab'], [xk])

                def post():
                    em.dma('sp', 'stx', dst[:, t0:t0 + T].rearrange("(c p) t -> p c t", p=128), Xb[:, :, 0:T],
                           reads=[xk], writes=['xs'])
                js = [FJ(l_, f_, s_, T, X) for (l_, f_) in ls]
                js[0].pre = pre
                js[-1].post = post
                return js
            i = 0
            while i < len(tiles):
                if i + 1 < len(tiles):
                    ja, jb = mk(tiles[i], XBUF[0]), mk(tiles[i + 1], XBUF[1])
                    for a_, b_ in zip(ja, jb):
                        jobs += [a_, b_]
                    i += 2
                else:
                    jobs += mk(tiles[i], XBUF[0])
                    i += 1
            return jobs

        xa_, xb_ = xs
        ca_, cb_ = cs
        for l in range(nl):
            if l > 0:
                em.new_epoch(['pe', 'act', 'dve'])
                em.depoch += 1
            last = (l == last_layer)
            em.dma('sp', 'prm', prl[:], prl_d[l], writes=['prm'])
            ls_ = [(l, 0)] if l == 0 else [(l - 1, 1), (l, 0)]
            tiles_ = [(1, cin if l == 0 else cb_, ca_, 0, NCTX, False)]
            tiles_ += [(0, xin if l == 0 else xb_, xa_, ti * TF, TF, l == 0) for ti in range(N // TF)]
            ffn_pipe(stage_jobs(ls_, tiles_))
            if stop == ('A', l):
                break
            em.op('dve', lambda e: e.memset(CB32[:], 0.0), [], ['CB32'])
            em.op('dve', lambda e: e.memset(CBB[:], 0.0), [], ['CBB'])
            for ti in reversed(range(NCTX // TM)):
                state_pass(l, 1, ca_, NCTX, ti, True, cb_c)
            for ti in reversed(range(NMT)):
                state_pass(l, 0, xa_, N, ti, True, cb_x)
            em.op('dve', lambda e: e.memset(CF32[:], 0.0), [], ['CF32'])
            em.op('dve', lambda e: e.memset(CFB[:], 0.0), [], ['CFB'])
            for ti in range(NCTX // TM):
                if last:
                    state_pass(l, 1, ca_, NCTX, ti, False, None)
                else:
                    mixer_tile(l, 1, ca_, cb_, NCTX, ti, cb_c)
            for ti in range(NMT):
                if ti == NMT // 2 and NMT >= 16:
                    em.new_epoch(['dve'])
                mixer_tile(l, 0, xa_, xb_, N, ti, cb_x)
        fin_src = xb_
        if stop is None:
            ffn_pipe(stage_jobs([(nl - 1, 1)], [(0, xb_, yout, ti * TF, TF, False) for ti in range(N // TF)]))
        else:
            fin_src = xa_ if stop[0] == 'A' else xb_
            for ti in range(N // TF):
                t0 = ti * TF
                em.dma('sp', 'ldx', XT[:, :, 0:TF], fin_src[:, t0:t0 + TF].rearrange("(c p) t -> p c t", p=128),
                       reads=['xs', 'xs2'], writes=['XT'])
                em.dma('sp', 'stx', yout[:, t0:t0 + TF].rearrange("(c p) t -> p c t", p=128), XT[:, :, 0:TF],
                       reads=['XT'], writes=['xs', 'xs2'])
        em.deps('sp', ['xs', 'xs2'], [])
        build.last_counts = dict(em.cnt)
        build.sbuf_left = nc.sbuf_bytes_remaining() if callable(nc.sbuf_bytes_remaining) else nc.sbuf_bytes_remaining
    return nc


def kernel(x, c, ctx, c_ctx, w_ada, b_ada, ln_g, ln_b, ffn_w_in, ffn_w_out, w_in, b_in,
           gmlp_ln_g, gmlp_ln_b, gmlp_ws, gmlp_bs, conv_w, conv_b, conv_ln_g, conv_ln_b,
           qk_conv_w, mlstm_ln_g, pool_w, pool_scale, w_branch, w_out, _nl=None, _stop=None):
    x = np.asarray(x, np.float32)
    B, N, _ = x.shape
    nl = _nl or DEPTH
    f = lambda a: np.ascontiguousarray(np.asarray(a, np.float32)[:nl])
    packs = [pack_params(nl, np.asarray(c)[b], np.asarray(c_ctx), *[np.asarray(a, np.float32) for a in (
        b_ada, ln_g, ln_b, b_in, gmlp_ln_g, gmlp_ln_b, gmlp_ws, gmlp_bs, conv_w, conv_b, conv_ln_g, conv_ln_b,
        qk_conv_w, mlstm_ln_g, pool_w, pool_scale)]) for b in range(B)]
    prm_off = packs[0][0].off
    prl_off = packs[0][1][0].off
    prms = [p[0].get() for p in packs]
    prls = [np.stack([q.get() for q in p[1]], axis=0) for p in packs]
    nc = build(N, nl, prm_off, prms[0].shape[1], prl_off, prls[0].shape[2], stop=_stop)
    shared = {"w_ada": f(w_ada), "ffn_w_in": f(ffn_w_in), "ffn_w_out": f(ffn_w_out), "w_in": f(w_in),
              "w_branch": f(w_branch), "w_out": f(w_out)}
    xT = [np.ascontiguousarray(x[b].T) for b in range(B)]
    cT = [np.ascontiguousarray(np.asarray(ctx, np.float32)[b].T) for b in range(B)]
    ncores = 2
    in_maps = []
    for i in range(ncores):
        b = i % B
        m = {"xT": xT[b], "ctxT": cT[b], "prm": prms[b], "prl": prls[b]}
        m.update(shared)
        in_maps.append(m)
    res = run_bass_kernel_spmd(nc, in_maps, core_ids=list(range(ncores)))
    out = np.stack([np.ascontiguousarray(res.results[b]["yT"].T) for b in range(B)], axis=0)
    return out.astype(np.float32)
```

```python
import math
from contextlib import ExitStack

import numpy as np
import concourse.bass as bass
import concourse.mybir as mybir
from concourse.bass_utils import run_bass_kernel_spmd

F32 = mybir.dt.float32
BF16 = mybir.dt.bfloat16
AF = mybir.ActivationFunctionType
ALU = mybir.AluOpType

D = 1024
DC = 8
DFF = 2816
FC = 22
MIX = 512
NCTX = 256
DEPTH = 4
ALPHA = (2 * DEPTH) ** 0.25
LN_EPS = 1e-6
DH = 128
LNS = math.log(DH ** -0.5)
OFF_A, OFF_B, OFF_C = 0, 1024, 2048
OFF_GATES = 3584
OFF_O = 3600
OFF_D = 4112
OFF_G = 4624
IN_COLS = 8720
HALO = 16
TM = 256
TMH = TM + 2 * HALO
TF = 256
GK = 2.0 * math.sqrt(2.0 / math.pi)


class Em:
    def __init__(self, nc, es):
        self.nc = nc
        self.es = es
        self.engs = {'pe': nc.tensor, 'act': nc.scalar, 'dve': nc.vector, 'pool': nc.gpsimd, 'sp': nc.sync}
        self.sems = {}
        self.cnt = {}
        self.cur = {}
        self.epoch = 0
        self.seen = {k: {} for k in self.engs}
        self.lastw = {}
        self.readers = {}
        self.depoch = 0
        self.new_epoch()

    def _mksem(self, name):
        return self.es.enter_context(self.nc.semaphore(name))

    def new_epoch(self, engs=None):
        self.epoch += 1
        for k in (engs or self.engs):
            if k == 'sp' and 'sp' in self.cur:
                continue
            key = f"{k}{self.epoch}"
            self.sems[key] = self._mksem("s_" + key)
            self.cnt[key] = 0
            self.cur[k] = key

    def dsem(self, slot):
        if slot not in self.sems:
            self.sems[slot] = self._mksem("d_" + slot)
            self.cnt[slot] = 0
        return self.sems[slot]

    def _wait(self, e, semkey, val):
        if self.seen[e].get(semkey, 0) >= val:
            return
        self.engs[e].wait_ge(self.sems[semkey], val)
        self.seen[e][semkey] = val

    def deps(self, e, reads, writes, skip_self=False):
        mine = self.cur[e]
        for k in reads:
            w = self.lastw.get(k)
            if w and not (skip_self and w[0] == mine):
                self._wait(e, *w)
        for k in writes:
            w = self.lastw.get(k)
            if w and not (skip_self and w[0] == mine):
                self._wait(e, *w)
            for sk, v in self.readers.get(k, {}).items():
                if not (skip_self and sk == mine):
                    self._wait(e, sk, v)

    def done(self, semkey, val, reads, writes):
        for k in reads:
            d = self.readers.setdefault(k, {})
            d[semkey] = max(d.get(semkey, 0), val)
        for k in writes:
            self.lastw[k] = (semkey, val)
            self.readers[k] = {}

    @staticmethod
    def _norm(reads, writes):
        r2 = [k for k in reads if not k.startswith('ps')]
        w2 = [k[:3] if k.startswith('ps') else k for k in writes]
        w2 += [k[:3] for k in reads if k.startswith('ps')]
        return r2, list(dict.fromkeys(w2))

    def op(self, e, fn, reads=(), writes=(), inc=True):
        reads, writes = self._norm(list(reads), list(writes))
        self.deps(e, reads, writes, e == 'pe')
        ins = fn(self.engs[e])
        key = self.cur[e]
        if inc:
            self.cnt[key] += 1
            ins.then_inc(self.sems[key], 1)
            self.done(key, self.cnt[key], reads, writes)
        else:
            self.done(key, self.cnt[key] + 1, reads, writes)
        return ins

    def barrier(self):
        for e in self.engs:
            for key, v in self.cnt.items():
                if v > 0:
                    self._wait(e, key, v)

    def dma(self, e, slot, out, in_, reads=(), writes=()):
        self.deps(e, reads, writes)
        sem = self.dsem(slot)
        ins = self.engs[e].dma_start(out=out, in_=in_)
        self.cnt[slot] += 16
        ins.then_inc(sem, 16)
        self.done(slot, self.cnt[slot], reads, writes)
        return ins


class Packer:
    def __init__(self):
        self.off = {}
        self.parts = []
        self.n = 0

    def add(self, name, arr):
        arr = np.ascontiguousarray(arr, dtype=np.float32).reshape(128, -1)
        self.off[name] = (self.n, arr.shape[1])
        self.parts.append(arr)
        self.n += arr.shape[1]

    def get(self):
        return np.concatenate(self.parts, axis=1)


def fm(v):
    v = np.asarray(v, np.float32)
    n = v.shape[-1] // 128
    return np.moveaxis(v.reshape(v.shape[:-1] + (n, 128)), -1, 0)


def bc(v):
    v = np.asarray(v, np.float32)
    return np.broadcast_to(v[None], (128,) + v.shape)


def pack_params(nl, c_b, c_ctx, b_ada, ln_g, ln_b, b_in, gmlp_ln_g, gmlp_ln_b, gmlp_ws, gmlp_bs, conv_w, conv_b,
                conv_ln_g, conv_ln_b, qk_conv_w, mlstm_ln_g, pool_w, pool_scale):
    pk = Packer()
    pk.add('cvec', np.stack([fm(c_b), fm(c_ctx)], axis=-1))
    ii = np.arange(128, dtype=np.float32)
    pk.add('ident', np.eye(128, dtype=np.float32))
    pk.add('triu', (ii[:, None] <= ii[None, :]).astype(np.float32))
    pk.add('tril', (ii[:, None] >= ii[None, :]).astype(np.float32))
    pk.add('ones', np.ones((128, 128), np.float32))
    pk.add('pidx', ii[:, None])
    pk.add('ridx', bc(np.arange(128, dtype=np.float32)))
    pk.add('cidx', bc(np.arange(64, dtype=np.float32)))
    for l in range(nl):
        pk.add(f'b_ada{l}', fm(b_ada[l]))
        pk.add(f'ln_g{l}', fm(ln_g[l]))
        pk.add(f'ln_b{l}', fm(ln_b[l]))
    pls = []
    for l in range(nl):
        pl = Packer()
        bi = b_in[l]
        fmcols = np.concatenate([bi[1024:2048], bi[2048:3072], bi[OFF_D:OFF_D + 512], bi[0:512],
                                 bi[OFF_O:OFF_O + 512], bi[OFF_G:OFF_G + 4096]])
        pl.add('b_fm', fm(fmcols))
        pl.add('b_gv', bc(bi[512:1024]))
        pl.add('b_mv', bc(bi[3072:3584]))
        pl.add('b_gt', bc(bi[OFF_GATES:OFF_GATES + 16]))
        pl.add('gln_g', bc(gmlp_ln_g[l]))
        pl.add('gln_b', bc(gmlp_ln_b[l]))
        pl.add('wsT', np.transpose(gmlp_ws[l], (2, 0, 1)))
        pl.add('bs', bc(gmlp_bs[l]))
        pl.add('conv_w', np.transpose(fm(conv_w[l]), (0, 2, 1)))
        pl.add('conv_b', fm(conv_b[l]))
        pl.add('cln_g', fm(conv_ln_g[l]))
        pl.add('cln_b', fm(conv_ln_b[l]))
        pl.add('qkw', np.transpose(fm(qk_conv_w[l]), (0, 2, 1)))
        pl.add('mln_g', fm(mlstm_ln_g[l]))
        pl.add('pool_w', np.transpose(pool_w[l], (1, 0, 2)))
        pl.add('pool_s', fm(pool_scale[l]))
        pls.append(pl)
    return pk, pls


def build(N, nl, prm_off, nprm, prl_off, nprl, stop=None):
    nc = bass.Bass("TRN2", target_bir_lowering=False)
    last_layer = nl - 1
    NFT = N // TF
    NMT = N // TM
    xin = nc.dram_tensor("xT", [D, N], F32, kind="ExternalInput").ap()
    cin = nc.dram_tensor("ctxT", [D, NCTX], F32, kind="ExternalInput").ap()
    prm_d = nc.dram_tensor("prm", [128, nprm], F32, kind="ExternalInput").ap()
    prl_d = nc.dram_tensor("prl", [nl, 128, nprl], F32, kind="ExternalInput").ap()
    w_ada = nc.dram_tensor("w_ada", [nl, D, 9 * D], F32, kind="ExternalInput").ap()
    ffn_w_in = nc.dram_tensor("ffn_w_in", [nl, 2, D, 2 * DFF], F32, kind="ExternalInput").ap()
    ffn_w_out = nc.dram_tensor("ffn_w_out", [nl, 2, DFF, D], F32, kind="ExternalInput").ap()
    w_in = nc.dram_tensor("w_in", [nl, D, IN_COLS], F32, kind="ExternalInput").ap()
    w_branch = nc.dram_tensor("w_branch", [nl, 4, MIX, D], F32, kind="ExternalInput").ap()
    w_out = nc.dram_tensor("w_out", [nl, D, D], F32, kind="ExternalInput").ap()
    yout = nc.dram_tensor("yT", [D, N], F32, kind="ExternalOutput").ap()
    xs = [nc.dram_tensor(f"xs{i}", [D, N], F32).ap() for i in range(2)]
    cs = [nc.dram_tensor(f"cs{i}", [D, NCTX], F32).ap() for i in range(2)]
    NCH = N // 128
    cb_x = nc.dram_tensor("cb_x", [NCH, 128, 4 * 130], BF16).ap()
    cb_c = nc.dram_tensor("cb_c", [NCTX // 128, 128, 4 * 130], BF16).ap()
    UPL = 2 * 11 + 2 * 8 + 18 + 4 + 2
    wq = nc.dram_tensor("wq", [nl * UPL, 128, 4096], BF16).ap()

    es = ExitStack()
    with es:
        em = Em(nc, es)
        _n = [0]

        def sb(shape, dt, name=None):
            _n[0] += 1
            return es.enter_context(nc.sbuf_tensor(name or f"t{_n[0]}", shape, dt))

        PS = [es.enter_context(nc.psum_tensor(f"ps{i}", [128, 512], F32)) for i in range(8)]
        prm = sb([128, nprm], F32, "prm_sb")
        prl = sb([128, nprl], F32, "prl_sb")

        def pp(name, a=None, b=None):
            import re as _re
            m = _re.match(r"^(.*?)(\d+)$", name)
            if name in prm_off:
                o, n = prm_off[name]
                buf = prm
            else:
                base = m.group(1) if (m and m.group(1) in prl_off) else name
                o, n = prl_off[base]
                buf = prl
            if a is None:
                return buf[:, o:o + n]
            return buf[:, o + a:o + (b if b is not None else a + 1)]

        em.dma('sp', 'prm', prm[:], prm_d, writes=['prm'])

        ident = pp('ident')
        triu = pp('triu')
        tril = pp('tril')
        ones = pp('ones')

        sc = sb([128, 8, 2], F32, "sc")
        mods = sb([128, nl, 72, 2], F32, "mods")
        es_pro = ExitStack()
        stg = [es_pro.enter_context(nc.sbuf_tensor(f"stg{i}", [128, 4096], F32)) for i in range(2)]
        stb = [es_pro.enter_context(nc.sbuf_tensor(f"stb{i}", [128, 4096], BF16)) for i in range(2)]
        ucount = [0]
        cast_eng = ['act', 'dve']

        def prologue_unit(uidx, srcs):
            i = ucount[0] % 2
            ucount[0] += 1
            tot = 0
            for (o, a, b, ap) in srcs:
                dst = stg[i][:, o:o + a * b].rearrange("p (a b) -> p a b", a=a, b=b)
                em.dma('sp', f'pl{i}', dst, ap, writes=[f'stg{i}'])
                tot = max(tot, o + a * b)
            ce = cast_eng[uidx % 2]
            if ce == 'act':
                em.op('act', lambda e: e.activation(out=stb[i][:, 0:tot], in_=stg[i][:, 0:tot], func=AF.Copy),
                      [f'stg{i}'], [f'stb{i}'])
            else:
                em.op(ce, lambda e: e.tensor_copy(out=stb[i][:, 0:tot], in_=stg[i][:, 0:tot]),
                      [f'stg{i}'], [f'stb{i}'])
            em.dma('pool', f'plst{i}', wq[uidx, :, 0:tot], stb[i][:, 0:tot], reads=[f'stb{i}'], writes=[f'wq{i}'])

        def kc_ap(w2d, c0, ncols):
            return w2d[:, c0:c0 + ncols].rearrange("(kc p) c -> p kc c", p=128)

        UNIT = {}
        u = 0
        for l in range(nl):
            for f in range(2):
                for j in range(11):
                    w2 = ffn_w_in[l, f]
                    prologue_unit(u, [(0, 8, 256, kc_ap(w2, j * 256, 256)),
                                      (2048, 8, 256, kc_ap(w2, DFF + j * 256, 256))])
                    UNIT[('fi', l, f, j)] = u
                    u += 1
                for i in range(8):
                    prologue_unit(u, [(0, 22, 128, kc_ap(ffn_w_out[l, f], i * 128, 128))])
                    UNIT[('fo', l, f, i)] = u
                    u += 1
            incols = [1024, 1536, 2048, 2560, OFF_D, 0, OFF_O, 512, 3072] + [OFF_G + 512 * i for i in range(8)]
            for j, c0 in enumerate(incols):
                prologue_unit(u, [(0, 8, 512, kc_ap(w_in[l], c0, 512))])
                UNIT[('in', l, j)] = u
                u += 1
            prologue_unit(u, [(0, 8, 16, kc_ap(w_in[l], OFF_GATES, 16))])
            UNIT[('in', l, 'gt')] = u
            u += 1
            for b in range(4):
                prologue_unit(u, [(0, 4, 1024, kc_ap(w_branch[l, b], 0, 1024))])
                UNIT[('br', l, b)] = u
                u += 1
            for j in range(2):
                prologue_unit(u, [(0, 8, 512, kc_ap(w_out[l], j * 512, 512))])
                UNIT[('wo', l, j)] = u
                u += 1
        assert u == nl * UPL, (u, nl * UPL)

        cv = pp('cvec').rearrange("p (k t) -> p k t", k=8, t=2)
        em.op('act', lambda e: e.activation(out=sc[:], in_=cv, func=AF.Silu), ['prm'], ['sc'])
        for l in range(nl):
            for og in range(18):
                i = ucount[0] % 2
                ucount[0] += 1
                em.dma('sp', f'pl{i}', stg[i][:].rearrange("p (a b) -> p a b", a=8, b=512),
                       kc_ap(w_ada[l], og * 512, 512), writes=[f'stg{i}'])
                for q in range(4):
                    oc = og * 4 + q
                    for kc in range(8):
                        em.op('pe', lambda e: e.matmul(PS[0][:, oc * 2:oc * 2 + 2],
                                                       lhsT=stg[i][:, kc * 512 + q * 128: kc * 512 + (q + 1) * 128],
                                                       rhs=sc[:, kc, :], start=(kc == 0), stop=(kc == 7)),
                              [f'stg{i}', 'sc'], ['ps0'], inc=(kc == 7))
            ba = pp(f'b_ada{l}')
            for t in range(2):
                em.op('dve', lambda e: e.tensor_tensor(out=mods[:, l, :, t],
                                                       in0=PS[0][:, 0:144].rearrange("p (c t) -> p c t", t=2)[:, :, t],
                                                       in1=ba, op=ALU.add), ['ps0', 'prm'], ['mods'])
        for l in range(nl):
            for j in range(3):
                r = 1.0 if j == 1 else 0.5
                em.op('dve', lambda e: e.tensor_scalar(out=mods[:, l, (j * 3 + 1) * 8:(j * 3 + 2) * 8, :],
                                                       in0=mods[:, l, (j * 3 + 1) * 8:(j * 3 + 2) * 8, :],
                                                       scalar1=1.0, scalar2=None, op0=ALU.add), ['mods'], ['mods'])
                em.op('dve', lambda e: e.tensor_scalar(out=mods[:, l, (j * 3 + 2) * 8:(j * 3 + 3) * 8, :],
                                                       in0=mods[:, l, (j * 3 + 2) * 8:(j * 3 + 3) * 8, :],
                                                       scalar1=r, scalar2=None, op0=ALU.mult), ['mods'], ['mods'])

        em.barrier()
        es_pro.close()

        def mod(l, j, t, i, s):
            c = (j * 3 + t) * 8 + i
            return mods[:, l, c, s:s + 1]

        RING = 4
        ring = [sb([128, 4096], BF16, f"ring{i}") for i in range(RING)]
        rcount = [0]

        def wload(key, ncols=4096):
            i = rcount[0] % RING
            rcount[0] += 1
            em.dma('sp', f'wr{i}_{em.depoch}', ring[i][:, 0:ncols], wq[UNIT[key], :, 0:ncols], reads=['wq0', 'wq1'], writes=[f'ring{i}'])
            return ring[i], f'ring{i}'

        ptab = sb([128, 2, 2, 128], F32, "ptab")
        ctab = sb([128, 2, 2, 64], F32, "ctab")
        frq = sb([128, 2], F32, "frq")
        ang = sb([128, 128], F32, "ang")
        angi = sb([128, 128], mybir.dt.int32, "angi")
        angf = sb([128, 128], F32, "angf")
        for c in range(2):
            em.op('dve', lambda e: e.tensor_scalar(out=frq[:, c:c + 1], in0=pp('pidx'), scalar1=float(c * 128),
                                                   scalar2=None, op0=ALU.add), ['prm'], ['frq'])
        em.op('act', lambda e: e.activation(out=frq[:], in_=frq[:], func=AF.Exp, scale=-math.log(10000.0) / 256.0),
              ['frq'], ['frq'])

        def sincos(dst, idx_ap, n, c, phase):
            em.op('dve', lambda e: e.tensor_scalar(out=ang[:, 0:n], in0=idx_ap, scalar1=frq[:, c:c + 1],
                                                   scalar2=1.0 / (2 * math.pi), op0=ALU.mult, op1=ALU.mult),
                  ['prm', 'frq'], ['ang'])
            if phase:
                em.op('dve', lambda e: e.tensor_scalar(out=ang[:, 0:n], in0=ang[:, 0:n], scalar1=phase,
                                                       scalar2=None, op0=ALU.add), ['ang'], ['ang'])
            em.op('dve', lambda e: e.tensor_copy(out=angi[:, 0:n], in_=ang[:, 0:n]), ['ang'], ['angi'])
            em.op('dve', lambda e: e.tensor_copy(out=angf[:, 0:n], in_=angi[:, 0:n]), ['angi'], ['angf'])
            em.op('dve', lambda e: e.tensor_tensor(out=ang[:, 0:n], in0=ang[:, 0:n], in1=angf[:, 0:n],
                                                   op=ALU.subtract), ['ang', 'angf'], ['ang'])
            em.op('dve', lambda e: e.tensor_scalar(out=angf[:, 0:n], in0=ang[:, 0:n], scalar1=0.5, scalar2=None,
                                                   op0=ALU.is_gt), ['ang'], ['angf'])
            em.op('dve', lambda e: e.tensor_tensor(out=ang[:, 0:n], in0=ang[:, 0:n], in1=angf[:, 0:n],
                                                   op=ALU.subtract), ['ang', 'angf'], ['ang'])
            em.op('dve', lambda e: e.tensor_scalar(out=angf[:, 0:n], in0=ang[:, 0:n], scalar1=-0.5, scalar2=None,
                                                   op0=ALU.is_lt), ['ang'], ['angf'])
            em.op('dve', lambda e: e.tensor_tensor(out=ang[:, 0:n], in0=ang[:, 0:n], in1=angf[:, 0:n],
                                                   op=ALU.add), ['ang', 'angf'], ['ang'])
            em.op('act', lambda e: e.activation(out=dst, in_=ang[:, 0:n], func=AF.Sin, scale=2 * math.pi),
                  ['ang'], ['ptab'])

        for c in range(2):
            sincos(ptab[:, c, 0, :], pp('ridx'), 128, c, 0.0)
            sincos(ptab[:, c, 1, :], pp('ridx'), 128, c, 0.25)
            sincos(ctab[:, c, 0, :], pp('cidx'), 64, c, 0.0)
            sincos(ctab[:, c, 1, :], pp('cidx'), 64, c, 0.25)

        XT = sb([128, 8, TMH], F32, "XT")
        XTB = sb([128, 8, TMH], F32, "XTb")
        XBUF = [(XT, 'XT'), (XTB, 'XTb')]
        XA = sb([128, 8, TMH], F32, "XA")
        HT = sb([128, 8, TMH], BF16, "HT")
        G = sb([128, FC, TF], BF16, "G")
        R = sb([128, 8, TF], F32, "R")
        SQ = sb([128, 8, TF], F32, "SQ")
        T1 = sb([128, 512], F32, "T1")
        T2 = sb([128, 512], F32, "T2")
        MEAN = sb([128, 512], F32, "MEAN")
        RSTD = sb([128, 512], F32, "RSTD")

        def layer_norm_T(T, gname, bname, gi, dst, key_dst, nch=8, src=R, src_key='R', silu=False, banks=(6, 7), tmp=None):
            nf = float(nch * 128)
            TT_, ttk = tmp if tmp is not None else (T1, 'T1')
            BK7 = ['ps7', 'ps7r0', 'ps7r1', 'ps7r2', 'ps7c']
            pa, pb_ = PS[banks[0]], PS[banks[1]]
            ka = BK7 if banks[0] == 7 else [f'ps{banks[0]}']
            kb = BK7 if banks[1] == 7 else [f'ps{banks[1]}']
            em.op('act', lambda e: e.activation(out=SQ[:, 0:nch, 0:T], in_=src[:, 0:nch, 0:T], func=AF.Square),
                  [src_key], ['SQ'])
            for i in range(nch):
                em.op('pe', lambda e: e.matmul(pa[:, 0:T], lhsT=ones, rhs=src[:, i, 0:T], start=(i == 0),
                                               stop=(i == nch - 1)), [src_key, 'prm'], ka, inc=(i == nch - 1))
            for i in range(nch):
                em.op('pe', lambda e: e.matmul(pb_[:, 0:T], lhsT=ones, rhs=SQ[:, i, 0:T], start=(i == 0),
                                               stop=(i == nch - 1)), ['SQ', 'prm'], kb, inc=(i == nch - 1))
            em.op('act', lambda e: e.activation(out=MEAN[:, 0:T], in_=pa[:, 0:T], func=AF.Copy, scale=1.0 / nf),
                  ka, ['MEAN'])
            em.op('dve', lambda e: e.tensor_tensor(out=TT_[:, 0:T], in0=MEAN[:, 0:T], in1=MEAN[:, 0:T], op=ALU.mult),
                  ['MEAN'], [ttk])
            em.op('dve', lambda e: e.scalar_tensor_tensor(out=TT_[:, 0:T], in0=pb_[:, 0:T], scalar=1.0 / nf,
                                                          in1=TT_[:, 0:T], op0=ALU.mult, op1=ALU.subtract),
                  kb + [ttk], [ttk])
            em.op('act', lambda e: e.activation(out=TT_[:, 0:T], in_=TT_[:, 0:T], func=AF.Sqrt, bias=LN_EPS),
                  [ttk], [ttk])
            em.op('dve', lambda e: e.reciprocal(out=RSTD[:, 0:T], in_=TT_[:, 0:T]), [ttk], ['RSTD'])
            for i in range(nch):
                em.op('dve', lambda e: e.tensor_tensor(out=SQ[:, i, 0:T], in0=src[:, i, 0:T], in1=MEAN[:, 0:T],
                                                       op=ALU.subtract), [src_key, 'MEAN'], ['SQ'])
                em.op('dve', lambda e: e.tensor_tensor(out=SQ[:, i, 0:T], in0=SQ[:, i, 0:T], in1=RSTD[:, 0:T],
                                                       op=ALU.mult), ['SQ', 'RSTD'], ['SQ'])
                em.op('act', lambda e: e.activation(out=dst(i), in_=SQ[:, i, 0:T],
                                                    func=AF.Silu if silu else AF.Identity,
                                                    scale=pp(gname, gi + i), bias=pp(bname, gi + i)),
                      ['SQ', 'prm'], [key_dst])

        def modulate(l, j, s, T, c0=0):
            for i in range(8):
                em.op('dve', lambda e: e.tensor_scalar(out=HT[:, i, 0:T], in0=XT[:, i, 0:T], scalar1=mod(l, j, 1, i, s),
                                                       scalar2=mod(l, j, 0, i, s), op0=ALU.mult, op1=ALU.add),
                      ['XT', 'mods'], ['HT'])
            em.op('pool', lambda e: e.tensor_scalar(out=XA[:, :, 0:T - 2 * c0], in0=XT[:, :, c0:T - c0], scalar1=ALPHA,
                                                    scalar2=0.0, op0=ALU.mult, op1=ALU.add), ['XT'], ['XA'])

        def ffn(l, f, s, T):
            j = 0 if f == 0 else 2
            modulate(l, j, s, T)
            for uu in range(11):
                W, wk = wload(('fi', l, f, uu))
                for q in range(2):
                    jj = uu * 2 + q
                    p1, p2 = PS[(jj % 2) * 2], PS[(jj % 2) * 2 + 1]
                    k1, k2 = f'ps{(jj % 2) * 2}', f'ps{(jj % 2) * 2 + 1}'
                    for kc in range(8):
                        em.op('pe', lambda e: e.matmul(p1[:, 0:T], lhsT=W[:, kc * 256 + q * 128:kc * 256 + (q + 1) * 128],
                                                       rhs=HT[:, kc, 0:T], start=(kc == 0), stop=(kc == 7)),
                              [wk, 'HT'], [k1], inc=(kc == 7))
                    for kc in range(8):
                        em.op('pe', lambda e: e.matmul(p2[:, 0:T],
                                                       lhsT=W[:, 2048 + kc * 256 + q * 128:2048 + kc * 256 + (q + 1) * 128],
                                                       rhs=HT[:, kc, 0:T], start=(kc == 0), stop=(kc == 7)),
                              [wk, 'HT'], [k2], inc=(kc == 7))
                    tt, tk = (T1, 'T1') if jj % 2 == 0 else (T2, 'T2')
                    em.op('act', lambda e: e.activation(out=tt[:, 0:T], in_=p1[:, 0:T], func=AF.Silu), [k1], [tk])
                    em.op('dve', lambda e: e.tensor_tensor(out=G[:, jj, 0:T], in0=tt[:, 0:T], in1=p2[:, 0:T],
                                                           op=ALU.mult), [tk, k2], ['G'])
            for i in range(8):
                W, wk = wload(('fo', l, f, i), 22 * 128)
                p, k = PS[4 + (i % 2)], f'ps{4 + (i % 2)}'
                for jj in range(FC):
                    em.op('pe', lambda e: e.matmul(p[:, 0:T], lhsT=W[:, jj * 128:(jj + 1) * 128], rhs=G[:, jj, 0:T],
                                                   start=(jj == 0), stop=(jj == FC - 1)), [wk, 'G'], [k],
                          inc=(jj == FC - 1))
                em.op('dve', lambda e: e.scalar_tensor_tensor(out=R[:, i, 0:T], in0=p[:, 0:T], scalar=mod(l, j, 2, i, s),
                                                              in1=XA[:, i, 0:T], op0=ALU.mult, op1=ALU.add),
                      [k, 'mods', 'XA'], ['R'])
            layer_norm_T(T, f'ln_g{l}', f'ln_b{l}', j * 8, lambda i: XT[:, i, 0:T], 'XT')

        CST = sb([128, 8, TMH], F32, "CST")
        CACC = sb([128, 4, TM], F32, "CACC")
        CA = sb([128, 4, TMH], F32, "CA")
        QT = sb([128, 4, TM], BF16, "QT")
        KTb = sb([128, 4, TM], BF16, "KTb")
        KTf = sb([128, 4, TM], F32, "KTf")
        UT = sb([128, 4, TM], F32, "UT")
        SO = sb([128, 4, TM], F32, "SO")
        VN = sb([128, 2, 512], BF16, "VN")
        VT = sb([128, 2, 512], F32, "VT")
        VE = sb([128, 2, 4, 130], BF16, "VE")
        YB = sb([128, 4, 4, TM], BF16, "YB")
        MB = sb([128, 8, TM], BF16, "MB")
        GT = sb([128, 16], F32, "GT")
        EG = sb([128, 16], F32, "EG")
        LF = sb([128, 8], F32, "LF")
        ARG = sb([128, 8, 4], F32, "ARG")
        EX = sb([128, 8, 4], F32, "EX")
        SF = sb([128, 4, 128], BF16, "SF")
        SB_ = sb([128, 4, 128], BF16, "SBk")
        KW = sb([128, 4, 128], BF16, "KW")
        CF32 = sb([128, 4, 130], F32, "CF32")
        CFB = sb([128, 4, 130], BF16, "CFB")
        CB32 = sb([128, 4, 130], F32, "CB32")
        CBB = sb([128, 4, 130], BF16, "CBB")
        CBL = [sb([128, 4, 130], BF16, f"CBL{i}") for i in range(2)]
        HS = sb([128, 4, 128], F32, "HS")
        HN = sb([128, 4, 128], F32, "HN")
        BST = sb([128, 4, 6], F32, "BST")
        MV = sb([128, 4, 2], F32, "MV")
        SM = sb([128, 16], F32, "SM")
        WST = sb([128, 4, 128], BF16, "WST")
        PWB = sb([128, 4, 128], BF16, "PWB")
        PT = [sb([128, TMH], F32, f"PT{i}") for i in range(3)]
        em.op('pool', lambda e: e.memset(VE[:], 1.0), [], ['VE'])

        def inproj_fm(l, ukey, T, cols, bias_c0, evac, after_group=None):
            W, wk = wload(ukey)
            for q in range(4):
                p, k = PS[q % 3], f'ps{q % 3}'
                for kc in range(8):
                    em.op('pe', lambda e: e.matmul(p[:, 0:T], lhsT=W[:, kc * 512 + q * 128:kc * 512 + (q + 1) * 128],
                                                   rhs=HT[:, kc, cols[0]:cols[1]], start=(kc == 0), stop=(kc == 7)),
                          [wk, 'HT'], [k], inc=(kc == 7))
                evac(q, p, k, pp(f'b_fm{l}', bias_c0 + q))
                if after_group:
                    after_group()

        def inproj_tm(l, ukey, ncols, c, evac, wcols=512):
            W, wk = ukey
            p, k = PS[c % 2], f'ps{c % 2}'
            t0 = HALO + c * 128
            for kc in range(8):
                em.op('pe', lambda e: e.matmul(p[:, 0:ncols], lhsT=HT[:, kc, t0:t0 + 128],
                                               rhs=W[:, kc * wcols:kc * wcols + ncols], start=(kc == 0), stop=(kc == 7)),
                      [wk, 'HT'], [k], inc=(kc == 7))
            evac(p, k)

        def zero_halo(buf, key, q, first, lastt):
            if first:
                em.op('pool', lambda e: e.memset(buf[:, q, 0:HALO], 0.0), [], [key])
            if lastt:
                em.op('pool', lambda e: e.memset(buf[:, q, HALO + TM:TMH], 0.0), [], [key])

        def load_tile_halo(src, ntok, ti, X=None):
            Xb, xk = X if X is not None else XBUF[0]
            t0 = ti * TM
            lo = max(t0 - HALO, 0)
            hi = min(t0 + TM + HALO, ntok)
            if lo > t0 - HALO:
                em.op('pool', lambda e: e.memset(Xb[:, :, 0:HALO], 0.0), [], [xk])
            if hi < t0 + TM + HALO:
                em.op('pool', lambda e: e.memset(Xb[:, :, HALO + TM:TMH], 0.0), [], [xk])
            em.dma('pool', 'ldx', Xb[:, :, lo - (t0 - HALO):hi - (t0 - HALO)],
                   src[:, lo:hi].rearrange("(c p) t -> p c t", p=128), reads=['xs'], writes=[xk])

        def mod_ht(l, j, s, T, X):
            Xb, xk = X
            for i in range(8):
                em.op('dve', lambda e: e.tensor_scalar(out=HT[:, i, 0:T], in0=Xb[:, i, 0:T], scalar1=mod(l, j, 1, i, s),
                                                       scalar2=mod(l, j, 0, i, s), op0=ALU.mult, op1=ALU.add),
                      [xk, 'mods'], ['HT'])

        def mod_xa(T, c0, X):
            Xb, xk = X
            em.op('pool', lambda e: e.tensor_scalar(out=XA[:, :, 0:T - 2 * c0], in0=Xb[:, :, c0:T - c0], scalar1=ALPHA,
                                                    scalar2=0.0, op0=ALU.mult, op1=ALU.add), [xk], ['XA'])

        def gate_scalars(l, c, W, wk):
            def ev(p, k):
                em.op('dve', lambda e: e.tensor_tensor(out=GT[:], in0=p[:, 0:16], in1=pp(f'b_gt{l}'), op=ALU.add),
                      [k, 'prm'], ['GT'])
            inproj_tm(l, (W, wk), 16, c, ev, wcols=16)
            em.op('act', lambda e: e.activation(out=EG[:], in_=GT[:], func=AF.Exp, scale=-1.0), ['GT'], ['EG'])
            em.op('act', lambda e: e.activation(out=EG[:], in_=EG[:], func=AF.Ln, bias=1.0), ['EG'], ['EG'])
            em.op('dve', lambda e: e.tensor_scalar(out=LF[:, 0:4], in0=EG[:, 4:8], scalar1=-1.0, scalar2=None,
                                                   op0=ALU.mult), ['EG'], ['LF'])
            em.op('dve', lambda e: e.tensor_scalar(out=LF[:, 4:8], in0=EG[:, 12:16], scalar1=-1.0, scalar2=None,
                                                   op0=ALU.mult), ['EG'], ['LF'])
            pc = PS[7]
            for n, m in enumerate([triu, tril, ones]):
                em.op('pe', lambda e: e.matmul(pc[:, 400 + n * 8:400 + n * 8 + 8], lhsT=m, rhs=LF[:], start=True,
                                               stop=True), ['LF', 'prm'], ['ps7c'])
            bf, bb = pc[:, 400:404], pc[:, 412:416]
            totf, totb = pc[:, 416:420], pc[:, 420:424]
            em.op('dve', lambda e: e.scalar_tensor_tensor(out=ARG[:, 0, :], in0=GT[:, 0:4], scalar=LNS, in1=bf,
                                                          op0=ALU.add, op1=ALU.subtract), ['GT', 'ps7c'], ['ARG'])
            em.op('dve', lambda e: e.scalar_tensor_tensor(out=ARG[:, 1, :], in0=GT[:, 8:12], scalar=LNS, in1=bb,
                                                          op0=ALU.add, op1=ALU.subtract), ['GT', 'ps7c'], ['ARG'])
            em.op('dve', lambda e: e.tensor_copy(out=ARG[:, 2, :], in_=bf), ['ps7c'], ['ARG'])
            em.op('dve', lambda e: e.tensor_copy(out=ARG[:, 3, :], in_=bb), ['ps7c'], ['ARG'])
            em.op('dve', lambda e: e.tensor_tensor(out=ARG[:, 4, :], in0=ARG[:, 0, :], in1=totf, op=ALU.add),
                  ['ARG', 'ps7c'], ['ARG'])
            em.op('dve', lambda e: e.tensor_tensor(out=ARG[:, 5, :], in0=ARG[:, 1, :], in1=totb, op=ALU.add),
                  ['ARG', 'ps7c'], ['ARG'])
            em.op('dve', lambda e: e.tensor_copy(out=ARG[:, 6, :], in_=totf), ['ps7c'], ['ARG'])
            em.op('dve', lambda e: e.tensor_copy(out=ARG[:, 7, :], in_=totb), ['ps7c'], ['ARG'])
            em.op('act', lambda e: e.activation(out=EX[:], in_=ARG[:], func=AF.Exp), ['ARG'], ['EX'])

        def qk_conv(l, q, qi, dst_list):
            w = lambda j: pp(f'qkw{l}', qi * 3 + j)
            em.op('dve', lambda e: e.tensor_scalar(out=T1[:, 0:TM], in0=CST[:, q, HALO - 1:HALO - 1 + TM], scalar1=w(0),
                                                   scalar2=None, op0=ALU.mult), ['CST', 'prm'], ['T1'])
            em.op('dve', lambda e: e.scalar_tensor_tensor(out=T1[:, 0:TM], in0=CST[:, q, HALO:HALO + TM], scalar=w(1),
                                                          in1=T1[:, 0:TM], op0=ALU.mult, op1=ALU.add),
                  ['CST', 'prm', 'T1'], ['T1'])
            em.op('dve', lambda e: e.scalar_tensor_tensor(out=T1[:, 0:TM], in0=CST[:, q, HALO + 1:HALO + 1 + TM],
                                                          scalar=w(2), in1=T1[:, 0:TM], op0=ALU.mult, op1=ALU.add),
                  ['CST', 'prm', 'T1'], ['T1'])
            for dst, key in dst_list:
                em.op('act', lambda e: e.activation(out=dst, in_=T1[:, 0:TM], func=AF.Silu), ['T1'], [key])

        def evac_bias(buf, key, qoff, first, lastt):
            def ev(q, p, k, b):
                em.op('act', lambda e: e.activation(out=buf[:, qoff + q, 0:TMH], in_=p[:, 0:TMH], func=AF.Identity,
                                                    bias=b), [k, 'prm'], [key])
                zero_halo(buf, key, qoff + q, first, lastt)
            return ev

        def k_transposed_scaled(c, h, grp):
            em.op('pe', lambda e: e.transpose(PS[4][:, h * 128:(h + 1) * 128], KTf[:, h, c * 128:(c + 1) * 128], ident),
                  ['KTf', 'prm'], ['ps4'])
            em.op('dve', lambda e: e.tensor_scalar(out=KW[:, h, :], in0=PS[4][:, h * 128:(h + 1) * 128],
                                                   scalar1=EX[:, grp, h:h + 1], scalar2=None, op0=ALU.mult),
                  ['ps4', 'EX'], ['KW'])

        def state_update(c, h, C32, Cb, key32, keyb, grp_dec):
            reg = h % 3
            pr = PS[7][:, reg * 130:reg * 130 + 129]
            em.op('pe', lambda e: e.matmul(pr, lhsT=KW[:, h, :], rhs=VE[:, c, h, 0:129], start=True, stop=True),
                  ['KW', 'VE'], [f'ps7r{reg}'])
            em.op('dve', lambda e: e.scalar_tensor_tensor(out=C32[:, h, 0:129], in0=C32[:, h, 0:129],
                                                          scalar=EX[:, grp_dec, h:h + 1], in1=pr, op0=ALU.mult,
                                                          op1=ALU.add), [key32, 'EX', f'ps7r{reg}'], [key32])
            em.op('pool', lambda e: e.tensor_copy(out=Cb[:, h, 0:129], in_=C32[:, h, 0:129]), [key32], [keyb])

        def state_pass(l, s, src, ntok, ti, backward, save_ap):
            first, lastt = ti == 0, ti == ntok // TM - 1
            load_tile_halo(src, ntok, ti)
            modulate(l, 1, s, TMH, c0=HALO)
            inproj_fm(l, ('in', l, 3), TMH, (0, TMH), 12, evac_bias(CST, 'CST', 4, first, lastt))
            for q in range(4):
                qk_conv(l, 4 + q, 4 + q, [(KTf[:, q, :], 'KTf')])
            Wg, wgk = wload(('in', l, 'gt'), 128)
            cs_order = [1, 0] if backward else [0, 1]
            Wv_ = wload(('in', l, 8))
            for c in cs_order:
                W, wk = Wv_

                def ev(p, k, c=c):
                    em.op('dve', lambda e: e.tensor_tensor(out=VE[:, c, :, 0:128],
                                                           in0=p[:, 0:512].rearrange("p (h d) -> p h d", h=4),
                                                           in1=pp(f'b_mv{l}').rearrange("p (h d) -> p h d", h=4),
                                                           op=ALU.add), [k, 'prm'], ['VE'])
                inproj_tm(l, (W, wk), 512, c, ev)
            for c in cs_order:
                gate_scalars(l, c, Wg, wgk)
                if backward:
                    gc = ti * 2 + c
                    em.dma('pool', 'stcb', save_ap[gc].rearrange("p (h d) -> p h d", h=4), CBB[:], reads=['CBB'],
                           writes=['cbscr'])
                for h in range(4):
                    k_transposed_scaled(c, h, 5 if backward else 4)
                    if backward:
                        state_update(c, h, CB32, CBB, 'CB32', 'CBB', 7)
                    else:
                        state_update(c, h, CF32, CFB, 'CF32', 'CFB', 6)

        def gelu_T(dst, src_ap, src_keys, dkey, T, tmp, tkey):
            em.op('act', lambda e: e.activation(out=tmp, in_=src_ap, func=AF.Square), src_keys, [tkey])
            em.op('dve', lambda e: e.tensor_scalar(out=tmp, in0=tmp, scalar1=0.044715, scalar2=1.0, op0=ALU.mult,
                                                   op1=ALU.add), [tkey], [tkey])
            em.op('dve', lambda e: e.tensor_tensor(out=tmp, in0=tmp, in1=src_ap, op=ALU.mult), [tkey] + src_keys,
                  [tkey])
            em.op('act', lambda e: e.activation(out=tmp, in_=tmp, func=AF.Sigmoid, scale=GK), [tkey], [tkey])
            em.op('dve', lambda e: e.tensor_tensor(out=dst, in0=tmp, in1=src_ap, op=ALU.mult), [tkey] + src_keys,
                  [dkey])

        def mixer_tile(l, s, src, dst, ntok, ti, cb_scr, want_out=True, X=None, skip_pre=False, next_pre=None,
                       next_xa=None):
            first, lastt = ti == 0, ti == ntok // TM - 1
            X = X if X is not None else XBUF[0]
            Xb, xk = X
            if not skip_pre:
                load_tile_halo(src, ntok, ti, X)
                mod_ht(l, 1, s, TMH, X)
                mod_xa(TMH, HALO, X)
            for c in range(2):
                gc = ti * 2 + c
                em.dma('pool', f'ldcb{c}', CBL[c][:], cb_scr[gc].rearrange("p (h d) -> p h d", h=4), reads=['cbscr'],
                       writes=[f'CBL{c}'])
            inproj_fm(l, ('in', l, 0), TMH, (0, TMH), 0, evac_bias(CA, 'CA', 0, False, False))

            def ev_glu(q, p, k, b):
                em.op('act', lambda e: e.activation(out=PT[0][:, 0:TMH], in_=p[:, 0:TMH], func=AF.Sigmoid, bias=b),
                      [k, 'prm'], ['PT0'])
                em.op('dve', lambda e: e.tensor_tensor(out=CA[:, q, 0:TMH], in0=CA[:, q, 0:TMH], in1=PT[0][:, 0:TMH],
                                                       op=ALU.mult), ['CA', 'PT0'], ['CA'])
                zero_halo(CA, 'CA', q, first, lastt)
            inproj_fm(l, ('in', l, 1), TMH, (0, TMH), 4, ev_glu)
            conv_thunks = []
            for q in range(4):
                def t0_(q=q):
                    em.op('dve', lambda e: e.tensor_scalar(out=CACC[:, q, :], in0=CA[:, q, 1:1 + TM],
                                                           scalar1=pp(f'conv_w{l}', q * 31),
                                                           scalar2=pp(f'conv_b{l}', q), op0=ALU.mult, op1=ALU.add),
                          ['CA', 'prm'], ['CACC'])
                conv_thunks.append(t0_)
                for j in range(1, 31):
                    def tj_(q=q, j=j):
                        em.op('dve', lambda e: e.scalar_tensor_tensor(out=CACC[:, q, :], in0=CA[:, q, 1 + j:1 + j + TM],
                                                                      scalar=pp(f'conv_w{l}', q * 31 + j),
                                                                      in1=CACC[:, q, :], op0=ALU.mult, op1=ALU.add),
                              ['CA', 'prm', 'CACC'], ['CACC'])
                    conv_thunks.append(tj_)

            def drain2():
                for _ in range(2):
                    if conv_thunks:
                        conv_thunks.pop(0)()
            inproj_fm(l, ('in', l, 2), TMH, (0, TMH), 8, evac_bias(CST, 'CST', 0, first, lastt), after_group=drain2)
            inproj_fm(l, ('in', l, 3), TMH, (0, TMH), 12, evac_bias(CST, 'CST', 4, first, lastt), after_group=drain2)
            for q in range(4):
                qk_conv(l, q, q, [(QT[:, q, :], 'QT')])
            for q in range(4):
                qk_conv(l, 4 + q, 4 + q, [(KTf[:, q, :], 'KTf'), (KTb[:, q, :], 'KTb')])
            inproj_fm(l, ('in', l, 4), TMH, (0, TMH), 16, evac_bias(CST, 'CST', 0, first, lastt), after_group=drain2)
            em.op('pool', lambda e: e.tensor_copy(out=PWB[:], in_=pp(f'pool_w{l}').rearrange("p (g e) -> p g e", g=4)),
                  ['prm'], ['PWB'])
            for g, win in enumerate((2, 4, 8, 16)):
                lo, hi = win // 2, win - 1 - win // 2
                cur, ck = CST[:, g, :], 'CST'
                k = 1
                nb = 0
                W0 = TMH
                while k < win:
                    nxt = PT[nb % 2]
                    nk = f'PT{nb % 2}'
                    cin_, cink = cur, ck
                    em.op('pool', lambda e: e.tensor_tensor(out=nxt[:, k:W0], in0=cin_[:, k:W0], in1=cin_[:, 0:W0 - k],
                                                            op=ALU.add), [cink], [nk])
                    cur, ck = nxt, nk
                    k *= 2
                    nb += 1
                a0 = HALO + hi
                ic = PT[2]
                em.op('pool', lambda e: e.memset(ic[:, 0:TM], 1.0 / win), [], ['PT2'])
                if first:
                    for t in range(lo):
                        em.op('pool', lambda e: e.memset(ic[:, t:t + 1], 1.0 / (t + hi + 1)), [], ['PT2'])
                if lastt:
                    for t in range(TM - hi, TM):
                        em.op('pool', lambda e: e.memset(ic[:, t:t + 1], 1.0 / (TM - t + lo)), [], ['PT2'])
                em.op('dve', lambda e: e.tensor_tensor(out=T1[:, 0:TM], in0=cur[:, a0:a0 + TM], in1=ic[:, 0:TM],
                                                       op=ALU.mult), [ck, 'PT2'], ['T1'])
                em.op('dve', lambda e: e.tensor_tensor(out=MB[:, g, :], in0=T1[:, 0:TM], in1=CST[:, g, HALO:HALO + TM],
                                                       op=ALU.subtract), ['T1', 'CST'], ['MB'])
                em.op('pe', lambda e: e.matmul(PS[3][:, 0:TM], lhsT=PWB[:, g, :], rhs=MB[:, g, :], start=True, stop=True),
                      ['PWB', 'MB'], ['ps3'])
                em.op('act', lambda e: e.activation(out=YB[:, 3, g, :], in_=PS[3][:, 0:TM], func=AF.Copy,
                                                    scale=pp(f'pool_s{l}', g)), ['ps3', 'prm'], ['YB'])
            def ev_u(q, p, k, b):
                em.op('act', lambda e: e.activation(out=PT[1][:, 0:TM], in_=p[:, 0:TM], func=AF.Identity, bias=b),
                      [k, 'prm'], ['PT1'])
                gelu_T(UT[:, q, :], PT[1][:, 0:TM], ['PT1'], 'UT', TM, PT[0][:, 0:TM], 'PT0')
            inproj_fm(l, ('in', l, 5), TM, (HALO, HALO + TM), 20, ev_u, after_group=drain2)

            def ev_o(q, p, k, b):
                em.op('act', lambda e: e.activation(out=SO[:, q, :], in_=p[:, 0:TM], func=AF.Sigmoid, bias=b),
                      [k, 'prm'], ['SO'])
            inproj_fm(l, ('in', l, 6), TM, (HALO, HALO + TM), 24, ev_o, after_group=drain2)
            em.op('pool', lambda e: e.tensor_copy(out=WST[:], in_=pp(f'wsT{l}').rearrange("p (g t) -> p g t", g=4)),
                  ['prm'], ['WST'])
            Wv = wload(('in', l, 7))
            for c in range(2):
                def ev_v(p, k, c=c):
                    em.op('dve', lambda e: e.tensor_tensor(out=VT[:, c, :], in0=p[:, 0:512], in1=pp(f'b_gv{l}'),
                                                           op=ALU.add), [k, 'prm'], ['VT'])
                inproj_tm(l, Wv, 512, c, ev_v)
                gelu_T(VT[:, c, :], VT[:, c, :], ['VT'], 'VT', 512, T2[:, 0:512], 'T2')
                em.op('dve', lambda e: e.bn_stats(out=BST[:, 0, :], in_=VT[:, c, :]), ['VT'], ['BST'])
                em.op('dve', lambda e: e.bn_aggr(out=MV[:, 0, :], in_=BST[:, 0, :]), ['BST'], ['MV'])
                em.op('act', lambda e: e.activation(out=SM[:, 0:1], in_=MV[:, 0, 1:2], func=AF.Sqrt, bias=LN_EPS),
                      ['MV'], ['SM'])
                em.op('dve', lambda e: e.reciprocal(out=SM[:, 0:1], in_=SM[:, 0:1]), ['SM'], ['SM'])
                em.op('dve', lambda e: e.tensor_scalar(out=VT[:, c, :], in0=VT[:, c, :], scalar1=MV[:, 0, 0:1],
                                                       scalar2=SM[:, 0:1], op0=ALU.subtract, op1=ALU.mult),
                      ['VT', 'MV', 'SM'], ['VT'])
                em.op('pool', lambda e: e.tensor_tensor(out=VT[:, c, :], in0=VT[:, c, :], in1=pp(f'gln_g{l}'),
                                                        op=ALU.mult), ['VT', 'prm'], ['VT'])
                em.op('pool', lambda e: e.tensor_tensor(out=VN[:, c, :], in0=VT[:, c, :], in1=pp(f'gln_b{l}'),
                                                        op=ALU.add), ['VT', 'prm'], ['VN'])
            for g in range(4):
                for c in range(2):
                    em.op('pe', lambda e: e.matmul(PS[3][:, c * 128:(c + 1) * 128], lhsT=VN[:, c, g * 128:(g + 1) * 128],
                                                   rhs=WST[:, g, :], start=True, stop=True), ['VN', 'WST'], ['ps3'])
                bsg = pp(f'bs{l}', g * 128, (g + 1) * 128)
                for c in range(2):
                    em.op('dve', lambda e: e.tensor_tensor(out=T1[:, c * 128:(c + 1) * 128],
                                                           in0=PS[3][:, c * 128:(c + 1) * 128], in1=bsg, op=ALU.add),
                          ['ps3', 'prm'], ['T1'])
                em.op('dve', lambda e: e.tensor_tensor(out=YB[:, 0, g, :], in0=T1[:, 0:TM], in1=UT[:, g, :], op=ALU.mult),
                      ['T1', 'UT'], ['YB'])
            Wmv = wload(('in', l, 8))
            for c in range(2):
                def ev_mv(p, k, c=c):
                    em.op('dve', lambda e: e.tensor_tensor(out=VE[:, c, :, 0:128],
                                                           in0=p[:, 0:512].rearrange("p (h d) -> p h d", h=4),
                                                           in1=pp(f'b_mv{l}').rearrange("p (h d) -> p h d", h=4),
                                                           op=ALU.add), [k, 'prm'], ['VE'])
                inproj_tm(l, Wmv, 512, c, ev_mv)
            Wg, wgk = wload(('in', l, 'gt'), 128)
            for c in range(2):
                gate_scalars(l, c, Wg, wgk)
                cs_ = slice(c * 128, (c + 1) * 128)
                for h in range(4):
                    em.op('pe', lambda e: e.matmul(PS[3][:, h * 128:(h + 1) * 128], lhsT=KTb[:, h, cs_], rhs=QT[:, h, cs_],
                                                   start=True, stop=True), ['KTb', 'QT'], ['ps3'])
                for h in range(4):
                    em.op('dve', lambda e: e.scalar_tensor_tensor(out=SF[:, h, :], in0=PS[3][:, h * 128:(h + 1) * 128],
                                                                  scalar=EX[:, 0, h:h + 1], in1=triu, op0=ALU.mult,
                                                                  op1=ALU.mult), ['ps3', 'EX', 'prm'], ['SF'])
                    em.op('dve', lambda e: e.scalar_tensor_tensor(out=SB_[:, h, :], in0=PS[3][:, h * 128:(h + 1) * 128],
                                                                  scalar=EX[:, 1, h:h + 1], in1=tril, op0=ALU.mult,
                                                                  op1=ALU.mult), ['ps3', 'EX', 'prm'], ['SBk'])
                for hp in range(2):
                    for (Sx, sk, Cx, ckey, pb, pk_) in ((SF, 'SF', CFB, 'CFB', PS[5], 'ps5'),
                                                        (SB_, 'SBk', CBL[c], f'CBL{c}', PS[6], 'ps6')):
                        for hh in range(2):
                            h = hp * 2 + hh
                            o_ = pb[:, hh * 130:hh * 130 + 129]
                            em.op('pe', lambda e: e.matmul(o_, lhsT=Sx[:, h, :], rhs=VE[:, c, h, 0:129], start=True,
                                                           stop=False), [sk, 'VE'], [pk_], inc=False)
                            em.op('pe', lambda e: e.matmul(o_, lhsT=QT[:, h, cs_], rhs=Cx[:, h, 0:129], start=False,
                                                           stop=True), ['QT', ckey], [pk_])
                    for d_, (pb, pk_) in enumerate(((PS[5], 'ps5'), (PS[6], 'ps6'))):
                        den = pb[:, 0:260].rearrange("p (h d) -> p h d", h=2)[:, :, 128]
                        eb = EX[:, 2 + d_, hp * 2:hp * 2 + 2]
                        sm = SM[:, 4 + d_ * 2:6 + d_ * 2]
                        em.op('act', lambda e: e.activation(out=sm, in_=den, func=AF.Abs), [pk_], ['SM'])
                        em.op('dve', lambda e: e.tensor_tensor(out=sm, in0=sm, in1=eb, op=ALU.mult), ['SM', 'EX'], ['SM'])
                        em.op('dve', lambda e: e.tensor_scalar(out=sm, in0=sm, scalar1=1.0, scalar2=None, op0=ALU.max),
                              ['SM'], ['SM'])
                        em.op('dve', lambda e: e.reciprocal(out=sm, in_=sm), ['SM'], ['SM'])
                        em.op('dve', lambda e: e.tensor_tensor(out=sm, in0=sm, in1=eb, op=ALU.mult), ['SM', 'EX'], ['SM'])
                    for hh in range(2):
                        h = hp * 2 + hh
                        em.op('act', lambda e: e.activation(out=HS[:, h, :], in_=PS[5][:, hh * 130:hh * 130 + 128],
                                                            func=AF.Copy, scale=SM[:, 4 + hh:5 + hh]), ['ps5', 'SM'],
                              ['HS'])
                        em.op('dve', lambda e: e.scalar_tensor_tensor(out=HS[:, h, :],
                                                                      in0=PS[6][:, hh * 130:hh * 130 + 128],
                                                                      scalar=SM[:, 6 + hh:7 + hh], in1=HS[:, h, :],
                                                                      op0=ALU.mult, op1=ALU.add),
                              ['ps6', 'SM', 'HS'], ['HS'])
                for h in range(4):
                    em.op('dve', lambda e: e.bn_stats(out=BST[:, h, :], in_=HS[:, h, :]), ['HS'], ['BST'])
                    em.op('dve', lambda e: e.bn_aggr(out=MV[:, h, :], in_=BST[:, h, :]), ['BST'], ['MV'])
                em.op('act', lambda e: e.activation(out=SM[:, 8:12], in_=MV[:, :, 1], func=AF.Sqrt, bias=LN_EPS),
                      ['MV'], ['SM'])
                em.op('dve', lambda e: e.reciprocal(out=SM[:, 8:12], in_=SM[:, 8:12]), ['SM'], ['SM'])
                for h in range(4):
                    em.op('dve', lambda e: e.tensor_scalar(out=HN[:, h, :], in0=HS[:, h, :], scalar1=MV[:, h, 0:1],
                                                           scalar2=SM[:, 8 + h:9 + h], op0=ALU.subtract, op1=ALU.mult),
                          ['HS', 'MV', 'SM'], ['HN'])
                    em.op('pe', lambda e: e.transpose(PS[4][:, h * 128:(h + 1) * 128], HN[:, h, :], ident),
                          ['HN', 'prm'], ['ps4'])
                    em.op('dve', lambda e: e.scalar_tensor_tensor(out=YB[:, 2, h, cs_], in0=PS[4][:, h * 128:(h + 1) * 128],
                                                                  scalar=pp(f'mln_g{l}', h), in1=SO[:, h, cs_],
                                                                  op0=ALU.mult, op1=ALU.mult),
                          ['ps4', 'prm', 'SO'], ['YB'])
                for h in range(4):
                    k_transposed_scaled(c, h, 4)
                    state_update(c, h, CF32, CFB, 'CF32', 'CFB', 6)
            if not want_out:
                return
            if not want_out:
                for th in conv_thunks:
                    pass
            for bi, b in enumerate([0, 2, 3, 1]):
                if b == 1:
                    while conv_thunks:
                        conv_thunks.pop(0)()
                    layer_norm_T(TM, f'cln_g{l}', f'cln_b{l}', 0, lambda i: YB[:, 1, i, :], 'YB', nch=4, src=CACC,
                                 src_key='CACC', silu=True, banks=(3, 4))
                Wb, wbk = wload(('br', l, b))
                Wgs = [None, None]
                for jj in range(8):
                    if jj % 4 == 0:
                        Wgs = wload(('in', l, 9 + b * 2 + jj // 4))
                    Wg2, wg2k = Wgs
                    pg, pgk = PS[jj % 2], f'ps{jj % 2}'
                    ppj, ppk = (PS[2], 'ps2') if jj % 2 == 0 else (PS[5], 'ps5')
                    q = jj % 4
                    for kc in range(8):
                        em.op('pe', lambda e: e.matmul(pg[:, 0:TM], lhsT=Wg2[:, kc * 512 + q * 128:kc * 512 + (q + 1) * 128],
                                                       rhs=HT[:, kc, HALO:HALO + TM], start=(kc == 0), stop=(kc == 7)),
                              [wg2k, 'HT'], [pgk], inc=(kc == 7))
                    for kc in range(4):
                        em.op('pe', lambda e: e.matmul(ppj[:, 0:TM], lhsT=Wb[:, kc * 1024 + jj * 128:kc * 1024 + (jj + 1) * 128],
                                                       rhs=YB[:, b, kc, :], start=(kc == 0), stop=(kc == 3)),
                              [wbk, 'YB'], [ppk], inc=(kc == 3))
                    tt, tk = (PT[0], 'PT0') if jj % 2 == 0 else (PT[1], 'PT1')
                    em.op('act', lambda e: e.activation(out=tt[:, 0:TM], in_=pg[:, 0:TM], func=AF.Sigmoid,
                                                        bias=pp(f'b_fm{l}', 28 + b * 8 + jj)), [pgk, 'prm'], [tk])
                    if bi == 0:
                        em.op('dve', lambda e: e.tensor_tensor(out=R[:, jj, 0:TM], in0=tt[:, 0:TM], in1=ppj[:, 0:TM],
                                                               op=ALU.mult), [tk, ppk], ['R'])
                    else:
                        em.op('dve', lambda e: e.tensor_tensor(out=tt[:, 0:TM], in0=tt[:, 0:TM], in1=ppj[:, 0:TM],
                                                               op=ALU.mult), [tk, ppk], [tk])
                        em.op('pool', lambda e: e.tensor_tensor(out=R[:, jj, 0:TM], in0=R[:, jj, 0:TM], in1=tt[:, 0:TM],
                                                                op=ALU.add), ['R', tk], ['R'])
                    for _ in range(3):
                        if conv_thunks and b != 1:
                            conv_thunks.pop(0)()
            em.op('act', lambda e: e.activation(out=MB[:], in_=R[:, :, 0:TM], func=AF.Copy), ['R'], ['MB'])
            if next_pre:
                next_pre()
            for j2 in range(2):
                Wo, wok = wload(('wo', l, j2))
                for q in range(4):
                    i = j2 * 4 + q
                    p, k = PS[q % 2], f'ps{q % 2}'
                    for kc in range(8):
                        em.op('pe', lambda e: e.matmul(p[:, 0:TM], lhsT=Wo[:, kc * 512 + q * 128:kc * 512 + (q + 1) * 128],
                                                       rhs=MB[:, kc, :], start=(kc == 0), stop=(kc == 7)),
                              [wok, 'MB'], [k], inc=(kc == 7))
                    em.op('dve', lambda e: e.scalar_tensor_tensor(out=R[:, i, 0:TM], in0=p[:, 0:TM],
                                                                  scalar=mod(l, 1, 2, i, s), in1=XA[:, i, 0:TM],
                                                                  op0=ALU.mult, op1=ALU.add), [k, 'mods', 'XA'], ['R'])
            if next_xa:
                next_xa()
            layer_norm_T(TM, f'ln_g{l}', f'ln_b{l}', 8, lambda i: Xb[:, i, 0:TM], xk, banks=(3, 4))
            t0 = ti * TM
            em.dma('pool', 'stx', dst[:, t0:t0 + TM].rearrange("(c p) t -> p c t", p=128), Xb[:, :, 0:TM], reads=[xk],
                   writes=['xs2'])

        def ffn_stage(l_prev, l_next, s, src, dst, ntok, T, add_pos=False):
            for ti in range(ntok // T):
                t0 = ti * T
                em.dma('pool', 'ldx', XT[:, :, 0:T], src[:, t0:t0 + T].rearrange("(c p) t -> p c t", p=128),
                       reads=['xs', 'xs2'], writes=['XT'])
                if add_pos:
                    nr = T // 64
                    r0 = t0 // 64
                    for c in range(2):
                        for sc_ in range(2):
                            ch = sc_ * 2 + c
                            ch2 = 4 + sc_ * 2 + c
                            for r in range(nr):
                                em.op('dve', lambda e: e.tensor_scalar(out=XT[:, ch, r * 64:(r + 1) * 64],
                                                                       in0=XT[:, ch, r * 64:(r + 1) * 64],
                                                                       scalar1=ptab[:, c, sc_, r0 + r:r0 + r + 1],
                                                                       scalar2=None, op0=ALU.add), ['XT', 'ptab'], ['XT'])
                                em.op('pool', lambda e: e.tensor_tensor(out=XT[:, ch2, r * 64:(r + 1) * 64],
                                                                        in0=XT[:, ch2, r * 64:(r + 1) * 64],
                                                                        in1=ctab[:, c, sc_, :], op=ALU.add),
                                      ['XT', 'ptab'], ['XT'])
                if l_prev is not None:
                    ffn(l_prev, 1, s, T)
                if l_next is not None:
                    ffn(l_next, 0, s, T)
                em.dma('pool', 'stx', dst[:, t0:t0 + T].rearrange("(c p) t -> p c t", p=128), XT[:, :, 0:T], reads=['XT'],
                       writes=['xs', 'xs2'])

        XBUF = [(XT, 'XT'), (XTB, 'XTb')]

        class FJ:
            def __init__(self, l, f, s, T, X, pre=None, post=None):
                self.l, self.f, self.s, self.T, self.X, self.pre, self.post = l, f, s, T, X, pre, post
                self.j = 0 if f == 0 else 2

            def mod_(self):
                X, xk = self.X
                l, j, s, T = self.l, self.j, self.s, self.T
                for i in range(8):
                    em.op('dve', lambda e: e.tensor_scalar(out=HT[:, i, 0:T], in0=X[:, i, 0:T], scalar1=mod(l, j, 1, i, s),
                                                           scalar2=mod(l, j, 0, i, s), op0=ALU.mult, op1=ALU.add),
                          [xk, 'mods'], ['HT'])

            def xa_(self):
                X, xk = self.X
                T = self.T
                em.op('pool', lambda e: e.tensor_scalar(out=XA[:, :, 0:T], in0=X[:, :, 0:T], scalar1=ALPHA, scalar2=0.0,
                                                        op0=ALU.mult, op1=ALU.add), [xk], ['XA'])

            def win_(self):
                l, f, T = self.l, self.f, self.T
                for uu in range(11):
                    W, wk = wload(('fi', l, f, uu))
                    for q in range(2):
                        jj = uu * 2 + q
                        p1, p2 = PS[(jj % 2) * 2], PS[(jj % 2) * 2 + 1]
                        k1, k2 = f'ps{(jj % 2) * 2}', f'ps{(jj % 2) * 2 + 1}'
                        for kc in range(8):
                            em.op('pe', lambda e: e.matmul(p1[:, 0:T], lhsT=W[:, kc * 256 + q * 128:kc * 256 + (q + 1) * 128],
                                                           rhs=HT[:, kc, 0:T], start=(kc == 0), stop=(kc == 7)),
                                  [wk, 'HT'], [k1], inc=(kc == 7))
                        for kc in range(8):
                            em.op('pe', lambda e: e.matmul(p2[:, 0:T],
                                                           lhsT=W[:, 2048 + kc * 256 + q * 128:2048 + kc * 256 + (q + 1) * 128],
                                                           rhs=HT[:, kc, 0:T], start=(kc == 0), stop=(kc == 7)),
                                  [wk, 'HT'], [k2], inc=(kc == 7))
                        tt, tk = (T1, 'T1') if jj % 2 == 0 else (T2, 'T2')
                        em.op('act', lambda e: e.activation(out=tt[:, 0:T], in_=p1[:, 0:T], func=AF.Silu), [k1], [tk])
                        em.op('dve', lambda e: e.tensor_tensor(out=G[:, jj, 0:T], in0=tt[:, 0:T], in1=p2[:, 0:T],
                                                               op=ALU.mult), [tk, k2], ['G'])

            def wout_(self):
                l, f, s, T, j = self.l, self.f, self.s, self.T, self.j
                for i in range(8):
                    W, wk = wload(('fo', l, f, i), 22 * 128)
                    p, k = PS[4 + (i % 2)], f'ps{4 + (i % 2)}'
                    for jj in range(FC):
                        em.op('pe', lambda e: e.matmul(p[:, 0:T], lhsT=W[:, jj * 128:(jj + 1) * 128], rhs=G[:, jj, 0:T],
                                                       start=(jj == 0), stop=(jj == FC - 1)), [wk, 'G'], [k],
                              inc=(jj == FC - 1))
                    em.op('dve', lambda e: e.scalar_tensor_tensor(out=R[:, i, 0:T], in0=p[:, 0:T], scalar=mod(l, j, 2, i, s),
                                                                  in1=XA[:, i, 0:T], op0=ALU.mult, op1=ALU.add),
                          [k, 'mods', 'XA'], ['R'])

            def ln_(self):
                X, xk = self.X
                T = self.T
                layer_norm_T(T, f'ln_g{self.l}', f'ln_b{self.l}', self.j * 8, lambda i: X[:, i, 0:T], xk,
                             tmp=(PT[2], 'PT2'))

        def ffn_pipe(jobs):
            n = len(jobs)
            if jobs[0].pre:
                jobs[0].pre()
            jobs[0].mod_()
            jobs[0].xa_()
            for k in range(n):
                jobs[k].win_()
                same = (k + 1 < n) and (jobs[k + 1].X[1] == jobs[k].X[1])
                if k + 1 < n and not same:
                    if jobs[k + 1].pre:
                        jobs[k + 1].pre()
                    jobs[k + 1].mod_()
                jobs[k].wout_()
                if k + 1 < n and not same:
                    jobs[k + 1].xa_()
                jobs[k].ln_()
                if jobs[k].post:
                    jobs[k].post()
                if same:
                    if jobs[k + 1].pre:
                        jobs[k + 1].pre()
                    jobs[k + 1].mod_()
                    jobs[k + 1].xa_()

        def stage_jobs(ls, tiles):
            jobs = []

            def mk(tile, X):
                s_, src, dst, t0, T, add_pos = tile
                Xb, xk = X

                def pre():
                    em.dma('pool', 'ldx', Xb[:, :, 0:T], src[:, t0:t0 + T].rearrange("(c p) t -> p c t", p=128),
                           reads=['xs2'], writes=[xk])
                    if add_pos:
                        nr = T // 64
                        r0 = t0 // 64
                        for c in range(2):
                            for sc_ in range(2):
                                ch = sc_ * 2 + c
                                ch2 = 4 + sc_ * 2 + c
                                for r in range(nr):
                                    em.op('dve', lambda e: e.tensor_scalar(out=Xb[:, ch, r * 64:(r + 1) * 64],
                                                                           in0=Xb[:, ch, r * 64:(r + 1) * 64],
                                                                           scalar1=ptab[:, c, sc_, r0 + r:r0 + r + 1],
                                                                           scalar2=None, op0=ALU.add), [xk, 'ptab'], [xk])
                                    em.op('pool', lambda e: e.tensor_tensor(out=Xb[:, ch2, r * 64:(r + 1) * 64],
                                                                            in0=Xb[:, ch2, r * 64:(r + 1) * 64],
                                                                            in1=ctab[:, c, sc_, :], op=ALU.add),
                                          [xk, 'ptab'], [xk])

                def post():
                    em.dma('pool', 'stx', dst[:, t0:t0 + T].rearrange("(c p) t -> p c t", p=128), Xb[:, :, 0:T],
                           reads=[xk], writes=['xs'])
                js = [FJ(l_, f_, s_, T, X) for (l_, f_) in ls]
                js[0].pre = pre
                js[-1].post = post
                return js
            i = 0
            while i < len(tiles):
                if i + 1 < len(tiles):
                    ja, jb = mk(tiles[i], XBUF[0]), mk(tiles[i + 1], XBUF[1])
                    for a_, b_ in zip(ja, jb):
                        jobs += [a_, b_]
                    i += 2
                else:
                    jobs += mk(tiles[i], XBUF[0])
                    i += 1
            return jobs

        xa_, xb_ = xs
        ca_, cb_ = cs
        for l in range(nl):
            if l > 0:
                em.new_epoch(['pe', 'act', 'dve'])
                em.depoch += 1
            last = (l == last_layer)
            em.dma('sp', 'prm', prl[:], prl_d[l], writes=['prm'])
            ls_ = [(l, 0)] if l == 0 else [(l - 1, 1), (l, 0)]
            tiles_ = [(1, cin if l == 0 else cb_, ca_, 0, NCTX, False)]
            tiles_ += [(0, xin if l == 0 else xb_, xa_, ti * TF, TF, l == 0) for ti in range(N // TF)]
            ffn_pipe(stage_jobs(ls_, tiles_))
            if stop == ('A', l):
                break
            em.op('dve', lambda e: e.memset(CB32[:], 0.0), [], ['CB32'])
            em.op('dve', lambda e: e.memset(CBB[:], 0.0), [], ['CBB'])
            for ti in reversed(range(NCTX // TM)):
                state_pass(l, 1, ca_, NCTX, ti, True, cb_c)
            for ti in reversed(range(NMT)):
                state_pass(l, 0, xa_, N, ti, True, cb_x)
            em.op('dve', lambda e: e.memset(CF32[:], 0.0), [], ['CF32'])
            em.op('dve', lambda e: e.memset(CFB[:], 0.0), [], ['CFB'])
            chain = []
            for ti in range(NCTX // TM):
                if last:
                    state_pass(l, 1, ca_, NCTX, ti, False, None)
                else:
                    chain.append((1, ca_, cb_, NCTX, ti, cb_c))
            for ti in range(NMT):
                chain.append((0, xa_, xb_, N, ti, cb_x))
            for ci, (s_, src_, dst_, nt_, ti, cbs_) in enumerate(chain):
                if s_ == 0 and ti == NMT // 2 and NMT >= 16:
                    em.new_epoch(['dve'])
                X_ = XBUF[ci % 2]
                npre = nxa = None
                if ci + 1 < len(chain):
                    (s2, src2, dst2, nt2, ti2, cbs2) = chain[ci + 1]
                    X2 = XBUF[(ci + 1) % 2]

                    def npre(s2=s2, src2=src2, nt2=nt2, ti2=ti2, X2=X2):
                        load_tile_halo(src2, nt2, ti2, X2)
                        mod_ht(l, 1, s2, TMH, X2)

                    def nxa(X2=X2):
                        mod_xa(TMH, HALO, X2)
                mixer_tile(l, s_, src_, dst_, nt_, ti, cbs_, X=X_, skip_pre=(ci > 0), next_pre=npre, next_xa=nxa)
        fin_src = xb_
        if stop is None:
            ffn_pipe(stage_jobs([(nl - 1, 1)], [(0, xb_, yout, ti * TF, TF, False) for ti in range(N // TF)]))
        else:
            fin_src = xa_ if stop[0] == 'A' else xb_
            for ti in range(N // TF):
                t0 = ti * TF
                em.dma('pool', 'ldx', XT[:, :, 0:TF], fin_src[:, t0:t0 + TF].rearrange("(c p) t -> p c t", p=128),
                       reads=['xs', 'xs2'], writes=['XT'])
                em.dma('pool', 'stx', yout[:, t0:t0 + TF].rearrange("(c p) t -> p c t", p=128), XT[:, :, 0:TF],
                       reads=['XT'], writes=['xs', 'xs2'])
        em.deps('sp', ['xs', 'xs2'], [])
        build.last_counts = dict(em.cnt)
        build.sbuf_left = nc.sbuf_bytes_remaining() if callable(nc.sbuf_bytes_remaining) else nc.sbuf_bytes_remaining
    return nc


def kernel(x, c, ctx, c_ctx, w_ada, b_ada, ln_g, ln_b, ffn_w_in, ffn_w_out, w_in, b_in,
           gmlp_ln_g, gmlp_ln_b, gmlp_ws, gmlp_bs, conv_w, conv_b, conv_ln_g, conv_ln_b,
           qk_conv_w, mlstm_ln_g, pool_w, pool_scale, w_branch, w_out, _nl=None, _stop=None):
    x = np.asarray(x, np.float32)
    B, N, _ = x.shape
    nl = _nl or DEPTH
    f = lambda a: np.ascontiguousarray(np.asarray(a, np.float32)[:nl])
    packs = [pack_params(nl, np.asarray(c)[b], np.asarray(c_ctx), *[np.asarray(a, np.float32) for a in (
        b_ada, ln_g, ln_b, b_in, gmlp_ln_g, gmlp_ln_b, gmlp_ws, gmlp_bs, conv_w, conv_b, conv_ln_g, conv_ln_b,
        qk_conv_w, mlstm_ln_g, pool_w, pool_scale)]) for b in range(B)]
    prm_off = packs[0][0].off
    prl_off = packs[0][1][0].off
    prms = [p[0].get() for p in packs]
    prls = [np.stack([q.get() for q in p[1]], axis=0) for p in packs]
    nc = build(N, nl, prm_off, prms[0].shape[1], prl_off, prls[0].shape[2], stop=_stop)
    shared = {"w_ada": f(w_ada), "ffn_w_in": f(ffn_w_in), "ffn_w_out": f(ffn_w_out), "w_in": f(w_in),
              "w_branch": f(w_branch), "w_out": f(w_out)}
    xT = [np.ascontiguousarray(x[b].T) for b in range(B)]
    cT = [np.ascontiguousarray(np.asarray(ctx, np.float32)[b].T) for b in range(B)]
    ncores = 2
    in_maps = []
    for i in range(ncores):
        b = i % B
        m = {"xT": xT[b], "ctxT": cT[b], "prm": prms[b], "prl": prls[b]}
        m.update(shared)
        in_maps.append(m)
    res = run_bass_kernel_spmd(nc, in_maps, core_ids=list(range(ncores)))
    out = np.stack([np.ascontiguousarray(res.results[b]["yT"].T) for b in range(B)], axis=0)
    return out.astype(np.float32)
```

```python
import math
from contextlib import ExitStack

import numpy as np
import concourse.bass as bass
import concourse.mybir as mybir
from concourse.bass_utils import run_bass_kernel_spmd

F32 = mybir.dt.float32
BF16 = mybir.dt.bfloat16
AF = mybir.ActivationFunctionType
ALU = mybir.AluOpType

D = 1024
DC = 8
DFF = 2816
FC = 22
MIX = 512
NCTX = 256
DEPTH = 4
ALPHA = (2 * DEPTH) ** 0.25
LN_EPS = 1e-6
DH = 128
LNS = math.log(DH ** -0.5)
OFF_A, OFF_B, OFF_C = 0, 1024, 2048
OFF_GATES = 3584
OFF_O = 3600
OFF_D = 4112
OFF_G = 4624
IN_COLS = 8720
HALO = 16
TM = 256
TMH = TM + 2 * HALO
TF = 256
GK = 2.0 * math.sqrt(2.0 / math.pi)


class Em:
    def __init__(self, nc, es):
        self.nc = nc
        self.es = es
        self.engs = {'pe': nc.tensor, 'act': nc.scalar, 'dve': nc.vector, 'pool': nc.gpsimd, 'sp': nc.sync}
        self.sems = {}
        self.cnt = {}
        self.cur = {}
        self.epoch = 0
        self.seen = {k: {} for k in self.engs}
        self.lastw = {}
        self.readers = {}
        self.depoch = 0
        self.new_epoch()

    def _mksem(self, name):
        return self.es.enter_context(self.nc.semaphore(name))

    def new_epoch(self, engs=None):
        self.epoch += 1
        for k in (engs or self.engs):
            if k == 'sp' and 'sp' in self.cur:
                continue
            key = f"{k}{self.epoch}"
            self.sems[key] = self._mksem("s_" + key)
            self.cnt[key] = 0
            self.cur[k] = key

    def dsem(self, slot):
        if slot not in self.sems:
            self.sems[slot] = self._mksem("d_" + slot)
            self.cnt[slot] = 0
        return self.sems[slot]

    def _wait(self, e, semkey, val):
        if self.seen[e].get(semkey, 0) >= val:
            return
        self.engs[e].wait_ge(self.sems[semkey], val)
        self.seen[e][semkey] = val

    def deps(self, e, reads, writes, skip_self=False):
        mine = self.cur[e]
        for k in reads:
            w = self.lastw.get(k)
            if w and not (skip_self and w[0] == mine):
                self._wait(e, *w)
        for k in writes:
            w = self.lastw.get(k)
            if w and not (skip_self and w[0] == mine):
                self._wait(e, *w)
            for sk, v in self.readers.get(k, {}).items():
                if not (skip_self and sk == mine):
                    self._wait(e, sk, v)

    def done(self, semkey, val, reads, writes):
        for k in reads:
            d = self.readers.setdefault(k, {})
            d[semkey] = max(d.get(semkey, 0), val)
        for k in writes:
            self.lastw[k] = (semkey, val)
            self.readers[k] = {}

    @staticmethod
    def _norm(reads, writes):
        r2 = [k for k in reads if not k.startswith('ps')]
        w2 = [k[:3] if k.startswith('ps') else k for k in writes]
        w2 += [k[:3] for k in reads if k.startswith('ps')]
        return r2, list(dict.fromkeys(w2))

    def op(self, e, fn, reads=(), writes=(), inc=True):
        reads, writes = self._norm(list(reads), list(writes))
        self.deps(e, reads, writes, e == 'pe')
        ins = fn(self.engs[e])
        key = self.cur[e]
        if inc:
            self.cnt[key] += 1
            ins.then_inc(self.sems[key], 1)
            self.done(key, self.cnt[key], reads, writes)
        else:
            self.done(key, self.cnt[key] + 1, reads, writes)
        return ins

    def barrier(self):
        for e in self.engs:
            for key, v in self.cnt.items():
                if v > 0:
                    self._wait(e, key, v)

    def dma(self, e, slot, out, in_, reads=(), writes=()):
        self.deps(e, reads, writes)
        sem = self.dsem(slot)
        ins = self.engs[e].dma_start(out=out, in_=in_)
        self.cnt[slot] += 16
        ins.then_inc(sem, 16)
        self.done(slot, self.cnt[slot], reads, writes)
        return ins


class Packer:
    def __init__(self):
        self.off = {}
        self.parts = []
        self.n = 0

    def add(self, name, arr):
        arr = np.ascontiguousarray(arr, dtype=np.float32).reshape(128, -1)
        self.off[name] = (self.n, arr.shape[1])
        self.parts.append(arr)
        self.n += arr.shape[1]

    def get(self):
        return np.concatenate(self.parts, axis=1)


def fm(v):
    v = np.asarray(v, np.float32)
    n = v.shape[-1] // 128
    return np.moveaxis(v.reshape(v.shape[:-1] + (n, 128)), -1, 0)


def bc(v):
    v = np.asarray(v, np.float32)
    return np.broadcast_to(v[None], (128,) + v.shape)


def pack_params(nl, c_b, c_ctx, b_ada, ln_g, ln_b, b_in, gmlp_ln_g, gmlp_ln_b, gmlp_ws, gmlp_bs, conv_w, conv_b,
                conv_ln_g, conv_ln_b, qk_conv_w, mlstm_ln_g, pool_w, pool_scale):
    pk = Packer()
    pk.add('cvec', np.stack([fm(c_b), fm(c_ctx)], axis=-1))
    ii = np.arange(128, dtype=np.float32)
    pk.add('ident', np.eye(128, dtype=np.float32))
    pk.add('triu', (ii[:, None] <= ii[None, :]).astype(np.float32))
    pk.add('tril', (ii[:, None] >= ii[None, :]).astype(np.float32))
    pk.add('ones', np.ones((128, 128), np.float32))
    pk.add('pidx', ii[:, None])
    pk.add('ridx', bc(np.arange(128, dtype=np.float32)))
    pk.add('cidx', bc(np.arange(64, dtype=np.float32)))
    for l in range(nl):
        pk.add(f'b_ada{l}', fm(b_ada[l]))
        pk.add(f'ln_g{l}', fm(ln_g[l]))
        pk.add(f'ln_b{l}', fm(ln_b[l]))
    pls = []
    for l in range(nl):
        pl = Packer()
        bi = b_in[l]
        fmcols = np.concatenate([bi[1024:2048], bi[2048:3072], bi[OFF_D:OFF_D + 512], bi[0:512],
                                 bi[OFF_O:OFF_O + 512], bi[OFF_G:OFF_G + 4096]])
        pl.add('b_fm', fm(fmcols))
        pl.add('b_gv', bc(bi[512:1024]))
        pl.add('b_mv', bc(bi[3072:3584]))
        pl.add('b_gt', bc(bi[OFF_GATES:OFF_GATES + 16]))
        pl.add('gln_g', bc(gmlp_ln_g[l]))
        pl.add('gln_b', bc(gmlp_ln_b[l]))
        pl.add('wsT', np.transpose(gmlp_ws[l], (2, 0, 1)))
        pl.add('bs', bc(gmlp_bs[l]))
        pl.add('conv_w', np.transpose(fm(conv_w[l]), (0, 2, 1)))
        pl.add('conv_b', fm(conv_b[l]))
        pl.add('cln_g', fm(conv_ln_g[l]))
        pl.add('cln_b', fm(conv_ln_b[l]))
        pl.add('qkw', np.transpose(fm(qk_conv_w[l]), (0, 2, 1)))
        pl.add('mln_g', fm(mlstm_ln_g[l]))
        pl.add('pool_w', np.transpose(pool_w[l], (1, 0, 2)))
        pl.add('pool_s', fm(pool_scale[l]))
        pls.append(pl)
    return pk, pls


def build(N, nl, prm_off, nprm, prl_off, nprl, stop=None):
    nc = bass.Bass("TRN2", target_bir_lowering=False)
    last_layer = nl - 1
    NFT = N // TF
    NMT = N // TM
    xin = nc.dram_tensor("xT", [D, N], F32, kind="ExternalInput").ap()
    cin = nc.dram_tensor("ctxT", [D, NCTX], F32, kind="ExternalInput").ap()
    prm_d = nc.dram_tensor("prm", [128, nprm], F32, kind="ExternalInput").ap()
    prl_d = nc.dram_tensor("prl", [nl, 128, nprl], F32, kind="ExternalInput").ap()
    w_ada = nc.dram_tensor("w_ada", [nl, D, 9 * D], F32, kind="ExternalInput").ap()
    ffn_w_in = nc.dram_tensor("ffn_w_in", [nl, 2, D, 2 * DFF], F32, kind="ExternalInput").ap()
    ffn_w_out = nc.dram_tensor("ffn_w_out", [nl, 2, DFF, D], F32, kind="ExternalInput").ap()
    w_in = nc.dram_tensor("w_in", [nl, D, IN_COLS], F32, kind="ExternalInput").ap()
    w_branch = nc.dram_tensor("w_branch", [nl, 4, MIX, D], F32, kind="ExternalInput").ap()
    w_out = nc.dram_tensor("w_out", [nl, D, D], F32, kind="ExternalInput").ap()
    yout = nc.dram_tensor("yT", [D, N], F32, kind="ExternalOutput").ap()
    xs = [nc.dram_tensor(f"xs{i}", [D, N], F32).ap() for i in range(2)]
    cs = [nc.dram_tensor(f"cs{i}", [D, NCTX], F32).ap() for i in range(2)]
    NCH = N // 128
    cb_x = nc.dram_tensor("cb_x", [NCH, 128, 4 * 130], BF16).ap()
    cb_c = nc.dram_tensor("cb_c", [NCTX // 128, 128, 4 * 130], BF16).ap()
    UPL = 2 * 11 + 2 * 8 + 18 + 4 + 2
    wq = nc.dram_tensor("wq", [nl * UPL, 128, 4096], BF16).ap()

    es = ExitStack()
    with es:
        em = Em(nc, es)
        _n = [0]

        def sb(shape, dt, name=None):
            _n[0] += 1
            return es.enter_context(nc.sbuf_tensor(name or f"t{_n[0]}", shape, dt))

        PS = [es.enter_context(nc.psum_tensor(f"ps{i}", [128, 512], F32)) for i in range(8)]
        prm = sb([128, nprm], F32, "prm_sb")
        prl = sb([128, nprl], F32, "prl_sb")

        def pp(name, a=None, b=None):
            import re as _re
            m = _re.match(r"^(.*?)(\d+)$", name)
            if name in prm_off:
                o, n = prm_off[name]
                buf = prm
            else:
                base = m.group(1) if (m and m.group(1) in prl_off) else name
                o, n = prl_off[base]
                buf = prl
            if a is None:
                return buf[:, o:o + n]
            return buf[:, o + a:o + (b if b is not None else a + 1)]

        em.dma('sp', 'prm', prm[:], prm_d, writes=['prm'])

        ident = pp('ident')
        triu = pp('triu')
        tril = pp('tril')
        ones = pp('ones')

        sc = sb([128, 8, 2], F32, "sc")
        mods = sb([128, nl, 72, 2], F32, "mods")
        es_pro = ExitStack()
        stg = [es_pro.enter_context(nc.sbuf_tensor(f"stg{i}", [128, 4096], F32)) for i in range(2)]
        stb = [es_pro.enter_context(nc.sbuf_tensor(f"stb{i}", [128, 4096], BF16)) for i in range(2)]
        ucount = [0]
        cast_eng = ['act', 'dve']

        def prologue_unit(uidx, srcs):
            i = ucount[0] % 2
            ucount[0] += 1
            tot = 0
            for (o, a, b, ap) in srcs:
                dst = stg[i][:, o:o + a * b].rearrange("p (a b) -> p a b", a=a, b=b)
                em.dma('sp', f'pl{i}', dst, ap, writes=[f'stg{i}'])
                tot = max(tot, o + a * b)
            ce = cast_eng[uidx % 2]
            if ce == 'act':
                em.op('act', lambda e: e.activation(out=stb[i][:, 0:tot], in_=stg[i][:, 0:tot], func=AF.Copy),
                      [f'stg{i}'], [f'stb{i}'])
            else:
                em.op(ce, lambda e: e.tensor_copy(out=stb[i][:, 0:tot], in_=stg[i][:, 0:tot]),
                      [f'stg{i}'], [f'stb{i}'])
            em.dma('pool', f'plst{i}', wq[uidx, :, 0:tot], stb[i][:, 0:tot], reads=[f'stb{i}'], writes=[f'wq{i}'])

        def kc_ap(w2d, c0, ncols):
            return w2d[:, c0:c0 + ncols].rearrange("(kc p) c -> p kc c", p=128)

        UNIT = {}
        u = 0
        for l in range(nl):
            for f in range(2):
                for j in range(11):
                    w2 = ffn_w_in[l, f]
                    prologue_unit(u, [(0, 8, 256, kc_ap(w2, j * 256, 256)),
                                      (2048, 8, 256, kc_ap(w2, DFF + j * 256, 256))])
                    UNIT[('fi', l, f, j)] = u
                    u += 1
                for i in range(8):
                    prologue_unit(u, [(0, 22, 128, kc_ap(ffn_w_out[l, f], i * 128, 128))])
                    UNIT[('fo', l, f, i)] = u
                    u += 1
            incols = [1024, 1536, 2048, 2560, OFF_D, 0, OFF_O, 512, 3072] + [OFF_G + 512 * i for i in range(8)]
            for j, c0 in enumerate(incols):
                prologue_unit(u, [(0, 8, 512, kc_ap(w_in[l], c0, 512))])
                UNIT[('in', l, j)] = u
                u += 1
            prologue_unit(u, [(0, 8, 16, kc_ap(w_in[l], OFF_GATES, 16))])
            UNIT[('in', l, 'gt')] = u
            u += 1
            for b in range(4):
                prologue_unit(u, [(0, 4, 1024, kc_ap(w_branch[l, b], 0, 1024))])
                UNIT[('br', l, b)] = u
                u += 1
            for j in range(2):
                prologue_unit(u, [(0, 8, 512, kc_ap(w_out[l], j * 512, 512))])
                UNIT[('wo', l, j)] = u
                u += 1
        assert u == nl * UPL, (u, nl * UPL)

        cv = pp('cvec').rearrange("p (k t) -> p k t", k=8, t=2)
        em.op('act', lambda e: e.activation(out=sc[:], in_=cv, func=AF.Silu), ['prm'], ['sc'])
        for l in range(nl):
            for og in range(18):
                i = ucount[0] % 2
                ucount[0] += 1
                em.dma('sp', f'pl{i}', stg[i][:].rearrange("p (a b) -> p a b", a=8, b=512),
                       kc_ap(w_ada[l], og * 512, 512), writes=[f'stg{i}'])
                for q in range(4):
                    oc = og * 4 + q
                    for kc in range(8):
                        em.op('pe', lambda e: e.matmul(PS[0][:, oc * 2:oc * 2 + 2],
                                                       lhsT=stg[i][:, kc * 512 + q * 128: kc * 512 + (q + 1) * 128],
                                                       rhs=sc[:, kc, :], start=(kc == 0), stop=(kc == 7)),
                              [f'stg{i}', 'sc'], ['ps0'], inc=(kc == 7))
            ba = pp(f'b_ada{l}')
            for t in range(2):
                em.op('dve', lambda e: e.tensor_tensor(out=mods[:, l, :, t],
                                                       in0=PS[0][:, 0:144].rearrange("p (c t) -> p c t", t=2)[:, :, t],
                                                       in1=ba, op=ALU.add), ['ps0', 'prm'], ['mods'])
        for l in range(nl):
            for j in range(3):
                r = 1.0 if j == 1 else 0.5
                em.op('dve', lambda e: e.tensor_scalar(out=mods[:, l, (j * 3 + 1) * 8:(j * 3 + 2) * 8, :],
                                                       in0=mods[:, l, (j * 3 + 1) * 8:(j * 3 + 2) * 8, :],
                                                       scalar1=1.0, scalar2=None, op0=ALU.add), ['mods'], ['mods'])
                em.op('dve', lambda e: e.tensor_scalar(out=mods[:, l, (j * 3 + 2) * 8:(j * 3 + 3) * 8, :],
                                                       in0=mods[:, l, (j * 3 + 2) * 8:(j * 3 + 3) * 8, :],
                                                       scalar1=r, scalar2=None, op0=ALU.mult), ['mods'], ['mods'])

        em.barrier()
        es_pro.close()

        def mod(l, j, t, i, s):
            c = (j * 3 + t) * 8 + i
            return mods[:, l, c, s:s + 1]

        RING = 4
        ring = [sb([128, 4096], BF16, f"ring{i}") for i in range(RING)]
        rcount = [0]

        def wload(key, ncols=4096):
            i = rcount[0] % RING
            rcount[0] += 1
            em.dma('sp', f'wr{i}_{em.depoch}', ring[i][:, 0:ncols], wq[UNIT[key], :, 0:ncols], reads=['wq0', 'wq1'], writes=[f'ring{i}'])
            return ring[i], f'ring{i}'

        ptab = sb([128, 2, 2, 128], F32, "ptab")
        ctab = sb([128, 2, 2, 64], F32, "ctab")
        frq = sb([128, 2], F32, "frq")
        ang = sb([128, 128], F32, "ang")
        angi = sb([128, 128], mybir.dt.int32, "angi")
        angf = sb([128, 128], F32, "angf")
        for c in range(2):
            em.op('dve', lambda e: e.tensor_scalar(out=frq[:, c:c + 1], in0=pp('pidx'), scalar1=float(c * 128),
                                                   scalar2=None, op0=ALU.add), ['prm'], ['frq'])
        em.op('act', lambda e: e.activation(out=frq[:], in_=frq[:], func=AF.Exp, scale=-math.log(10000.0) / 256.0),
              ['frq'], ['frq'])

        def sincos(dst, idx_ap, n, c, phase):
            em.op('dve', lambda e: e.tensor_scalar(out=ang[:, 0:n], in0=idx_ap, scalar1=frq[:, c:c + 1],
                                                   scalar2=1.0 / (2 * math.pi), op0=ALU.mult, op1=ALU.mult),
                  ['prm', 'frq'], ['ang'])
            if phase:
                em.op('dve', lambda e: e.tensor_scalar(out=ang[:, 0:n], in0=ang[:, 0:n], scalar1=phase,
                                                       scalar2=None, op0=ALU.add), ['ang'], ['ang'])
            em.op('dve', lambda e: e.tensor_copy(out=angi[:, 0:n], in_=ang[:, 0:n]), ['ang'], ['angi'])
            em.op('dve', lambda e: e.tensor_copy(out=angf[:, 0:n], in_=angi[:, 0:n]), ['angi'], ['angf'])
            em.op('dve', lambda e: e.tensor_tensor(out=ang[:, 0:n], in0=ang[:, 0:n], in1=angf[:, 0:n],
                                                   op=ALU.subtract), ['ang', 'angf'], ['ang'])
            em.op('dve', lambda e: e.tensor_scalar(out=angf[:, 0:n], in0=ang[:, 0:n], scalar1=0.5, scalar2=None,
                                                   op0=ALU.is_gt), ['ang'], ['angf'])
            em.op('dve', lambda e: e.tensor_tensor(out=ang[:, 0:n], in0=ang[:, 0:n], in1=angf[:, 0:n],
                                                   op=ALU.subtract), ['ang', 'angf'], ['ang'])
            em.op('dve', lambda e: e.tensor_scalar(out=angf[:, 0:n], in0=ang[:, 0:n], scalar1=-0.5, scalar2=None,
                                                   op0=ALU.is_lt), ['ang'], ['angf'])
            em.op('dve', lambda e: e.tensor_tensor(out=ang[:, 0:n], in0=ang[:, 0:n], in1=angf[:, 0:n],
                                                   op=ALU.add), ['ang', 'angf'], ['ang'])
            em.op('act', lambda e: e.activation(out=dst, in_=ang[:, 0:n], func=AF.Sin, scale=2 * math.pi),
                  ['ang'], ['ptab'])

        for c in range(2):
            sincos(ptab[:, c, 0, :], pp('ridx'), 128, c, 0.0)
            sincos(ptab[:, c, 1, :], pp('ridx'), 128, c, 0.25)
            sincos(ctab[:, c, 0, :], pp('cidx'), 64, c, 0.0)
            sincos(ctab[:, c, 1, :], pp('cidx'), 64, c, 0.25)

        XT = sb([128, 8, TMH], F32, "XT")
        XTB = sb([128, 8, TMH], F32, "XTb")
        XBUF = [(XT, 'XT'), (XTB, 'XTb')]
        XA = sb([128, 8, TMH], F32, "XA")
        HT = sb([128, 8, TMH], BF16, "HT")
        G = sb([128, FC, TF], BF16, "G")
        R = sb([128, 8, TF], F32, "R")
        SQ = sb([128, 8, TF], F32, "SQ")
        T1 = sb([128, 512], F32, "T1")
        T2 = sb([128, 512], F32, "T2")
        MEAN = sb([128, 512], F32, "MEAN")
        RSTD = sb([128, 512], F32, "RSTD")

        def layer_norm_T(T, gname, bname, gi, dst, key_dst, nch=8, src=R, src_key='R', silu=False, banks=(6, 7), tmp=None):
            nf = float(nch * 128)
            TT_, ttk = tmp if tmp is not None else (T1, 'T1')
            BK7 = ['ps7', 'ps7r0', 'ps7r1', 'ps7r2', 'ps7c']
            pa, pb_ = PS[banks[0]], PS[banks[1]]
            ka = BK7 if banks[0] == 7 else [f'ps{banks[0]}']
            kb = BK7 if banks[1] == 7 else [f'ps{banks[1]}']
            em.op('act', lambda e: e.activation(out=SQ[:, 0:nch, 0:T], in_=src[:, 0:nch, 0:T], func=AF.Square),
                  [src_key], ['SQ'])
            for i in range(nch):
                em.op('pe', lambda e: e.matmul(pa[:, 0:T], lhsT=ones, rhs=src[:, i, 0:T], start=(i == 0),
                                               stop=(i == nch - 1)), [src_key, 'prm'], ka, inc=(i == nch - 1))
            for i in range(nch):
                em.op('pe', lambda e: e.matmul(pb_[:, 0:T], lhsT=ones, rhs=SQ[:, i, 0:T], start=(i == 0),
                                               stop=(i == nch - 1)), ['SQ', 'prm'], kb, inc=(i == nch - 1))
            em.op('act', lambda e: e.activation(out=MEAN[:, 0:T], in_=pa[:, 0:T], func=AF.Copy, scale=1.0 / nf),
                  ka, ['MEAN'])
            em.op('dve', lambda e: e.tensor_tensor(out=TT_[:, 0:T], in0=MEAN[:, 0:T], in1=MEAN[:, 0:T], op=ALU.mult),
                  ['MEAN'], [ttk])
            em.op('dve', lambda e: e.scalar_tensor_tensor(out=TT_[:, 0:T], in0=pb_[:, 0:T], scalar=1.0 / nf,
                                                          in1=TT_[:, 0:T], op0=ALU.mult, op1=ALU.subtract),
                  kb + [ttk], [ttk])
            em.op('act', lambda e: e.activation(out=TT_[:, 0:T], in_=TT_[:, 0:T], func=AF.Sqrt, bias=LN_EPS),
                  [ttk], [ttk])
            em.op('dve', lambda e: e.reciprocal(out=RSTD[:, 0:T], in_=TT_[:, 0:T]), [ttk], ['RSTD'])
            for i in range(nch):
                em.op('dve', lambda e: e.tensor_tensor(out=SQ[:, i, 0:T], in0=src[:, i, 0:T], in1=MEAN[:, 0:T],
                                                       op=ALU.subtract), [src_key, 'MEAN'], ['SQ'])
                em.op('dve', lambda e: e.tensor_tensor(out=SQ[:, i, 0:T], in0=SQ[:, i, 0:T], in1=RSTD[:, 0:T],
                                                       op=ALU.mult), ['SQ', 'RSTD'], ['SQ'])
                em.op('act', lambda e: e.activation(out=dst(i), in_=SQ[:, i, 0:T],
                                                    func=AF.Silu if silu else AF.Identity,
                                                    scale=pp(gname, gi + i), bias=pp(bname, gi + i)),
                      ['SQ', 'prm'], [key_dst])

        def modulate(l, j, s, T, c0=0):
            for i in range(8):
                em.op('dve', lambda e: e.tensor_scalar(out=HT[:, i, 0:T], in0=XT[:, i, 0:T], scalar1=mod(l, j, 1, i, s),
                                                       scalar2=mod(l, j, 0, i, s), op0=ALU.mult, op1=ALU.add),
                      ['XT', 'mods'], ['HT'])
            em.op('pool', lambda e: e.tensor_scalar(out=XA[:, :, 0:T - 2 * c0], in0=XT[:, :, c0:T - c0], scalar1=ALPHA,
                                                    scalar2=0.0, op0=ALU.mult, op1=ALU.add), ['XT'], ['XA'])

        def ffn(l, f, s, T):
            j = 0 if f == 0 else 2
            modulate(l, j, s, T)
            for uu in range(11):
                W, wk = wload(('fi', l, f, uu))
                for q in range(2):
                    jj = uu * 2 + q
                    p1, p2 = PS[(jj % 2) * 2], PS[(jj % 2) * 2 + 1]
                    k1, k2 = f'ps{(jj % 2) * 2}', f'ps{(jj % 2) * 2 + 1}'
                    for kc in range(8):
                        em.op('pe', lambda e: e.matmul(p1[:, 0:T], lhsT=W[:, kc * 256 + q * 128:kc * 256 + (q + 1) * 128],
                                                       rhs=HT[:, kc, 0:T], start=(kc == 0), stop=(kc == 7)),
                              [wk, 'HT'], [k1], inc=(kc == 7))
                    for kc in range(8):
                        em.op('pe', lambda e: e.matmul(p2[:, 0:T],
                                                       lhsT=W[:, 2048 + kc * 256 + q * 128:2048 + kc * 256 + (q + 1) * 128],
                                                       rhs=HT[:, kc, 0:T], start=(kc == 0), stop=(kc == 7)),
                              [wk, 'HT'], [k2], inc=(kc == 7))
                    tt, tk = (T1, 'T1') if jj % 2 == 0 else (T2, 'T2')
                    em.op('act', lambda e: e.activation(out=tt[:, 0:T], in_=p1[:, 0:T], func=AF.Silu), [k1], [tk])
                    em.op('dve', lambda e: e.tensor_tensor(out=G[:, jj, 0:T], in0=tt[:, 0:T], in1=p2[:, 0:T],
                                                           op=ALU.mult), [tk, k2], ['G'])
            for i in range(8):
                W, wk = wload(('fo', l, f, i), 22 * 128)
                p, k = PS[4 + (i % 2)], f'ps{4 + (i % 2)}'
                for jj in range(FC):
                    em.op('pe', lambda e: e.matmul(p[:, 0:T], lhsT=W[:, jj * 128:(jj + 1) * 128], rhs=G[:, jj, 0:T],
                                                   start=(jj == 0), stop=(jj == FC - 1)), [wk, 'G'], [k],
                          inc=(jj == FC - 1))
                em.op('dve', lambda e: e.scalar_tensor_tensor(out=R[:, i, 0:T], in0=p[:, 0:T], scalar=mod(l, j, 2, i, s),
                                                              in1=XA[:, i, 0:T], op0=ALU.mult, op1=ALU.add),
                      [k, 'mods', 'XA'], ['R'])
            layer_norm_T(T, f'ln_g{l}', f'ln_b{l}', j * 8, lambda i: XT[:, i, 0:T], 'XT')

        CST = sb([128, 8, TMH], F32, "CST")
        CACC = sb([128, 4, TM], F32, "CACC")
        CA = sb([128, 4, TMH], F32, "CA")
        QT = sb([128, 4, TM], BF16, "QT")
        KTb = sb([128, 4, TM], BF16, "KTb")
        KTf = sb([128, 4, TM], F32, "KTf")
        UT = sb([128, 4, TM], F32, "UT")
        SO = sb([128, 4, TM], F32, "SO")
        VN = sb([128, 2, 512], BF16, "VN")
        VT = sb([128, 2, 512], F32, "VT")
        VE = sb([128, 2, 4, 130], BF16, "VE")
        YB = sb([128, 4, 4, TM], BF16, "YB")
        MB = sb([128, 8, TM], BF16, "MB")
        GT = sb([128, 16], F32, "GT")
        EG = sb([128, 16], F32, "EG")
        LF = sb([128, 8], F32, "LF")
        ARG = sb([128, 8, 4], F32, "ARG")
        EX = sb([128, 8, 4], F32, "EX")
        SF = sb([128, 4, 128], BF16, "SF")
        SB_ = sb([128, 4, 128], BF16, "SBk")
        KW = sb([128, 4, 128], BF16, "KW")
        CF32 = sb([128, 4, 130], F32, "CF32")
        CFB = sb([128, 4, 130], BF16, "CFB")
        CB32 = sb([128, 4, 130], F32, "CB32")
        CBB = sb([128, 4, 130], BF16, "CBB")
        CBL = [sb([128, 4, 130], BF16, f"CBL{i}") for i in range(2)]
        HS = sb([128, 4, 128], F32, "HS")
        HN = sb([128, 4, 128], F32, "HN")
        BST = sb([128, 4, 6], F32, "BST")
        MV = sb([128, 4, 2], F32, "MV")
        SM = sb([128, 16], F32, "SM")
        WST = sb([128, 4, 128], BF16, "WST")
        PWB = sb([128, 4, 128], BF16, "PWB")
        PT = [sb([128, TMH], F32, f"PT{i}") for i in range(3)]
        em.op('pool', lambda e: e.memset(VE[:], 1.0), [], ['VE'])
        for i_ in range(3):
            em.op('pool', lambda e: e.memset(PT[i_][:], 0.0), [], [f'PT{i_}'])

        def inproj_fm(l, ukey, T, cols, bias_c0, evac, after_group=None):
            W, wk = wload(ukey)
            for q in range(4):
                p, k = PS[q % 3], f'ps{q % 3}'
                for kc in range(8):
                    em.op('pe', lambda e: e.matmul(p[:, 0:T], lhsT=W[:, kc * 512 + q * 128:kc * 512 + (q + 1) * 128],
                                                   rhs=HT[:, kc, cols[0]:cols[1]], start=(kc == 0), stop=(kc == 7)),
                          [wk, 'HT'], [k], inc=(kc == 7))
                evac(q, p, k, pp(f'b_fm{l}', bias_c0 + q))
                if after_group:
                    after_group()

        def inproj_tm(l, ukey, ncols, c, evac, wcols=512):
            W, wk = ukey
            p, k = PS[c % 2], f'ps{c % 2}'
            t0 = HALO + c * 128
            for kc in range(8):
                em.op('pe', lambda e: e.matmul(p[:, 0:ncols], lhsT=HT[:, kc, t0:t0 + 128],
                                               rhs=W[:, kc * wcols:kc * wcols + ncols], start=(kc == 0), stop=(kc == 7)),
                      [wk, 'HT'], [k], inc=(kc == 7))
            evac(p, k)

        def zero_halo(buf, key, q, first, lastt):
            if first:
                em.op('pool', lambda e: e.memset(buf[:, q, 0:HALO], 0.0), [], [key])
            if lastt:
                em.op('pool', lambda e: e.memset(buf[:, q, HALO + TM:TMH], 0.0), [], [key])

        def load_tile_halo(src, ntok, ti, X=None):
            Xb, xk = X if X is not None else XBUF[0]
            t0 = ti * TM
            lo = max(t0 - HALO, 0)
            hi = min(t0 + TM + HALO, ntok)
            if lo > t0 - HALO:
                em.op('pool', lambda e: e.memset(Xb[:, :, 0:HALO], 0.0), [], [xk])
            if hi < t0 + TM + HALO:
                em.op('pool', lambda e: e.memset(Xb[:, :, HALO + TM:TMH], 0.0), [], [xk])
            em.dma('pool', 'ldx', Xb[:, :, lo - (t0 - HALO):hi - (t0 - HALO)],
                   src[:, lo:hi].rearrange("(c p) t -> p c t", p=128), reads=['xs'], writes=[xk])

        def mod_ht(l, j, s, T, X):
            Xb, xk = X
            for i in range(8):
                em.op('dve', lambda e: e.tensor_scalar(out=HT[:, i, 0:T], in0=Xb[:, i, 0:T], scalar1=mod(l, j, 1, i, s),
                                                       scalar2=mod(l, j, 0, i, s), op0=ALU.mult, op1=ALU.add),
                      [xk, 'mods'], ['HT'])

        def mod_xa(T, c0, X):
            Xb, xk = X
            em.op('pool', lambda e: e.tensor_scalar(out=XA[:, :, 0:T - 2 * c0], in0=Xb[:, :, c0:T - c0], scalar1=ALPHA,
                                                    scalar2=0.0, op0=ALU.mult, op1=ALU.add), [xk], ['XA'])

        def gate_scalars(l, c, W, wk):
            def ev(p, k):
                em.op('dve', lambda e: e.tensor_tensor(out=GT[:], in0=p[:, 0:16], in1=pp(f'b_gt{l}'), op=ALU.add),
                      [k, 'prm'], ['GT'])
            inproj_tm(l, (W, wk), 16, c, ev, wcols=16)
            em.op('act', lambda e: e.activation(out=EG[:], in_=GT[:], func=AF.Exp, scale=-1.0), ['GT'], ['EG'])
            em.op('act', lambda e: e.activation(out=EG[:], in_=EG[:], func=AF.Ln, bias=1.0), ['EG'], ['EG'])
            em.op('dve', lambda e: e.tensor_scalar(out=LF[:, 0:4], in0=EG[:, 4:8], scalar1=-1.0, scalar2=None,
                                                   op0=ALU.mult), ['EG'], ['LF'])
            em.op('dve', lambda e: e.tensor_scalar(out=LF[:, 4:8], in0=EG[:, 12:16], scalar1=-1.0, scalar2=None,
                                                   op0=ALU.mult), ['EG'], ['LF'])
            pc = PS[7]
            for n, m in enumerate([triu, tril, ones]):
                em.op('pe', lambda e: e.matmul(pc[:, 400 + n * 8:400 + n * 8 + 8], lhsT=m, rhs=LF[:], start=True,
                                               stop=True), ['LF', 'prm'], ['ps7c'])
            bf, bb = pc[:, 400:404], pc[:, 412:416]
            totf, totb = pc[:, 416:420], pc[:, 420:424]
            em.op('dve', lambda e: e.scalar_tensor_tensor(out=ARG[:, 0, :], in0=GT[:, 0:4], scalar=LNS, in1=bf,
                                                          op0=ALU.add, op1=ALU.subtract), ['GT', 'ps7c'], ['ARG'])
            em.op('dve', lambda e: e.scalar_tensor_tensor(out=ARG[:, 1, :], in0=GT[:, 8:12], scalar=LNS, in1=bb,
                                                          op0=ALU.add, op1=ALU.subtract), ['GT', 'ps7c'], ['ARG'])
            em.op('dve', lambda e: e.tensor_copy(out=ARG[:, 2, :], in_=bf), ['ps7c'], ['ARG'])
            em.op('dve', lambda e: e.tensor_copy(out=ARG[:, 3, :], in_=bb), ['ps7c'], ['ARG'])
            em.op('dve', lambda e: e.tensor_tensor(out=ARG[:, 4, :], in0=ARG[:, 0, :], in1=totf, op=ALU.add),
                  ['ARG', 'ps7c'], ['ARG'])
            em.op('dve', lambda e: e.tensor_tensor(out=ARG[:, 5, :], in0=ARG[:, 1, :], in1=totb, op=ALU.add),
                  ['ARG', 'ps7c'], ['ARG'])
            em.op('dve', lambda e: e.tensor_copy(out=ARG[:, 6, :], in_=totf), ['ps7c'], ['ARG'])
            em.op('dve', lambda e: e.tensor_copy(out=ARG[:, 7, :], in_=totb), ['ps7c'], ['ARG'])
            em.op('act', lambda e: e.activation(out=EX[:], in_=ARG[:], func=AF.Exp), ['ARG'], ['EX'])

        def qk_conv(l, q, qi, dst_list):
            w = lambda j: pp(f'qkw{l}', qi * 3 + j)
            em.op('dve', lambda e: e.tensor_scalar(out=T1[:, 0:TM], in0=CST[:, q, HALO - 1:HALO - 1 + TM], scalar1=w(0),
                                                   scalar2=None, op0=ALU.mult), ['CST', 'prm'], ['T1'])
            em.op('dve', lambda e: e.scalar_tensor_tensor(out=T1[:, 0:TM], in0=CST[:, q, HALO:HALO + TM], scalar=w(1),
                                                          in1=T1[:, 0:TM], op0=ALU.mult, op1=ALU.add),
                  ['CST', 'prm', 'T1'], ['T1'])
            em.op('dve', lambda e: e.scalar_tensor_tensor(out=T1[:, 0:TM], in0=CST[:, q, HALO + 1:HALO + 1 + TM],
                                                          scalar=w(2), in1=T1[:, 0:TM], op0=ALU.mult, op1=ALU.add),
                  ['CST', 'prm', 'T1'], ['T1'])
            for dst, key in dst_list:
                em.op('act', lambda e: e.activation(out=dst, in_=T1[:, 0:TM], func=AF.Silu), ['T1'], [key])

        def evac_bias(buf, key, qoff, first, lastt):
            def ev(q, p, k, b):
                em.op('act', lambda e: e.activation(out=buf[:, qoff + q, 0:TMH], in_=p[:, 0:TMH], func=AF.Identity,
                                                    bias=b), [k, 'prm'], [key])
                zero_halo(buf, key, qoff + q, first, lastt)
            return ev

        def k_transposed_scaled(c, h, grp):
            em.op('pe', lambda e: e.transpose(PS[4][:, h * 128:(h + 1) * 128], KTf[:, h, c * 128:(c + 1) * 128], ident),
                  ['KTf', 'prm'], ['ps4'])
            em.op('dve', lambda e: e.tensor_scalar(out=KW[:, h, :], in0=PS[4][:, h * 128:(h + 1) * 128],
                                                   scalar1=EX[:, grp, h:h + 1], scalar2=None, op0=ALU.mult),
                  ['ps4', 'EX'], ['KW'])

        def state_update(c, h, C32, Cb, key32, keyb, grp_dec):
            reg = h % 3
            pr = PS[7][:, reg * 130:reg * 130 + 129]
            em.op('pe', lambda e: e.matmul(pr, lhsT=KW[:, h, :], rhs=VE[:, c, h, 0:129], start=True, stop=True),
                  ['KW', 'VE'], [f'ps7r{reg}'])
            em.op('dve', lambda e: e.scalar_tensor_tensor(out=C32[:, h, 0:129], in0=C32[:, h, 0:129],
                                                          scalar=EX[:, grp_dec, h:h + 1], in1=pr, op0=ALU.mult,
                                                          op1=ALU.add), [key32, 'EX', f'ps7r{reg}'], [key32])
            em.op('pool', lambda e: e.tensor_copy(out=Cb[:, h, 0:129], in_=C32[:, h, 0:129]), [key32], [keyb])

        def state_pass(l, s, src, ntok, ti, backward, save_ap):
            first, lastt = ti == 0, ti == ntok // TM - 1
            load_tile_halo(src, ntok, ti)
            modulate(l, 1, s, TMH, c0=HALO)
            inproj_fm(l, ('in', l, 3), TMH, (0, TMH), 12, evac_bias(CST, 'CST', 4, first, lastt))
            for q in range(4):
                qk_conv(l, 4 + q, 4 + q, [(KTf[:, q, :], 'KTf')])
            Wg, wgk = wload(('in', l, 'gt'), 128)
            cs_order = [1, 0] if backward else [0, 1]
            Wv_ = wload(('in', l, 8))
            for c in cs_order:
                W, wk = Wv_

                def ev(p, k, c=c):
                    em.op('dve', lambda e: e.tensor_tensor(out=VE[:, c, :, 0:128],
                                                           in0=p[:, 0:512].rearrange("p (h d) -> p h d", h=4),
                                                           in1=pp(f'b_mv{l}').rearrange("p (h d) -> p h d", h=4),
                                                           op=ALU.add), [k, 'prm'], ['VE'])
                inproj_tm(l, (W, wk), 512, c, ev)
            for c in cs_order:
                gate_scalars(l, c, Wg, wgk)
                if backward:
                    gc = ti * 2 + c
                    em.dma('pool', 'stcb', save_ap[gc].rearrange("p (h d) -> p h d", h=4), CBB[:], reads=['CBB'],
                           writes=['cbscr'])
                for h in range(4):
                    k_transposed_scaled(c, h, 5 if backward else 4)
                    if backward:
                        state_update(c, h, CB32, CBB, 'CB32', 'CBB', 7)
                    else:
                        state_update(c, h, CF32, CFB, 'CF32', 'CFB', 6)

        def gelu_T(dst, src_ap, src_keys, dkey, T, tmp, tkey):
            em.op('act', lambda e: e.activation(out=tmp, in_=src_ap, func=AF.Square), src_keys, [tkey])
            em.op('dve', lambda e: e.tensor_scalar(out=tmp, in0=tmp, scalar1=0.044715, scalar2=1.0, op0=ALU.mult,
                                                   op1=ALU.add), [tkey], [tkey])
            em.op('dve', lambda e: e.tensor_tensor(out=tmp, in0=tmp, in1=src_ap, op=ALU.mult), [tkey] + src_keys,
                  [tkey])
            em.op('act', lambda e: e.activation(out=tmp, in_=tmp, func=AF.Sigmoid, scale=GK), [tkey], [tkey])
            em.op('dve', lambda e: e.tensor_tensor(out=dst, in0=tmp, in1=src_ap, op=ALU.mult), [tkey] + src_keys,
                  [dkey])

        def mixer_tile(l, s, src, dst, ntok, ti, cb_scr, want_out=True, X=None, skip_pre=False, next_pre=None,
                       next_xa=None):
            first, lastt = ti == 0, ti == ntok // TM - 1
            X = X if X is not None else XBUF[0]
            Xb, xk = X
            if not skip_pre:
                load_tile_halo(src, ntok, ti, X)
                mod_ht(l, 1, s, TMH, X)
                mod_xa(TMH, HALO, X)
            for c in range(2):
                gc = ti * 2 + c
                em.dma('pool', f'ldcb{c}', CBL[c][:], cb_scr[gc].rearrange("p (h d) -> p h d", h=4), reads=['cbscr'],
                       writes=[f'CBL{c}'])
            inproj_fm(l, ('in', l, 0), TMH, (0, TMH), 0, evac_bias(CA, 'CA', 0, False, False))

            def ev_glu(q, p, k, b):
                em.op('act', lambda e: e.activation(out=PT[0][:, 0:TMH], in_=p[:, 0:TMH], func=AF.Sigmoid, bias=b),
                      [k, 'prm'], ['PT0'])
                em.op('dve', lambda e: e.tensor_tensor(out=CA[:, q, 0:TMH], in0=CA[:, q, 0:TMH], in1=PT[0][:, 0:TMH],
                                                       op=ALU.mult), ['CA', 'PT0'], ['CA'])
                zero_halo(CA, 'CA', q, first, lastt)
            inproj_fm(l, ('in', l, 1), TMH, (0, TMH), 4, ev_glu)
            conv_thunks = []
            for q in range(4):
                def t0_(q=q):
                    em.op('dve', lambda e: e.tensor_scalar(out=CACC[:, q, :], in0=CA[:, q, 1:1 + TM],
                                                           scalar1=pp(f'conv_w{l}', q * 31),
                                                           scalar2=pp(f'conv_b{l}', q), op0=ALU.mult, op1=ALU.add),
                          ['CA', 'prm'], ['CACC'])
                conv_thunks.append(t0_)
                for j in range(1, 31):
                    def tj_(q=q, j=j):
                        em.op('dve', lambda e: e.scalar_tensor_tensor(out=CACC[:, q, :], in0=CA[:, q, 1 + j:1 + j + TM],
                                                                      scalar=pp(f'conv_w{l}', q * 31 + j),
                                                                      in1=CACC[:, q, :], op0=ALU.mult, op1=ALU.add),
                              ['CA', 'prm', 'CACC'], ['CACC'])
                    conv_thunks.append(tj_)

            def drain2():
                for _ in range(2):
                    if conv_thunks:
                        conv_thunks.pop(0)()
            inproj_fm(l, ('in', l, 2), TMH, (0, TMH), 8, evac_bias(CST, 'CST', 0, first, lastt), after_group=drain2)
            inproj_fm(l, ('in', l, 3), TMH, (0, TMH), 12, evac_bias(CST, 'CST', 4, first, lastt), after_group=drain2)
            for q in range(4):
                qk_conv(l, q, q, [(QT[:, q, :], 'QT')])
            for q in range(4):
                qk_conv(l, 4 + q, 4 + q, [(KTf[:, q, :], 'KTf'), (KTb[:, q, :], 'KTb')])
            inproj_fm(l, ('in', l, 4), TMH, (0, TMH), 16, evac_bias(CST, 'CST', 0, first, lastt), after_group=drain2)
            em.op('pool', lambda e: e.tensor_copy(out=PWB[:], in_=pp(f'pool_w{l}').rearrange("p (g e) -> p g e", g=4)),
                  ['prm'], ['PWB'])
            for g, win in enumerate((2, 4, 8, 16)):
                lo, hi = win // 2, win - 1 - win // 2
                cur, ck = CST[:, g, :], 'CST'
                k = 1
                nb = 0
                W0 = TMH
                while k < win:
                    nxt = PT[nb % 2]
                    nk = f'PT{nb % 2}'
                    cin_, cink = cur, ck
                    em.op('pool', lambda e: e.tensor_tensor(out=nxt[:, k:W0], in0=cin_[:, k:W0], in1=cin_[:, 0:W0 - k],
                                                            op=ALU.add), [cink], [nk])
                    cur, ck = nxt, nk
                    k *= 2
                    nb += 1
                a0 = HALO + hi
                ic = PT[2]
                em.op('pool', lambda e: e.memset(ic[:, 0:TM], 1.0 / win), [], ['PT2'])
                if first:
                    for t in range(lo):
                        em.op('pool', lambda e: e.memset(ic[:, t:t + 1], 1.0 / (t + hi + 1)), [], ['PT2'])
                if lastt:
                    for t in range(TM - hi, TM):
                        em.op('pool', lambda e: e.memset(ic[:, t:t + 1], 1.0 / (TM - t + lo)), [], ['PT2'])
                em.op('dve', lambda e: e.tensor_tensor(out=T1[:, 0:TM], in0=cur[:, a0:a0 + TM], in1=ic[:, 0:TM],
                                                       op=ALU.mult), [ck, 'PT2'], ['T1'])
                em.op('dve', lambda e: e.tensor_tensor(out=MB[:, g, :], in0=T1[:, 0:TM], in1=CST[:, g, HALO:HALO + TM],
                                                       op=ALU.subtract), ['T1', 'CST'], ['MB'])
                em.op('pe', lambda e: e.matmul(PS[3][:, 0:TM], lhsT=PWB[:, g, :], rhs=MB[:, g, :], start=True, stop=True),
                      ['PWB', 'MB'], ['ps3'])
                em.op('act', lambda e: e.activation(out=YB[:, 3, g, :], in_=PS[3][:, 0:TM], func=AF.Copy,
                                                    scale=pp(f'pool_s{l}', g)), ['ps3', 'prm'], ['YB'])
            def ev_u(q, p, k, b):
                em.op('act', lambda e: e.activation(out=PT[1][:, 0:TM], in_=p[:, 0:TM], func=AF.Identity, bias=b),
                      [k, 'prm'], ['PT1'])
                gelu_T(UT[:, q, :], PT[1][:, 0:TM], ['PT1'], 'UT', TM, PT[0][:, 0:TM], 'PT0')
            inproj_fm(l, ('in', l, 5), TM, (HALO, HALO + TM), 20, ev_u, after_group=drain2)

            def ev_o(q, p, k, b):
                em.op('act', lambda e: e.activation(out=SO[:, q, :], in_=p[:, 0:TM], func=AF.Sigmoid, bias=b),
                      [k, 'prm'], ['SO'])
            inproj_fm(l, ('in', l, 6), TM, (HALO, HALO + TM), 24, ev_o, after_group=drain2)
            em.op('pool', lambda e: e.tensor_copy(out=WST[:], in_=pp(f'wsT{l}').rearrange("p (g t) -> p g t", g=4)),
                  ['prm'], ['WST'])
            Wv = wload(('in', l, 7))
            for c in range(2):
                def ev_v(p, k, c=c):
                    em.op('dve', lambda e: e.tensor_tensor(out=VT[:, c, :], in0=p[:, 0:512], in1=pp(f'b_gv{l}'),
                                                           op=ALU.add), [k, 'prm'], ['VT'])
                inproj_tm(l, Wv, 512, c, ev_v)
                gelu_T(VT[:, c, :], VT[:, c, :], ['VT'], 'VT', 512, T2[:, 0:512], 'T2')
                em.op('dve', lambda e: e.bn_stats(out=BST[:, 0, :], in_=VT[:, c, :]), ['VT'], ['BST'])
                em.op('dve', lambda e: e.bn_aggr(out=MV[:, 0, :], in_=BST[:, 0, :]), ['BST'], ['MV'])
                em.op('act', lambda e: e.activation(out=SM[:, 0:1], in_=MV[:, 0, 1:2], func=AF.Sqrt, bias=LN_EPS),
                      ['MV'], ['SM'])
                em.op('dve', lambda e: e.reciprocal(out=SM[:, 0:1], in_=SM[:, 0:1]), ['SM'], ['SM'])
                em.op('dve', lambda e: e.tensor_scalar(out=VT[:, c, :], in0=VT[:, c, :], scalar1=MV[:, 0, 0:1],
                                                       scalar2=SM[:, 0:1], op0=ALU.subtract, op1=ALU.mult),
                      ['VT', 'MV', 'SM'], ['VT'])
                em.op('pool', lambda e: e.tensor_tensor(out=VT[:, c, :], in0=VT[:, c, :], in1=pp(f'gln_g{l}'),
                                                        op=ALU.mult), ['VT', 'prm'], ['VT'])
                em.op('pool', lambda e: e.tensor_tensor(out=VN[:, c, :], in0=VT[:, c, :], in1=pp(f'gln_b{l}'),
                                                        op=ALU.add), ['VT', 'prm'], ['VN'])
            for g in range(4):
                for c in range(2):
                    em.op('pe', lambda e: e.matmul(PS[3][:, c * 128:(c + 1) * 128], lhsT=VN[:, c, g * 128:(g + 1) * 128],
                                                   rhs=WST[:, g, :], start=True, stop=True), ['VN', 'WST'], ['ps3'])
                bsg = pp(f'bs{l}', g * 128, (g + 1) * 128)
                for c in range(2):
                    em.op('dve', lambda e: e.tensor_tensor(out=T1[:, c * 128:(c + 1) * 128],
                                                           in0=PS[3][:, c * 128:(c + 1) * 128], in1=bsg, op=ALU.add),
                          ['ps3', 'prm'], ['T1'])
                em.op('dve', lambda e: e.tensor_tensor(out=YB[:, 0, g, :], in0=T1[:, 0:TM], in1=UT[:, g, :], op=ALU.mult),
                      ['T1', 'UT'], ['YB'])
            Wmv = wload(('in', l, 8))
            for c in range(2):
                def ev_mv(p, k, c=c):
                    em.op('dve', lambda e: e.tensor_tensor(out=VE[:, c, :, 0:128],
                                                           in0=p[:, 0:512].rearrange("p (h d) -> p h d", h=4),
                                                           in1=pp(f'b_mv{l}').rearrange("p (h d) -> p h d", h=4),
                                                           op=ALU.add), [k, 'prm'], ['VE'])
                inproj_tm(l, Wmv, 512, c, ev_mv)
            Wg, wgk = wload(('in', l, 'gt'), 128)
            for c in range(2):
                gate_scalars(l, c, Wg, wgk)
                cs_ = slice(c * 128, (c + 1) * 128)
                for h in range(4):
                    em.op('pe', lambda e: e.matmul(PS[3][:, h * 128:(h + 1) * 128], lhsT=KTb[:, h, cs_], rhs=QT[:, h, cs_],
                                                   start=True, stop=True), ['KTb', 'QT'], ['ps3'])
                for h in range(4):
                    em.op('dve', lambda e: e.scalar_tensor_tensor(out=SF[:, h, :], in0=PS[3][:, h * 128:(h + 1) * 128],
                                                                  scalar=EX[:, 0, h:h + 1], in1=triu, op0=ALU.mult,
                                                                  op1=ALU.mult), ['ps3', 'EX', 'prm'], ['SF'])
                    em.op('dve', lambda e: e.scalar_tensor_tensor(out=SB_[:, h, :], in0=PS[3][:, h * 128:(h + 1) * 128],
                                                                  scalar=EX[:, 1, h:h + 1], in1=tril, op0=ALU.mult,
                                                                  op1=ALU.mult), ['ps3', 'EX', 'prm'], ['SBk'])
                for hp in range(2):
                    for (Sx, sk, Cx, ckey, pb, pk_) in ((SF, 'SF', CFB, 'CFB', PS[5], 'ps5'),
                                                        (SB_, 'SBk', CBL[c], f'CBL{c}', PS[6], 'ps6')):
                        for hh in range(2):
                            h = hp * 2 + hh
                            o_ = pb[:, hh * 130:hh * 130 + 129]
                            em.op('pe', lambda e: e.matmul(o_, lhsT=Sx[:, h, :], rhs=VE[:, c, h, 0:129], start=True,
                                                           stop=False), [sk, 'VE'], [pk_], inc=False)
                            em.op('pe', lambda e: e.matmul(o_, lhsT=QT[:, h, cs_], rhs=Cx[:, h, 0:129], start=False,
                                                           stop=True), ['QT', ckey], [pk_])
                    for d_, (pb, pk_) in enumerate(((PS[5], 'ps5'), (PS[6], 'ps6'))):
                        den = pb[:, 0:260].rearrange("p (h d) -> p h d", h=2)[:, :, 128]
                        eb = EX[:, 2 + d_, hp * 2:hp * 2 + 2]
                        sm = SM[:, 4 + d_ * 2:6 + d_ * 2]
                        em.op('act', lambda e: e.activation(out=sm, in_=den, func=AF.Abs), [pk_], ['SM'])
                        em.op('dve', lambda e: e.tensor_tensor(out=sm, in0=sm, in1=eb, op=ALU.mult), ['SM', 'EX'], ['SM'])
                        em.op('dve', lambda e: e.tensor_scalar(out=sm, in0=sm, scalar1=1.0, scalar2=None, op0=ALU.max),
                              ['SM'], ['SM'])
                        em.op('dve', lambda e: e.reciprocal(out=sm, in_=sm), ['SM'], ['SM'])
                        em.op('dve', lambda e: e.tensor_tensor(out=sm, in0=sm, in1=eb, op=ALU.mult), ['SM', 'EX'], ['SM'])
                    for hh in range(2):
                        h = hp * 2 + hh
                        em.op('act', lambda e: e.activation(out=HS[:, h, :], in_=PS[5][:, hh * 130:hh * 130 + 128],
                                                            func=AF.Copy, scale=SM[:, 4 + hh:5 + hh]), ['ps5', 'SM'],
                              ['HS'])
                        em.op('dve', lambda e: e.scalar_tensor_tensor(out=HS[:, h, :],
                                                                      in0=PS[6][:, hh * 130:hh * 130 + 128],
                                                                      scalar=SM[:, 6 + hh:7 + hh], in1=HS[:, h, :],
                                                                      op0=ALU.mult, op1=ALU.add),
                              ['ps6', 'SM', 'HS'], ['HS'])
                for h in range(4):
                    em.op('dve', lambda e: e.bn_stats(out=BST[:, h, :], in_=HS[:, h, :]), ['HS'], ['BST'])
                    em.op('dve', lambda e: e.bn_aggr(out=MV[:, h, :], in_=BST[:, h, :]), ['BST'], ['MV'])
                em.op('act', lambda e: e.activation(out=SM[:, 8:12], in_=MV[:, :, 1], func=AF.Sqrt, bias=LN_EPS),
                      ['MV'], ['SM'])
                em.op('dve', lambda e: e.reciprocal(out=SM[:, 8:12], in_=SM[:, 8:12]), ['SM'], ['SM'])
                for h in range(4):
                    em.op('dve', lambda e: e.tensor_scalar(out=HN[:, h, :], in0=HS[:, h, :], scalar1=MV[:, h, 0:1],
                                                           scalar2=SM[:, 8 + h:9 + h], op0=ALU.subtract, op1=ALU.mult),
                          ['HS', 'MV', 'SM'], ['HN'])
                    em.op('pe', lambda e: e.transpose(PS[4][:, h * 128:(h + 1) * 128], HN[:, h, :], ident),
                          ['HN', 'prm'], ['ps4'])
                    em.op('dve', lambda e: e.scalar_tensor_tensor(out=YB[:, 2, h, cs_], in0=PS[4][:, h * 128:(h + 1) * 128],
                                                                  scalar=pp(f'mln_g{l}', h), in1=SO[:, h, cs_],
                                                                  op0=ALU.mult, op1=ALU.mult),
                          ['ps4', 'prm', 'SO'], ['YB'])
                for h in range(4):
                    k_transposed_scaled(c, h, 4)
                    state_update(c, h, CF32, CFB, 'CF32', 'CFB', 6)
            if not want_out:
                return
            if not want_out:
                for th in conv_thunks:
                    pass
            for bi, b in enumerate([0, 2, 3, 1]):
                if b == 1:
                    while conv_thunks:
                        conv_thunks.pop(0)()
                    layer_norm_T(TM, f'cln_g{l}', f'cln_b{l}', 0, lambda i: YB[:, 1, i, :], 'YB', nch=4, src=CACC,
                                 src_key='CACC', silu=True, banks=(3, 4))
                Wb, wbk = wload(('br', l, b))
                Wgs = [None, None]
                for jj in range(8):
                    if jj % 4 == 0:
                        Wgs = wload(('in', l, 9 + b * 2 + jj // 4))
                    Wg2, wg2k = Wgs
                    pg, pgk = PS[jj % 2], f'ps{jj % 2}'
                    ppj, ppk = (PS[2], 'ps2') if jj % 2 == 0 else (PS[5], 'ps5')
                    q = jj % 4
                    for kc in range(8):
                        em.op('pe', lambda e: e.matmul(pg[:, 0:TM], lhsT=Wg2[:, kc * 512 + q * 128:kc * 512 + (q + 1) * 128],
                                                       rhs=HT[:, kc, HALO:HALO + TM], start=(kc == 0), stop=(kc == 7)),
                              [wg2k, 'HT'], [pgk], inc=(kc == 7))
                    for kc in range(4):
                        em.op('pe', lambda e: e.matmul(ppj[:, 0:TM], lhsT=Wb[:, kc * 1024 + jj * 128:kc * 1024 + (jj + 1) * 128],
                                                       rhs=YB[:, b, kc, :], start=(kc == 0), stop=(kc == 3)),
                              [wbk, 'YB'], [ppk], inc=(kc == 3))
                    tt, tk = (PT[0], 'PT0') if jj % 2 == 0 else (PT[1], 'PT1')
                    em.op('act', lambda e: e.activation(out=tt[:, 0:TM], in_=pg[:, 0:TM], func=AF.Sigmoid,
                                                        bias=pp(f'b_fm{l}', 28 + b * 8 + jj)), [pgk, 'prm'], [tk])
                    if bi == 0:
                        em.op('dve', lambda e: e.tensor_tensor(out=R[:, jj, 0:TM], in0=tt[:, 0:TM], in1=ppj[:, 0:TM],
                                                               op=ALU.mult), [tk, ppk], ['R'])
                    else:
                        em.op('dve', lambda e: e.tensor_tensor(out=tt[:, 0:TM], in0=tt[:, 0:TM], in1=ppj[:, 0:TM],
                                                               op=ALU.mult), [tk, ppk], [tk])
                        em.op('pool', lambda e: e.tensor_tensor(out=R[:, jj, 0:TM], in0=R[:, jj, 0:TM], in1=tt[:, 0:TM],
                                                                op=ALU.add), ['R', tk], ['R'])
                    for _ in range(3):
                        if conv_thunks and b != 1:
                            conv_thunks.pop(0)()
            em.op('act', lambda e: e.activation(out=MB[:], in_=R[:, :, 0:TM], func=AF.Copy), ['R'], ['MB'])
            if next_pre:
                next_pre()
            for j2 in range(2):
                Wo, wok = wload(('wo', l, j2))
                for q in range(4):
                    i = j2 * 4 + q
                    p, k = PS[q % 2], f'ps{q % 2}'
                    for kc in range(8):
                        em.op('pe', lambda e: e.matmul(p[:, 0:TM], lhsT=Wo[:, kc * 512 + q * 128:kc * 512 + (q + 1) * 128],
                                                       rhs=MB[:, kc, :], start=(kc == 0), stop=(kc == 7)),
                              [wok, 'MB'], [k], inc=(kc == 7))
                    em.op('dve', lambda e: e.scalar_tensor_tensor(out=R[:, i, 0:TM], in0=p[:, 0:TM],
                                                                  scalar=mod(l, 1, 2, i, s), in1=XA[:, i, 0:TM],
                                                                  op0=ALU.mult, op1=ALU.add), [k, 'mods', 'XA'], ['R'])
            if next_xa:
                next_xa()
            layer_norm_T(TM, f'ln_g{l}', f'ln_b{l}', 8, lambda i: Xb[:, i, 0:TM], xk, banks=(3, 4))
            t0 = ti * TM
            em.dma('pool', 'stx', dst[:, t0:t0 + TM].rearrange("(c p) t -> p c t", p=128), Xb[:, :, 0:TM], reads=[xk],
                   writes=['xs2'])

        def ffn_stage(l_prev, l_next, s, src, dst, ntok, T, add_pos=False):
            for ti in range(ntok // T):
                t0 = ti * T
                em.dma('pool', 'ldx', XT[:, :, 0:T], src[:, t0:t0 + T].rearrange("(c p) t -> p c t", p=128),
                       reads=['xs', 'xs2'], writes=['XT'])
                if add_pos:
                    nr = T // 64
                    r0 = t0 // 64
                    for c in range(2):
                        for sc_ in range(2):
                            ch = sc_ * 2 + c
                            ch2 = 4 + sc_ * 2 + c
                            for r in range(nr):
                                em.op('dve', lambda e: e.tensor_scalar(out=XT[:, ch, r * 64:(r + 1) * 64],
                                                                       in0=XT[:, ch, r * 64:(r + 1) * 64],
                                                                       scalar1=ptab[:, c, sc_, r0 + r:r0 + r + 1],
                                                                       scalar2=None, op0=ALU.add), ['XT', 'ptab'], ['XT'])
                                em.op('pool', lambda e: e.tensor_tensor(out=XT[:, ch2, r * 64:(r + 1) * 64],
                                                                        in0=XT[:, ch2, r * 64:(r + 1) * 64],
                                                                        in1=ctab[:, c, sc_, :], op=ALU.add),
                                      ['XT', 'ptab'], ['XT'])
                if l_prev is not None:
                    ffn(l_prev, 1, s, T)
                if l_next is not None:
                    ffn(l_next, 0, s, T)
                em.dma('pool', 'stx', dst[:, t0:t0 + T].rearrange("(c p) t -> p c t", p=128), XT[:, :, 0:T], reads=['XT'],
                       writes=['xs', 'xs2'])

        XBUF = [(XT, 'XT'), (XTB, 'XTb')]

        class FJ:
            def __init__(self, l, f, s, T, X, pre=None, post=None):
                self.l, self.f, self.s, self.T, self.X, self.pre, self.post = l, f, s, T, X, pre, post
                self.j = 0 if f == 0 else 2

            def mod_(self):
                X, xk = self.X
                l, j, s, T = self.l, self.j, self.s, self.T
                for i in range(8):
                    em.op('dve', lambda e: e.tensor_scalar(out=HT[:, i, 0:T], in0=X[:, i, 0:T], scalar1=mod(l, j, 1, i, s),
                                                           scalar2=mod(l, j, 0, i, s), op0=ALU.mult, op1=ALU.add),
                          [xk, 'mods'], ['HT'])

            def xa_(self):
                X, xk = self.X
                T = self.T
                em.op('pool', lambda e: e.tensor_scalar(out=XA[:, :, 0:T], in0=X[:, :, 0:T], scalar1=ALPHA, scalar2=0.0,
                                                        op0=ALU.mult, op1=ALU.add), [xk], ['XA'])

            def win_(self):
                l, f, T = self.l, self.f, self.T
                for uu in range(11):
                    W, wk = wload(('fi', l, f, uu))
                    for q in range(2):
                        jj = uu * 2 + q
                        p1, p2 = PS[(jj % 2) * 2], PS[(jj % 2) * 2 + 1]
                        k1, k2 = f'ps{(jj % 2) * 2}', f'ps{(jj % 2) * 2 + 1}'
                        for kc in range(8):
                            em.op('pe', lambda e: e.matmul(p1[:, 0:T], lhsT=W[:, kc * 256 + q * 128:kc * 256 + (q + 1) * 128],
                                                           rhs=HT[:, kc, 0:T], start=(kc == 0), stop=(kc == 7)),
                                  [wk, 'HT'], [k1], inc=(kc == 7))
                        for kc in range(8):
                            em.op('pe', lambda e: e.matmul(p2[:, 0:T],
                                                           lhsT=W[:, 2048 + kc * 256 + q * 128:2048 + kc * 256 + (q + 1) * 128],
                                                           rhs=HT[:, kc, 0:T], start=(kc == 0), stop=(kc == 7)),
                                  [wk, 'HT'], [k2], inc=(kc == 7))
                        tt, tk = (T1, 'T1') if jj % 2 == 0 else (T2, 'T2')
                        em.op('act', lambda e: e.activation(out=tt[:, 0:T], in_=p1[:, 0:T], func=AF.Silu), [k1], [tk])
                        em.op('dve', lambda e: e.tensor_tensor(out=G[:, jj, 0:T], in0=tt[:, 0:T], in1=p2[:, 0:T],
                                                               op=ALU.mult), [tk, k2], ['G'])

            def wout_(self):
                l, f, s, T, j = self.l, self.f, self.s, self.T, self.j
                for i in range(8):
                    W, wk = wload(('fo', l, f, i), 22 * 128)
                    p, k = PS[4 + (i % 2)], f'ps{4 + (i % 2)}'
                    for jj in range(FC):
                        em.op('pe', lambda e: e.matmul(p[:, 0:T], lhsT=W[:, jj * 128:(jj + 1) * 128], rhs=G[:, jj, 0:T],
                                                       start=(jj == 0), stop=(jj == FC - 1)), [wk, 'G'], [k],
                              inc=(jj == FC - 1))
                    em.op('dve', lambda e: e.scalar_tensor_tensor(out=R[:, i, 0:T], in0=p[:, 0:T], scalar=mod(l, j, 2, i, s),
                                                                  in1=XA[:, i, 0:T], op0=ALU.mult, op1=ALU.add),
                          [k, 'mods', 'XA'], ['R'])

            def ln_(self):
                X, xk = self.X
                T = self.T
                layer_norm_T(T, f'ln_g{self.l}', f'ln_b{self.l}', self.j * 8, lambda i: X[:, i, 0:T], xk,
                             tmp=(PT[2], 'PT2'))

        def ffn_pipe(jobs):
            n = len(jobs)
            if jobs[0].pre:
                jobs[0].pre()
            jobs[0].mod_()
            jobs[0].xa_()
            for k in range(n):
                jobs[k].win_()
                same = (k + 1 < n) and (jobs[k + 1].X[1] == jobs[k].X[1])
                if k + 1 < n and not same:
                    if jobs[k + 1].pre:
                        jobs[k + 1].pre()
                    jobs[k + 1].mod_()
                jobs[k].wout_()
                if k + 1 < n and not same:
                    jobs[k + 1].xa_()
                jobs[k].ln_()
                if jobs[k].post:
                    jobs[k].post()
                if same:
                    if jobs[k + 1].pre:
                        jobs[k + 1].pre()
                    jobs[k + 1].mod_()
                    jobs[k + 1].xa_()

        def stage_jobs(ls, tiles):
            jobs = []

            def mk(tile, X):
                s_, src, dst, t0, T, add_pos = tile
                Xb, xk = X

                def pre():
                    em.dma('pool', 'ldx', Xb[:, :, 0:T], src[:, t0:t0 + T].rearrange("(c p) t -> p c t", p=128),
                           reads=['xs2'], writes=[xk])
                    if add_pos:
                        nr = T // 64
                        r0 = t0 // 64
                        for c in range(2):
                            for sc_ in range(2):
                                ch = sc_ * 2 + c
                                ch2 = 4 + sc_ * 2 + c
                                for r in range(nr):
                                    em.op('dve', lambda e: e.tensor_scalar(out=Xb[:, ch, r * 64:(r + 1) * 64],
                                                                           in0=Xb[:, ch, r * 64:(r + 1) * 64],
                                                                           scalar1=ptab[:, c, sc_, r0 + r:r0 + r + 1],
                                                                           scalar2=None, op0=ALU.add), [xk, 'ptab'], [xk])
                                    em.op('pool', lambda e: e.tensor_tensor(out=Xb[:, ch2, r * 64:(r + 1) * 64],
                                                                            in0=Xb[:, ch2, r * 64:(r + 1) * 64],
                                                                            in1=ctab[:, c, sc_, :], op=ALU.add),
                                          [xk, 'ptab'], [xk])

                def post():
                    em.dma('pool', 'stx', dst[:, t0:t0 + T].rearrange("(c p) t -> p c t", p=128), Xb[:, :, 0:T],
                           reads=[xk], writes=['xs'])
                js = [FJ(l_, f_, s_, T, X) for (l_, f_) in ls]
                js[0].pre = pre
                js[-1].post = post
                return js
            i = 0
            while i < len(tiles):
                if i + 1 < len(tiles):
                    ja, jb = mk(tiles[i], XBUF[0]), mk(tiles[i + 1], XBUF[1])
                    for a_, b_ in zip(ja, jb):
                        jobs += [a_, b_]
                    i += 2
                else:
                    jobs += mk(tiles[i], XBUF[0])
                    i += 1
            return jobs

        xa_, xb_ = xs
        ca_, cb_ = cs
        for l in range(nl):
            if l > 0:
                em.new_epoch(['pe', 'act', 'dve'])
                em.depoch += 1
            last = (l == last_layer)
            em.dma('sp', 'prm', prl[:], prl_d[l], writes=['prm'])
            ls_ = [(l, 0)] if l == 0 else [(l - 1, 1), (l, 0)]
            tiles_ = [(1, cin if l == 0 else cb_, ca_, 0, NCTX, False)]
            tiles_ += [(0, xin if l == 0 else xb_, xa_, ti * TF, TF, l == 0) for ti in range(N // TF)]
            ffn_pipe(stage_jobs(ls_, tiles_))
            if stop == ('A', l):
                break
            em.op('dve', lambda e: e.memset(CB32[:], 0.0), [], ['CB32'])
            em.op('dve', lambda e: e.memset(CBB[:], 0.0), [], ['CBB'])
            for ti in reversed(range(NCTX // TM)):
                state_pass(l, 1, ca_, NCTX, ti, True, cb_c)
            for ti in reversed(range(NMT)):
                state_pass(l, 0, xa_, N, ti, True, cb_x)
            em.op('dve', lambda e: e.memset(CF32[:], 0.0), [], ['CF32'])
            em.op('dve', lambda e: e.memset(CFB[:], 0.0), [], ['CFB'])
            chain = []
            for ti in range(NCTX // TM):
                if last:
                    state_pass(l, 1, ca_, NCTX, ti, False, None)
                else:
                    chain.append((1, ca_, cb_, NCTX, ti, cb_c))
            for ti in range(NMT):
                chain.append((0, xa_, xb_, N, ti, cb_x))
            for ci, (s_, src_, dst_, nt_, ti, cbs_) in enumerate(chain):
                if s_ == 0 and ti == NMT // 2 and NMT >= 16:
                    em.new_epoch(['dve'])
                X_ = XBUF[ci % 2]
                npre = nxa = None
                if ci + 1 < len(chain):
                    (s2, src2, dst2, nt2, ti2, cbs2) = chain[ci + 1]
                    X2 = XBUF[(ci + 1) % 2]

                    def npre(s2=s2, src2=src2, nt2=nt2, ti2=ti2, X2=X2):
                        load_tile_halo(src2, nt2, ti2, X2)
                        mod_ht(l, 1, s2, TMH, X2)

                    def nxa(X2=X2):
                        mod_xa(TMH, HALO, X2)
                mixer_tile(l, s_, src_, dst_, nt_, ti, cbs_, X=X_, skip_pre=(ci > 0), next_pre=npre, next_xa=nxa)
        fin_src = xb_
        if stop is None:
            ffn_pipe(stage_jobs([(nl - 1, 1)], [(0, xb_, yout, ti * TF, TF, False) for ti in range(N // TF)]))
        else:
            fin_src = xa_ if stop[0] == 'A' else xb_
            for ti in range(N // TF):
                t0 = ti * TF
                em.dma('pool', 'ldx', XT[:, :, 0:TF], fin_src[:, t0:t0 + TF].rearrange("(c p) t -> p c t", p=128),
                       reads=['xs', 'xs2'], writes=['XT'])
                em.dma('pool', 'stx', yout[:, t0:t0 + TF].rearrange("(c p) t -> p c t", p=128), XT[:, :, 0:TF],
                       reads=['XT'], writes=['xs', 'xs2'])
        em.deps('sp', ['xs', 'xs2'], [])
        build.last_counts = dict(em.cnt)
        build.sbuf_left = nc.sbuf_bytes_remaining() if callable(nc.sbuf_bytes_remaining) else nc.sbuf_bytes_remaining
    return nc


def kernel(x, c, ctx, c_ctx, w_ada, b_ada, ln_g, ln_b, ffn_w_in, ffn_w_out, w_in, b_in,
           gmlp_ln_g, gmlp_ln_b, gmlp_ws, gmlp_bs, conv_w, conv_b, conv_ln_g, conv_ln_b,
           qk_conv_w, mlstm_ln_g, pool_w, pool_scale, w_branch, w_out, _nl=None, _stop=None):
    x = np.asarray(x, np.float32)
    B, N, _ = x.shape
    nl = _nl or DEPTH
    f = lambda a: np.ascontiguousarray(np.asarray(a, np.float32)[:nl])
    packs = [pack_params(nl, np.asarray(c)[b], np.asarray(c_ctx), *[np.asarray(a, np.float32) for a in (
        b_ada, ln_g, ln_b, b_in, gmlp_ln_g, gmlp_ln_b, gmlp_ws, gmlp_bs, conv_w, conv_b, conv_ln_g, conv_ln_b,
        qk_conv_w, mlstm_ln_g, pool_w, pool_scale)]) for b in range(B)]
    prm_off = packs[0][0].off
    prl_off = packs[0][1][0].off
    prms = [p[0].get() for p in packs]
    prls = [np.stack([q.get() for q in p[1]], axis=0) for p in packs]
    nc = build(N, nl, prm_off, prms[0].shape[1], prl_off, prls[0].shape[2], stop=_stop)
    shared = {"w_ada": f(w_ada), "ffn_w_in": f(ffn_w_in), "ffn_w_out": f(ffn_w_out), "w_in": f(w_in),
              "w_branch": f(w_branch), "w_out": f(w_out)}
    xT = [np.ascontiguousarray(x[b].T) for b in range(B)]
    cT = [np.ascontiguousarray(np.asarray(ctx, np.float32)[b].T) for b in range(B)]
    ncores = 2
    in_maps = []
    for i in range(ncores):
        b = i % B
        m = {"xT": xT[b], "ctxT": cT[b], "prm": prms[b], "prl": prls[b]}
        m.update(shared)
        in_maps.append(m)
    res = run_bass_kernel_spmd(nc, in_maps, core_ids=list(range(ncores)))
    out = np.stack([np.ascontiguousarray(res.results[b]["yT"].T) for b in range(B)], axis=0)
    return out.astype(np.float32)
```
